# Optimizing a Trainium2 kernel written in Bass

```python
import jax, jax.numpy as jnp
from jax import lax
import numpy as np

D_MODEL = 1024
BATCH = 8
SEQ = 2048
DEPTH = 1
DEC_BATCH = 128
DEC_SEQ = 4
PAST_LEN = 16384
PAGE_SIZE = 128

D_CONV = D_MODEL
CONV_W = 31
HEAD_N = 64
N_HEADS = D_MODEL // HEAD_N
D_R = N_HEADS * HEAD_N
R_DECAY = 64
R_ICLR = 64
R_GATE = 128
N_SHIFT = 3 * D_R + R_DECAY + R_ICLR + R_GATE
N_IN = 2 * D_CONV + N_SHIFT + 2 * D_MODEL
D_FF = 4 * D_MODEL
RMS_EPS = 1e-6
LN_EPS = 1e-5
GN_EPS = 64e-5

kernel_name = "conformer_rwkv7_gated_hybrid_step"


def _rms_norm(x, g):
    xf = x.astype(jnp.float32)
    y = xf * lax.rsqrt(jnp.mean(xf * xf, axis=-1, keepdims=True) + RMS_EPS)
    return (y * g).astype(x.dtype)


def _layer_norm(x, g, b):
    xf = x.astype(jnp.float32)
    mu = jnp.mean(xf, axis=-1, keepdims=True)
    xc = xf - mu
    var = jnp.mean(xc * xc, axis=-1, keepdims=True)
    return (xc * lax.rsqrt(var + LN_EPS) * g + b).astype(x.dtype)


def _conformer_branch(u_val, u_gate, conv_buf, conv_w, conv_b, ln_g, ln_b, w_conv_out):
    u = u_val * jax.nn.sigmoid(u_gate)
    ext = jnp.concatenate([conv_buf.astype(u.dtype), u], axis=1)
    c = lax.conv_general_dilated(
        ext, conv_w[:, None, :].astype(u.dtype), window_strides=(1,), padding='VALID',
        dimension_numbers=('NWC', 'WIO', 'NWC'), feature_group_count=D_CONV) + conv_b
    c = jax.nn.silu(_layer_norm(c, ln_g, ln_b))
    return c @ w_conv_out, ext[:, -(CONV_W - 1):]


def _wkv_scan(r, decay, k, v, kk, a, s0):
    def step(s, inp):
        r_t, w_t, k_t, v_t, kk_t, a_t = inp
        sk = jnp.einsum('bhvk,bhk->bhv', s, kk_t)
        s = (s * w_t[:, :, None, :]
             - sk[..., None] * (kk_t * a_t)[:, :, None, :]
             + v_t[..., None] * k_t[:, :, None, :])
        return s, jnp.einsum('bhvk,bhk->bhv', s, r_t)
    xs = tuple(jnp.swapaxes(t, 0, 1) for t in (r, decay, k, v, kk, a))
    s, y = lax.scan(step, s0, xs)
    return jnp.swapaxes(y, 0, 1), s


def _mixer(h, conv_buf, shift_row, wkv, w_in, conv_w, conv_b, conv_ln_g, conv_ln_b, w_conv_out,
           shift_mu, decay_base, w_decay_up, iclr_base, w_iclr_up, w_gate_up, k_k, k_a, r_k,
           lnx_g, lnx_b, w_rwkv_out, w_out):
    B, T, _ = h.shape
    i0 = 2 * D_CONV
    i1 = i0 + N_SHIFT
    proj = h @ w_in
    o_a, new_buf = _conformer_branch(proj[..., :D_CONV], proj[..., D_CONV:i0], conv_buf,
                                     conv_w, conv_b, conv_ln_g, conv_ln_b, w_conv_out)
    p_rw = proj[..., i0:i1]
    p_prev0 = shift_row.astype(h.dtype) @ w_in[:, i0:i1]
    p_prev = jnp.concatenate([p_prev0[:, None], p_rw[:, :-1]], axis=1)
    p_mix = (p_rw + (p_prev - p_rw) * shift_mu).astype(jnp.float32)
    r, k, v, l_dec, l_iclr, l_gate = jnp.split(
        p_mix, [D_R, 2 * D_R, 3 * D_R, 3 * D_R + R_DECAY, 3 * D_R + R_DECAY + R_ICLR], axis=-1)
    w_log = -jax.nn.softplus(-(decay_base + jnp.tanh(l_dec) @ w_decay_up)) - 0.5
    decay = jnp.exp(-jnp.exp(w_log))
    a = jax.nn.sigmoid(iclr_base + l_iclr @ w_iclr_up)
    g = jax.nn.sigmoid(l_gate) @ w_gate_up
    hs = lambda t: t.reshape(B, T, N_HEADS, HEAD_N)
    r, k, v, a, decay = hs(r), hs(k), hs(v), hs(a), hs(decay)
    kk = k * k_k
    kk = kk * lax.rsqrt(jnp.maximum(jnp.sum(kk * kk, axis=-1, keepdims=True), 1e-24))
    k = k * (1.0 + (a - 1.0) * k_a)
    y, s_new = _wkv_scan(r, decay, k, v, kk, a, wkv.astype(jnp.float32))
    mu = jnp.mean(y, axis=-1, keepdims=True)
    yc = y - mu
    y_n = yc * lax.rsqrt(jnp.mean(yc * yc, axis=-1, keepdims=True) + GN_EPS)
    y_n = y_n.reshape(B, T, D_R) * lnx_g + lnx_b
    bonus = (jnp.sum(r * k * r_k, axis=-1, keepdims=True) * v).reshape(B, T, D_R)
    o_b = ((y_n + bonus) * g).astype(h.dtype) @ w_rwkv_out
    gate_a = jax.nn.sigmoid(proj[..., i1:i1 + D_MODEL])
    gate_b = jax.nn.sigmoid(proj[..., i1 + D_MODEL:])
    merged = gate_a * o_a + gate_b * o_b
    return merged @ w_out, new_buf, h[:, -1], s_new.astype(wkv.dtype)


def _block(x, conv_buf, shift_row, wkv, pre_mix_g, post_mix_g, pre_ffn_g, post_ffn_g, w_in,
           conv_w, conv_b, conv_ln_g, conv_ln_b, w_conv_out, shift_mu, decay_base, w_decay_up,
           iclr_base, w_iclr_up, w_gate_up, k_k, k_a, r_k, lnx_g, lnx_b, w_rwkv_out, w_out,
           w_ff_up, w_ff_down):
    h = _rms_norm(x, pre_mix_g)
    m, new_buf, new_shift, new_wkv = _mixer(
        h, conv_buf, shift_row, wkv, w_in, conv_w, conv_b, conv_ln_g, conv_ln_b, w_conv_out,
        shift_mu, decay_base, w_decay_up, iclr_base, w_iclr_up, w_gate_up, k_k, k_a, r_k,
        lnx_g, lnx_b, w_rwkv_out, w_out)
    x = x + _rms_norm(m, post_mix_g)
    h2 = _rms_norm(x, pre_ffn_g)
    f = jnp.square(jax.nn.relu(h2 @ w_ff_up)) @ w_ff_down
    x = x + _rms_norm(f, post_ffn_g)
    return x, new_buf, new_shift, new_wkv


def setup_inputs(seed: int = 0) -> dict:
    key = jax.random.key(seed)
    ks = jax.random.split(key, 40)
    f32 = jnp.float32
    nrm = lambda i, shape, s: (jax.random.normal(ks[i], shape, f32) * s)
    gain = lambda i, shape: 1.0 + nrm(i, shape, 0.05)
    L = DEPTH
    return {
        "x_prompt": nrm(0, (BATCH, SEQ, D_MODEL), 1.0),
        "x_sample": nrm(1, (DEC_BATCH, DEC_SEQ, D_MODEL), 1.0),
        "state_conv": nrm(2, (L, DEC_BATCH, CONV_W - 1, D_CONV), 0.5),
        "state_shift": nrm(3, (L, DEC_BATCH, D_MODEL), 1.0),
        "state_wkv": nrm(4, (L, DEC_BATCH, N_HEADS, HEAD_N, HEAD_N), 1.0),
        "pre_mix_g": gain(5, (L, D_MODEL)),
        "post_mix_g": gain(6, (L, D_MODEL)),
        "pre_ffn_g": gain(7, (L, D_MODEL)),
        "post_ffn_g": gain(8, (L, D_MODEL)),
        "w_in": nrm(9, (L, D_MODEL, N_IN), D_MODEL ** -0.5),
        "conv_w": nrm(10, (L, CONV_W, D_CONV), CONV_W ** -0.5),
        "conv_b": nrm(11, (L, D_CONV), 0.02),
        "conv_ln_g": gain(12, (L, D_CONV)),
        "conv_ln_b": nrm(13, (L, D_CONV), 0.02),
        "w_conv_out": nrm(14, (L, D_CONV, D_MODEL), D_CONV ** -0.5),
        "shift_mu": jax.random.uniform(ks[15], (L, N_SHIFT), f32),
        "decay_base": -1.0 + nrm(16, (L, D_R), 0.5),
        "w_decay_up": nrm(17, (L, R_DECAY, D_R), 0.1),
        "iclr_base": nrm(18, (L, D_R), 0.1),
        "w_iclr_up": nrm(19, (L, R_ICLR, D_R), 0.1),
        "w_gate_up": nrm(20, (L, R_GATE, D_R), R_GATE ** -0.5),
        "k_k": 0.85 + nrm(21, (L, N_HEADS, HEAD_N), 0.05),
        "k_a": gain(22, (L, N_HEADS, HEAD_N)),
        "r_k": nrm(23, (L, N_HEADS, HEAD_N), 0.1),
        "lnx_g": gain(24, (L, D_R)),
        "lnx_b": nrm(25, (L, D_R), 0.02),
        "w_rwkv_out": nrm(26, (L, D_R, D_MODEL), D_R ** -0.5),
        "w_out": nrm(27, (L, D_MODEL, D_MODEL), D_MODEL ** -0.5),
        "w_ff_up": nrm(28, (L, D_MODEL, D_FF), D_MODEL ** -0.5),
        "w_ff_down": nrm(29, (L, D_FF, D_MODEL), D_FF ** -0.5),
    }


def reference(x_prompt, x_sample, state_conv, state_shift, state_wkv, pre_mix_g, post_mix_g,
              pre_ffn_g, post_ffn_g, w_in, conv_w, conv_b, conv_ln_g, conv_ln_b, w_conv_out,
              shift_mu, decay_base, w_decay_up, iclr_base, w_iclr_up, w_gate_up, k_k, k_a, r_k,
              lnx_g, lnx_b, w_rwkv_out, w_out, w_ff_up, w_ff_down):
    xp = x_prompt
    xs = x_sample
    conv_p, shift_p, wkv_p = [], [], []
    conv_s, shift_s, wkv_s = [], [], []
    for l in range(DEPTH):
        lw = (pre_mix_g[l], post_mix_g[l], pre_ffn_g[l], post_ffn_g[l], w_in[l], conv_w[l],
              conv_b[l], conv_ln_g[l], conv_ln_b[l], w_conv_out[l], shift_mu[l], decay_base[l],
              w_decay_up[l], iclr_base[l], w_iclr_up[l], w_gate_up[l], k_k[l], k_a[l], r_k[l],
              lnx_g[l], lnx_b[l], w_rwkv_out[l], w_out[l], w_ff_up[l], w_ff_down[l])
        b = xp.shape[0]
        z_conv = jnp.zeros((b, CONV_W - 1, D_CONV), xp.dtype)
        z_shift = jnp.zeros((b, D_MODEL), xp.dtype)
        z_wkv = jnp.zeros((b, N_HEADS, HEAD_N, HEAD_N), xp.dtype)
        xp, cb, sr, sw = _block(xp, z_conv, z_shift, z_wkv, *lw)
        conv_p.append(cb); shift_p.append(sr); wkv_p.append(sw)
        xs, cb, sr, sw = _block(xs, state_conv[l], state_shift[l], state_wkv[l], *lw)
        conv_s.append(cb); shift_s.append(sr); wkv_s.append(sw)
    return (xp, xs, jnp.stack(conv_p), jnp.stack(shift_p), jnp.stack(wkv_p),
            jnp.stack(conv_s), jnp.stack(shift_s), jnp.stack(wkv_s))
```

```python
import numpy as np
from contextlib import ExitStack
import concourse.bass as bass
import concourse.mybir as mybir
from concourse.bass_utils import run_bass_kernel_spmd
from concourse.alu_op_type import AluOpType as ALU

F32 = mybir.dt.float32
BF16 = mybir.dt.bfloat16
AF = mybir.ActivationFunctionType
AX = mybir.AxisListType

NCORES = 8
D = 1024
SEQ = 2048
TT = 512
NPASS = 4
NS = 64
NSEQ = 16
NCOL = 592
KC = 8
DFF = 4096
NIN = 7424
C0 = float(np.exp(-0.5))


class Sched:
    COMPUTE = ("pe", "act", "dve", "pool")

    def __init__(self, nc, n_dma_sems=10, epoch_cap=4000):
        self.nc = nc
        self.ops = []
        self.last_w = {}
        self.readers = {}
        self.n_dma_sems = n_dma_sems
        self.epoch_cap = epoch_cap
        self.eng = {"pe": nc.tensor, "act": nc.scalar, "dve": nc.vector,
                    "pool": nc.gpsimd, "sp": nc.sync}

    def add(self, eng, fn, reads=(), writes=(), dma=False):
        i = len(self.ops)
        deps = set()
        for k in reads:
            w = self.last_w.get(k)
            if w is not None:
                deps.add(w)
        for k in writes:
            w = self.last_w.get(k)
            if w is not None:
                deps.add(w)
            for r in self.readers.get(k, ()):
                deps.add(r)
        for k in reads:
            self.readers.setdefault(k, []).append(i)
        for k in writes:
            self.last_w[k] = i
            self.readers[k] = []
        deps.discard(i)
        self.ops.append(dict(eng=eng, fn=fn, deps=deps, dma=dma, sig=False))
        return i

    def emit(self, sem_ctx):
        ops = self.ops

        def skip(p, o):
            return (not p["dma"]) and (not o["dma"]) and p["eng"] == "pe" and o["eng"] == "pe"

        for o in ops:
            for d in o["deps"]:
                p = ops[d]
                if p["dma"] or skip(p, o):
                    continue
                p["sig"] = True
        cnt = {e: 0 for e in self.COMPUTE}
        epoch = {e: 0 for e in self.COMPUTE}
        sems = {}

        def get_sem(name):
            if name not in sems:
                sems[name] = sem_ctx(name)
            return sems[name]

        for o in ops:
            if o["dma"] or o["eng"] not in self.COMPUTE:
                continue
            e = o["eng"]
            if o["sig"]:
                if cnt[e] >= self.epoch_cap:
                    epoch[e] += 1
                    cnt[e] = 0
                cnt[e] += 1
                o["semname"] = f"s_{e}_{epoch[e]}"
                o["semval"] = cnt[e]
        dcount, dlast, dlast_idx = {}, {}, {}
        for _i, _o in enumerate(ops):
            _o["_i"] = _i
        for o in ops:
            if not o["dma"]:
                continue
            q = o["eng"]
            n = dcount.get(q, 0)
            dcount[q] = n + 1
            name = f"d_{q}_{n % self.n_dma_sems}"
            prev = dlast.get(name, 0)
            o["semname"], o["semval"], o["prev_val"] = name, prev + 16, prev
            o["prev_idx"] = dlast_idx.get(name, -1)
            dlast[name] = prev + 16
            dlast_idx[name] = ops.index(o) if False else o["_i"]
        eclock = {e: {} for e in self.eng}
        oclock = {}
        for idx, o in enumerate(ops):
            e = o["eng"]
            E = self.eng[e]
            ck = eclock[e]
            need = {}
            for d in o["deps"]:
                p = ops[d]
                if skip(p, o):
                    continue
                cur = need.get(p["semname"])
                if cur is None or p["semval"] > cur[0]:
                    need[p["semname"]] = (p["semval"], d)
            if o["dma"] and o["prev_val"] > 0:
                cur = need.get(o["semname"])
                if cur is None or o["prev_val"] > cur[0]:
                    need[o["semname"]] = (o["prev_val"], o.get("prev_idx", -1))
            for sn, (v, pidx) in sorted(need.items(), key=lambda kv: -kv[1][1]):
                if ck.get(sn, 0) >= v:
                    continue
                E.wait_ge(get_sem(sn), v)
                pc_ = oclock.get(pidx)
                if pc_:
                    for k2, v2 in pc_.items():
                        if ck.get(k2, 0) < v2:
                            ck[k2] = v2
                if ck.get(sn, 0) < v:
                    ck[sn] = v
            if o["fn"] is None:
                continue
            ins = o["fn"](E)
            if o["dma"]:
                ins.then_inc(get_sem(o["semname"]), 16)
                c2 = dict(ck)
                c2[o["semname"]] = o["semval"]
                oclock[idx] = c2
            elif o["sig"]:
                ins.then_inc(get_sem(o["semname"]), 1)
                c2 = dict(ck)
                c2[o["semname"]] = o["semval"]
                oclock[idx] = c2


def _consts():
    c = {}
    p = np.arange(128)
    col = np.arange(64)
    s = (p % 64)[:, None]
    t = col[None, :]
    c["ident"] = np.eye(128, dtype=np.float32)
    c["m_su"] = (t > s).astype(np.float32)
    c["m_u"] = (t >= s).astype(np.float32)
    c["m_sl"] = (t < s).astype(np.float32)
    same = ((s // 4) == (t // 4)).astype(np.float32)
    c["s_su"] = c["m_su"] * same
    c["s_u"] = c["m_u"] * same
    c["s_sl"] = c["m_sl"] * same
    c["eq"] = (t == s).astype(np.float32)
    c["bones"] = ((p[:, None] // 64) == (np.arange(128)[None, :] // 64)).astype(np.float32)
    c["odiv"] = np.full((128, 128), 1.0 / 1024.0, np.float32)
    sm = np.ones((128, 576), np.float32)
    sm[:, 0:512:64] = 0.0
    sm[:, 512:576:4] = 0.0
    c["scanm"] = sm
    qm = np.zeros((128, 16, 64), np.float32)
    for j in range(16):
        qm[:, j, 4 * j:4 * j + 4] = 1.0
    c["qmask"] = qm.reshape(128, 1024)
    rm = np.zeros((128, 16), np.float32)
    for r in range(64):
        rm[r, r // 4] = 1.0
    c["rowm"] = rm
    names = list(c.keys())
    offs, o = {}, 0
    for n in names:
        offs[n] = (o, c[n].shape[1])
        o += c[n].shape[1]
    return np.ascontiguousarray(np.concatenate([c[n] for n in names], axis=1)), offs


_CST, _COFF = _consts()
_PCN = ["conv_b", "conv_ln_g", "conv_ln_b", "decay_base", "iclr_base", "k_k", "k_a", "r_k", "lnx_g", "lnx_b"]
PC_MU = 80
PC_CW = 106
PC_N = 106 + 248


def _fm(v):
    return np.ascontiguousarray(np.asarray(v, np.float32).reshape(8, 128).T)


def _pack_pcol(inp):
    cols = [_fm(inp[n][0].reshape(-1)) for n in _PCN]
    mu = np.asarray(inp["shift_mu"][0], np.float32).reshape(26, 128).T
    cw = np.asarray(inp["conv_w"][0], np.float32)
    cwf = cw.reshape(31, 8, 128).transpose(2, 1, 0).reshape(128, 248)
    return np.ascontiguousarray(np.concatenate(cols + [mu, cwf], axis=1))


def build_nc(cfg=None):
    cfg = cfg or {}
    npass = cfg.get("npass", NPASS)
    do_rwkv = cfg.get("rwkv", True)
    taps = cfg.get("taps", ())
    nc = bass.Bass("TRN2", target_bir_lowering=False)
    dram = lambda n, s, kind="ExternalInput": nc.dram_tensor(n, list(s), F32, kind=kind).ap()
    x_p = dram("x_p", [SEQ, D]); x_s = dram("x_s", [NS, D])
    sconv = dram("sconv", [NSEQ, 30, D]); sshift = dram("sshift", [NSEQ, D]); swkv = dram("swkv", [NSEQ, 16, 64, 64])
    gains = dram("gains", [4, D]); pcol_d = dram("pcol", [128, PC_N]); cst_d = dram("cst", [128, _CST.shape[1]])
    w_in = dram("w_in", [D, NIN]); w_co = dram("w_co", [D, D]); w_ro = dram("w_ro", [D, D]); w_out = dram("w_out", [D, D])
    w_up = dram("w_up", [D, DFF]); w_dn = dram("w_dn", [DFF, D])
    w_dec = dram("w_dec", [64, D]); w_icl = dram("w_icl", [64, D]); w_g = dram("w_g", [128, D])
    OUT = "ExternalOutput"
    y_p = dram("y_p", [SEQ, D], OUT); y_s = dram("y_s", [NS, D], OUT)
    conv_p = dram("conv_p", [30, D], OUT); shift_p = dram("shift_p", [1, D], OUT); wkv_p = dram("wkv_p", [16, 64, 64], OUT)
    conv_s = dram("conv_s", [NSEQ, 30, D], OUT); shift_s = dram("shift_s", [NSEQ, D], OUT); wkv_s = dram("wkv_s", [NSEQ, 16, 64, 64], OUT)
    tapd = {}

    es = ExitStack()
    with es:
        S = Sched(nc, epoch_cap=cfg.get("epoch_cap", 4000))
        sbt = lambda n, s, d: es.enter_context(nc.sbuf_tensor(n, list(s), d))
        cstb = sbt("cstb_sb", [128, _CST.shape[1]], BF16)
        identf_t = sbt("identf_sb", [128, 128], F32)
        pcol = sbt("pcolt", [128, PC_N], F32)
        pneg = sbt("pneg", [128, 34], F32)
        gain = [sbt(f"gain{i}", [128, D], F32) for i in range(2)]
        wsm = sbt("wsm", [128, 3, D], BF16)
        wb = [sbt(f"wb{i}", [128, KC, 512], BF16) for i in range(3)]
        xs = [sbt(f"xs{i}", [128, D], F32) for i in range(2)]
        hb = sbt("hb", [128, D], BF16)
        junk = sbt("junk", [128, D], F32)
        st = sbt("st", [128, 64], F32)
        hT = sbt("hT", [128, KC, NCOL], BF16)
        merged = sbt("merged", [128, KC, NCOL], BF16)
        gbuf = sbt("gbuf", [128, KC, NCOL], BF16)
        uhist = sbt("uhist", [128, KC, 30], BF16)
        plast = sbt("plast", [128, 26], F32)
        diag = [sbt(f"diag{i}", [128, 128], BF16) for i in range(8)]
        Hst = [sbt(f"H32_{g}", [128, 4, 64], F32) for g in range(2)]
        Hbd = [[sbt(f"Hbd_{g}_{q_}", [128, 4, 128], BF16) for q_ in range(2)] for g in range(2)]
        SHN = cfg.get("shn", 56) * 1024
        SH = sbt("SH", [128, SHN], BF16)
        psb = [es.enter_context(nc.psum_tensor(f"ps{i}", [128, 512], F32)) for i in range(8)]

        def cs(name, dt=BF16):
            o, n = _COFF[name]
            return cstb[:, o:o + n]

        identf = identf_t[:]; identb = cs("ident", BF16)
        cst = SH[:, 0:2 * _CST.shape[1]].bitcast(F32)

        class Arena:
            def __init__(self):
                self.off = 0
            def reset(self):
                self.off = 0
            def take(self, nelem, dt):
                n16 = nelem * (2 if dt == F32 else 1)
                o = self.off
                self.off += (n16 + 1) // 2 * 2
                assert self.off <= SHN, ("arena overflow", self.off)
                v = SH[:, o:o + n16]
                return v.bitcast(F32) if dt == F32 else v
        AR = Arena()

        cur_stream = [None]

        def op(eng, fn, reads=(), writes=(), dma=False):
            r = [k for k in reads if k[0] != "ps"]
            w = list(writes) + [k for k in reads if k[0] == "ps"]
            if cur_stream[0] is not None:
                cur_stream[0].append((eng, fn, r, w, dma))
                return None
            return S.add(eng, fn, r, w, dma)

        def interleave(gens):
            streams = []
            for f, banks in gens:
                cur_stream[0] = []
                saved = (pspool[0], pscur[0])
                pspool[0], pscur[0] = banks, 0
                f()
                streams.append(cur_stream[0])
                pspool[0], pscur[0] = saved
                cur_stream[0] = None
            idx = [0] * len(streams)
            while any(idx[i] < len(streams[i]) for i in range(len(streams))):
                for i in range(len(streams)):
                    if idx[i] < len(streams[i]):
                        S.add(*streams[i][idx[i]])
                        idx[i] += 1

        def tap(name, view, shape, reads):
            if name not in taps:
                return
            t = dram("tap_" + name, shape, OUT)
            tapd[name] = t
            op("pool", lambda E: E.dma_start(out=t, in_=view), reads=list(reads) + [("SHE",)], writes=[("tap", name)], dma=True)

        pscur = [0]
        pspool = [list(range(8))]
        def ps_take():
            b = pspool[0][pscur[0] % len(pspool[0])]
            pscur[0] += 1
            return b
        PSK = lambda b: [("ps", b)]

        op("sp", lambda E: E.dma_start(out=cst, in_=cst_d[:, :]), writes=[("cst0",), ("SHE",)], dma=True)
        op("sp", lambda E: E.dma_start(out=pcol[:], in_=pcol_d[:, :]), writes=[("pcol",)], dma=True)
        op("dve", lambda E: E.tensor_copy(out=cstb[:], in_=cst), reads=[("cst0",)], writes=[("cstb",)])
        op("dve", lambda E: E.tensor_copy(out=identf_t[:], in_=cst[:, 0:128]), reads=[("cst0",)], writes=[("cst",)])
        op("dve", lambda E: E.tensor_scalar(out=pneg[:, 0:8], in0=pcol[:, 48:56], scalar1=-1.0, scalar2=1.0, op0=ALU.mult, op1=ALU.add),
           reads=[("pcol",)], writes=[("pneg",)])
        op("dve", lambda E: E.tensor_scalar(out=pneg[:, 8:34], in0=pcol[:, PC_MU:PC_MU + 26], scalar1=-1.0, scalar2=1.0, op0=ALU.mult, op1=ALU.add),
           reads=[("pcol",)], writes=[("pneg",)])
        op("pool", lambda E: E.memset(wsm[:], 0.0), writes=[("wsm",)])
        op("pool", lambda E: E.dma_start(out=wsm[0:64, 0, :], in_=w_dec[:, :]), reads=[("wsm",)], writes=[("wsm", 0)], dma=True)
        op("pool", lambda E: E.dma_start(out=wsm[64:128, 1, :], in_=w_icl[:, :]), reads=[("wsm",)], writes=[("wsm", 1)], dma=True)
        op("pool", lambda E: E.dma_start(out=wsm[:, 2, :], in_=w_g[:, :]), reads=[("wsm",)], writes=[("wsm", 2)], dma=True)
        op("pool", lambda E: E.memset(uhist[:], 0.0), writes=[("uhist",)])
        op("pool", lambda E: E.memset(plast[:], 0.0), writes=[("plast",)])
        for g in range(2):
            op("pool", lambda E, g=g: E.memset(Hst[g][:], 0.0), writes=[("H32", g, 0), ("H32", g, 1)])
            for q_ in range(2):
                op("pool", lambda E, g=g, q_=q_: E.memset(Hbd[g][q_][:], 0.0), writes=[("Hbd", g, q_, 0), ("Hbd", g, q_, 1)])
        NZ = 32
        for zi in range(NZ):
            z0, z1 = zi * SHN // NZ, (zi + 1) * SHN // NZ
            op("pool", lambda E, z0=z0, z1=z1: E.memset(SH[:, z0:z1], 0.0), reads=[("SHE",)] if zi == 0 else [], writes=[("SHE",), ("cst0",)])

        pc = lambda pi, kc: pcol[:, pi * 8 + kc: pi * 8 + kc + 1]
        PI = {n: i for i, n in enumerate(_PCN)}

        def load_gain(slot, gi):
            op("sp", lambda E: E.dma_start(out=gain[slot][:], in_=gains[gi:gi + 1, :].partition_broadcast(128)),
               writes=[("gain", slot)], dma=True)

        wcur = [0]
        def wload(dview, ncols=512):
            s = wcur[0] % 3
            wcur[0] += 1
            op("pool", lambda E: E.dma_start(out=wb[s][:, :, 0:ncols], in_=dview), writes=[("wb", s)], dma=True)
            return s

        def wload_v(dview):
            s_ = wcur[0] % 3
            wcur[0] += 1
            if dview.shape[1] == 32:
                outv = wb[s_][:].rearrange("p k c -> p (k c)").rearrange("p (f c) -> p f c", f=32)
            else:
                outv = wb[s_][:, :, 0:dview.shape[2]]
            op("pool", lambda E: E.dma_start(out=outv, in_=dview), writes=[("wb", s_)], dma=True)
            return s_

        def all_weight_views():
            v = []
            for p_ in range(npass):
                if do_rwkv:
                    v.append(wview(w_in, 2048 + 3072, 256))
                    for hg in range(2):
                        for kind in range(3):
                            v.append(wview(w_in, 2048 + kind * 1024 + hg * 512, 512))
                    v += [wview(w_in, 6400, 512), wview(w_in, 6400 + 512, 512), wview(w_ro, 0, 512), wview(w_ro, 512, 512)]
                v += [wview(w_in, 1024, 512), wview(w_in, 0, 512), wview(w_in, 1024 + 512, 512), wview(w_in, 512, 512)]
                v += [wview(w_in, 5376, 512), wview(w_in, 5376 + 512, 512), wview(w_co, 0, 512), wview(w_co, 512, 512)]
                v += [wview(w_out, 0, 512), wview(w_out, 512, 512)]
                v += [wview(w_up, g * 512, 512) for g in range(8)]
                v += [w_dn.rearrange("(f p) n -> p f n", p=128)[:, :, oc * 128:(oc + 1) * 128] for oc in range(KC)]
            return v

        wq = dict(views=None, issued=0, taken=0, slots={})
        def wnext(ahead=2):
            if wq["views"] is None:
                wq["views"] = all_weight_views()
            vs = wq["views"]
            while wq["issued"] < min(len(vs), wq["taken"] + 1 + ahead):
                i = wq["issued"]
                wq["slots"][i] = wload_v(vs[i])
                wq["issued"] += 1
            sl = wq["slots"].pop(wq["taken"])
            wq["taken"] += 1
            return sl

        def wstream(views, ahead=2):
            n = len(views)
            slots = {}
            for i in range(min(ahead, n)):
                slots[i] = wload(views[i])
            for i in range(n):
                if i + ahead < n:
                    slots[i + ahead] = wload(views[i + ahead])
                yield i, slots[i]

        def wview(W, c0, n):
            return W.rearrange("(kc p) n -> p kc n", p=128)[:, :, c0:c0 + n]

        def phase_barrier():
            op("pool", lambda E: E.memset(st[:, 60:64], 0.0), reads=[], writes=[("SHE",)])
        SHR = [("SHE",)]

        def rstd_from_ssq(ssq_ap, out_ap, n, eps, rk, wk, rows=128):
            op("act", lambda E: E.activation(out=out_ap, in_=ssq_ap, func=AF.Sqrt, bias=eps, scale=1.0 / n), reads=rk, writes=wk)
            op("dve", lambda E: E.reciprocal(out=out_ap, in_=out_ap), reads=wk, writes=wk)


        GN_EPS = 64e-5

        def rwkv_phase(p, P0, LAST, nblk, segs, segs_r, HTK, ws_mm):
            phase_barrier()
            AR.reset()
            n = 576 if P0 else 512
            nch = 9 if P0 else 8
            t3 = lambda k, c, dt: AR.take(k * c, dt).rearrange("p (k c) -> p k c", k=k)
            loraA = AR.take(592, BF16); sgl = AR.take(592, BF16)
            zb = t3(KC, 576, BF16)
            QR = AR.take(4 * 9 * 128, BF16).rearrange("p (k c q) -> p k c q", k=4, c=9)
            RKV = [t3(4, 592, BF16) for _ in range(3)]
            bon = t3(4, 576, BF16); gT = t3(4, 576, BF16)
            WC = t3(4, 9, F32); WCs = t3(4, 16, F32)
            w_full = [AR.take(1184, F32) for _ in range(6)]
            w = w_full
            sqb2 = [AR.take(576, BF16) for _ in range(2)]; rkb2 = [AR.take(576, BF16) for _ in range(2)]
            Ktok = AR.take(512, BF16); Btok = AR.take(512, BF16)
            Vc = [AR.take(512, BF16) for _ in range(2)]
            h8 = lambda t: t.rearrange("p (h v) -> p h v", h=8)
            LkT = [AR.take(512, BF16) for _ in range(2)]; MkT = [AR.take(512, BF16) for _ in range(2)]; MbT = [AR.take(512, BF16) for _ in range(2)]
            _nn = [AR.take(512, BF16) for _ in range(2)]; NN = [_nn, _nn]
            _aa = [AR.take(512, BF16) for _ in range(2)]; AA = [_aa, _aa]
            _z0 = AR.take(512, BF16); ZZ = [[_z0, AR.take(512, BF16)], [_z0, AR.take(512, BF16)]]
            Xs = [AR.take(512, BF16) for _ in range(2)]; Us = [AR.take(512, BF16) for _ in range(2)]
            ysq = AR.take(512, F32); yc = ysq; yh = AR.take(512, BF16)
            zt = AR.take(512, F32).rearrange("p (k c) -> p k c", k=4)
            Khs = t3(4, 64, BF16); Bhs = t3(4, 64, BF16)
            Khtok = AR.take(512, BF16); Bhtok = AR.take(512, BF16)
            H0bd = AR.take(2048, BF16).rearrange("p (j c) -> p j c", j=16)
            KtT = gbuf[:, 0:4, :]; BtT = gbuf[:, 4:8, :]
            a1, mixo = w[4], w[5]
            mu = lambda rc: pcol[:, PC_MU + rc: PC_MU + rc + 1]
            omu = lambda rc: pneg[:, 8 + rc: 9 + rc]
            WK = lambda i: [("w", i)]
            bc_mid = lambda ap, nmid: ap.unsqueeze(1).to_broadcast([ap.shape[0], nmid, ap.shape[1]])
            bc_last = lambda ap, nl: ap.unsqueeze(2).to_broadcast([ap.shape[0], ap.shape[1], nl])
            c64 = lambda ap: ap.rearrange("p (c q) -> p c q", q=64)
            s4 = lambda ap: ap.rearrange("p (s t) -> p s t", t=4)
            pairv = lambda t, kl: t[:, kl * 128:(kl + 1) * 128]
            bones = cs("bones"); eqm = cs("eq")
            GK = [("gbuf", k) for k in range(8)]
            QK = [("QR", k) for k in range(4)]
            for i, tz in enumerate(LkT + MkT + MbT + [ZZ[0][1], ZZ[1][1]] + Xs + Us + Vc + [Khtok, Bhtok]):
                op("pool", lambda E, tz=tz: E.memset(tz, 0.0), reads=SHR, writes=[("zp", i)])
            if P0:
                op("pool", lambda E: E.memset(H0bd, 0.0), reads=SHR, writes=[("H0bd",)])
            ZP = [("zp", i) for i in range(18)]

            def mix(rc, pvs, dst, dkey):
                for (pv, pb, c0, c1) in pvs:
                    if c0 == 0:
                        op("act", lambda E, pv=pv: E.activation(out=a1[:, 0:512], in_=pv, func=AF.Identity, scale=omu(rc)), reads=PSK(pb) + [("pneg",)] + SHR, writes=WK(4))
                        op("dve", lambda E, pv=pv: E.scalar_tensor_tensor(out=dst[:, 1:512], in0=pv[:, 0:511], scalar=mu(rc), in1=a1[:, 1:512], op0=ALU.mult, op1=ALU.add),
                           reads=PSK(pb) + WK(4) + [("pcol",)] + SHR, writes=[dkey])
                        op("dve", lambda E: E.scalar_tensor_tensor(out=dst[:, 0:1], in0=plast[:, rc:rc + 1], scalar=mu(rc), in1=a1[:, 0:1], op0=ALU.mult, op1=ALU.add),
                           reads=WK(4) + [("plast", rc), ("plast",), ("pcol",)] + SHR, writes=[dkey])
                        op("act", lambda E, pv=pv: E.copy(out=plast[:, rc:rc + 1], in_=pv[:, 511:512]), reads=PSK(pb) + [("plast",)], writes=[("plast", rc)])
                    else:
                        op("act", lambda E, pv=pv: E.activation(out=a1[:, 512:576], in_=pv[:, 0:64], func=AF.Identity, scale=omu(rc)), reads=PSK(pb) + [("pneg",)] + SHR, writes=WK(4))
                        d3, a3, p3 = s4(dst[:, 512:576]), s4(a1[:, 512:576]), s4(pv[:, 0:64])
                        op("dve", lambda E, d3=d3, a3=a3, p3=p3: E.scalar_tensor_tensor(out=d3[:, :, 1:4], in0=p3[:, :, 0:3], scalar=mu(rc), in1=a3[:, :, 1:4], op0=ALU.mult, op1=ALU.add),
                           reads=PSK(pb) + WK(4) + [("pcol",)] + SHR, writes=[dkey])
                        op("dve", lambda E, d3=d3, a3=a3, pv=pv: E.scalar_tensor_tensor(out=d3[:, :, 0], in0=pv[:, 64:80], scalar=mu(rc), in1=a3[:, :, 0], op0=ALU.mult, op1=ALU.add),
                           reads=PSK(pb) + WK(4) + [("pcol",)] + SHR, writes=[dkey])

            sw = wnext()
            def ev_lora(oc, pvs):
                mix(24 + oc, pvs, mixo, ("w", 5))
                if oc == 0:
                    op("act", lambda E: E.activation(out=loraA[0:64, 0:n], in_=mixo[0:64, 0:n], func=AF.Tanh), reads=WK(5) + SHR, writes=[("loraA", 0)])
                    op("act", lambda E: E.copy(out=loraA[64:128, 0:n], in_=mixo[64:128, 0:n]), reads=WK(5) + SHR, writes=[("loraA", 1)])
                else:
                    op("act", lambda E: E.activation(out=sgl[:, 0:n], in_=mixo[:, 0:n], func=AF.Sigmoid), reads=WK(5) + SHR, writes=[("sgl",)])
            ws_mm(sw, 2, hT, HTK, segs_r, KC, ev_lora)

            def elem(hg, kl, sid=0):
                kc = hg * 4 + kl
                w = [t[:, sid * 592:(sid + 1) * 592] for t in w_full]
                WK = lambda i: [("w", i) if sid == 0 else ("wB", i)]
                sqb, rkb = sqb2[sid], rkb2[sid]
                SQK, RKK = [("sqb", sid)], [("rkb", sid)]
                Rr, Kk, Vv = RKV[0][:, kl, :], RKV[1][:, kl, :], RKV[2][:, kl, :]
                RK = lambda i: [("rkv", i, kl)]
                pz = [None, None, None]
                for qi, (lh, rh, rkey) in enumerate([(wsm[:, 0, kc * 128:(kc + 1) * 128], loraA, [("loraA", 0), ("loraA", 1), ("wsm", 0), ("wsm",)]),
                                                     (wsm[:, 1, kc * 128:(kc + 1) * 128], loraA, [("loraA", 0), ("loraA", 1), ("wsm", 1), ("wsm",)]),
                                                     (wsm[:, 2, kc * 128:(kc + 1) * 128], sgl, [("sgl",), ("wsm", 2), ("wsm",)])]):
                    pz[qi] = [ps_take() for _ in segs]
                    for si, (c0, c1) in enumerate(segs):
                        op("pe", lambda E, lh=lh, rh=rh, c0=c0, c1=c1, pb=pz[qi][si]: E.matmul(psb[pb][:, 0:c1 - c0], lhsT=lh, rhs=rh[:, c0:c1], start=True, stop=True),
                           reads=rkey + SHR, writes=PSK(pz[qi][si]))
                    for si, (c0, c1) in enumerate(segs):
                        if qi == 0:
                            op("act", lambda E, c0=c0, c1=c1, pb=pz[0][si]: E.activation(out=w[0][:, c0:c1], in_=psb[pb][:, 0:c1 - c0], func=AF.Sigmoid, bias=pc(PI["decay_base"], kc), scale=1.0),
                               reads=PSK(pz[0][si]) + [("pcol",)] + SHR, writes=WK(0))
                        elif qi == 1:
                            op("act", lambda E, c0=c0, c1=c1, pb=pz[1][si]: E.activation(out=w[1][:, c0:c1], in_=psb[pb][:, 0:c1 - c0], func=AF.Sigmoid, bias=pc(PI["iclr_base"], kc), scale=1.0),
                               reads=PSK(pz[1][si]) + [("pcol",)] + SHR, writes=WK(1))
                        else:
                            op("act", lambda E, c0=c0, c1=c1, pb=pz[2][si]: E.copy(out=gT[:, kl, c0:c1], in_=psb[pb][:, 0:c1 - c0]), reads=PSK(pz[2][si]) + SHR, writes=[("gT", kl)])
                op("dve", lambda E: E.tensor_tensor_scan(out=w[2][:, 0:n], data0=cs("scanm")[:, 0:n], data1=w[0][:, 0:n], initial=0.0, op0=ALU.mult, op1=ALU.add),
                   reads=WK(0) + [("cstb",)] + SHR, writes=WK(2))
                op("pool", lambda E: E.tensor_tensor(out=w[3][:, 0:n], in0=w[2][:, 0:n], in1=w[0][:, 0:n], op=ALU.subtract), reads=WK(2) + WK(0) + SHR, writes=WK(3))
                op("act", lambda E: E.activation(out=w[3][:, 0:n], in_=w[3][:, 0:n], func=AF.Exp, scale=-C0), reads=WK(3) + SHR, writes=WK(3))
                op("act", lambda E: E.activation(out=w[4][:, 0:n], in_=w[2][:, 0:n], func=AF.Exp, scale=C0), reads=WK(2) + SHR, writes=WK(4))
                op("act", lambda E: E.activation(out=w[2][:, 0:n], in_=w[2][:, 0:n], func=AF.Exp, scale=-C0), reads=WK(2) + WK(4) + SHR, writes=WK(2))
                op("pool", lambda E: E.tensor_copy(out=WC[:, kl, 0:8], in_=w[2][:, 63:512:64]), reads=WK(2) + SHR, writes=[("WC", kl)])
                if P0:
                    op("pool", lambda E: E.tensor_copy(out=WCs[:, kl, :], in_=w[2][:, 515:576:4]), reads=WK(2) + SHR, writes=[("WCs", kl)])
                op("dve", lambda E: E.tensor_scalar(out=w[5][:, 0:n], in0=Kk[:, 0:n], scalar1=pc(PI["k_k"], kc), scalar2=None, op0=ALU.mult), reads=RK(1) + [("pcol",)] + SHR, writes=WK(5))
                op("act", lambda E: E.activation(out=sqb[:, 0:n], in_=w[5][:, 0:n], func=AF.Square), reads=WK(5) + SHR, writes=SQK)
                pss = [ps_take() for _ in segs]
                for si, (c0, c1) in enumerate(segs):
                    op("pe", lambda E, c0=c0, c1=c1, pb=pss[si]: E.matmul(psb[pb][:, 0:c1 - c0], lhsT=bones, rhs=sqb[:, c0:c1], start=True, stop=True), reads=SQK + [("cstb",)] + SHR, writes=PSK(pss[si]))
                    op("act", lambda E, c0=c0, c1=c1, pb=pss[si]: E.activation(out=w[0][:, c0:c1], in_=psb[pb][:, 0:c1 - c0], func=AF.Sqrt, bias=1e-24, scale=1.0), reads=PSK(pss[si]) + WK(3) + SHR, writes=WK(0))
                op("dve", lambda E: E.reciprocal(out=w[0][:, 0:n], in_=w[0][:, 0:n]), reads=WK(0) + SHR, writes=WK(0))
                op("dve", lambda E: E.tensor_tensor(out=w[5][:, 0:n], in0=w[5][:, 0:n], in1=w[0][:, 0:n], op=ALU.mult), reads=WK(5) + WK(0) + SHR, writes=WK(5))
                op("dve", lambda E: E.tensor_scalar(out=w[0][:, 0:n], in0=w[1][:, 0:n], scalar1=pc(PI["k_a"], kc), scalar2=pneg[:, kc:kc + 1], op0=ALU.mult, op1=ALU.add),
                   reads=WK(1) + WK(0) + [("pcol",), ("pneg",)] + SHR, writes=WK(0))
                op("dve", lambda E: E.tensor_tensor(out=w[0][:, 0:n], in0=Kk[:, 0:n], in1=w[0][:, 0:n], op=ALU.mult), reads=RK(1) + WK(0) + SHR, writes=WK(0))
                op("dve", lambda E: E.tensor_tensor(out=QR[:, kl, 0:nch, 0:64], in0=c64(w[5][:, 0:n]), in1=c64(w[3][:, 0:n]), op=ALU.mult), reads=WK(5) + WK(3) + SHR, writes=[("QR", kl)])
                op("dve", lambda E: E.tensor_tensor(out=w[5][:, 0:n], in0=w[5][:, 0:n], in1=w[1][:, 0:n], op=ALU.mult), reads=WK(5) + WK(1) + [("QR", kl)] + SHR, writes=WK(5))
                op("dve", lambda E: E.scalar_tensor_tensor(out=BtT[:, kl, 0:n], in0=w[5][:, 0:n], scalar=-1.0, in1=w[4][:, 0:n], op0=ALU.mult, op1=ALU.mult), reads=WK(5) + WK(4) + SHR, writes=[("gbuf", 4 + kl)])
                op("pool", lambda E: E.tensor_tensor(out=KtT[:, kl, 0:n], in0=w[0][:, 0:n], in1=w[4][:, 0:n], op=ALU.mult), reads=WK(0) + WK(4) + SHR, writes=[("gbuf", kl)])
                op("pool", lambda E: E.tensor_tensor(out=QR[:, kl, 0:nch, 64:128], in0=c64(Rr[:, 0:n]), in1=c64(w[2][:, 0:n]), op=ALU.mult), reads=RK(0) + WK(2) + SHR, writes=[("QR", kl)])
                op("dve", lambda E: E.scalar_tensor_tensor(out=rkb[:, 0:n], in0=Rr[:, 0:n], scalar=pc(PI["r_k"], kc), in1=w[0][:, 0:n], op0=ALU.mult, op1=ALU.mult), reads=RK(0) + WK(0) + [("pcol",)] + SHR, writes=RKK)
                psq = [ps_take() for _ in segs]
                for si, (c0, c1) in enumerate(segs):
                    op("pe", lambda E, c0=c0, c1=c1, pb=psq[si]: E.matmul(psb[pb][:, 0:c1 - c0], lhsT=bones, rhs=rkb[:, c0:c1], start=True, stop=True), reads=RKK + [("cstb",)] + SHR, writes=PSK(psq[si]))
                    op("dve", lambda E, c0=c0, c1=c1, pb=psq[si]: E.tensor_tensor(out=bon[:, kl, c0:c1], in0=psb[pb][:, 0:c1 - c0], in1=Vv[:, c0:c1], op=ALU.mult), reads=PSK(psq[si]) + RK(2) + SHR, writes=[("bon", kl)])
                if P0:
                    op("dve", lambda E: E.tensor_tensor(out=s4(Khs[:, kl, :]), in0=s4(KtT[:, kl, 512:576]), in1=bc_last(WCs[:, kl, :], 4), op=ALU.mult), reads=[("gbuf", kl), ("WCs", kl)] + SHR, writes=[("Khs", kl)])
                    op("dve", lambda E: E.tensor_tensor(out=s4(Bhs[:, kl, :]), in0=s4(BtT[:, kl, 512:576]), in1=bc_last(WCs[:, kl, :], 4), op=ALU.mult), reads=[("gbuf", 4 + kl), ("WCs", kl)] + SHR, writes=[("Bhs", kl)])

            def ypost(pyc, rs, h0, nh):
                hsl = slice(h0 * 64, (h0 + nh) * 64)
                y3 = psb[pyc][rs, hsl].rearrange("p (h v) -> p h v", h=nh)
                v3 = lambda t: t[rs, hsl].rearrange("p (h v) -> p h v", h=nh)
                sa, sb_, sc_ = st[rs, 16 + h0:16 + h0 + nh], st[rs, 24 + h0:24 + h0 + nh], st[rs, 32 + h0:32 + h0 + nh]
                if cfg.get('yp_stop', 99) < 1:
                    return
                op("act", lambda E: E.activation(out=ysq[rs, hsl], in_=psb[pyc][rs, hsl], func=AF.Square), reads=PSK(pyc) + SHR, writes=[("ysq",)])
                if cfg.get('yp_stop', 99) < 2:
                    return
                op("dve", lambda E: E.tensor_reduce(out=sa, in_=y3, axis=AX.X, op=ALU.add), reads=PSK(pyc), writes=[("st", 16)])
                if cfg.get('yp_stop', 99) < 3:
                    return
                op("dve", lambda E: E.tensor_reduce(out=sb_, in_=v3(ysq), axis=AX.X, op=ALU.add), reads=[("ysq",)] + SHR, writes=[("st", 24)])
                if cfg.get('yp_stop', 99) < 4:
                    return
                op("dve", lambda E: E.tensor_scalar(out=sa, in0=sa, scalar1=1.0 / 64, scalar2=None, op0=ALU.mult), reads=[("st", 16)], writes=[("st", 16)])
                if cfg.get('yp_stop', 99) < 5:
                    return
                op("dve", lambda E: E.tensor_tensor(out=sc_, in0=sa, in1=sa, op=ALU.mult), reads=[("st", 16)], writes=[("st", 32)])
                if cfg.get('yp_stop', 99) < 6:
                    return
                op("dve", lambda E: E.scalar_tensor_tensor(out=sb_, in0=sb_, scalar=1.0 / 64, in1=sc_, op0=ALU.mult, op1=ALU.subtract), reads=[("st", 24), ("st", 32)], writes=[("st", 24)])
                if cfg.get('yp_stop', 99) < 7:
                    return
                op("act", lambda E: E.activation(out=sb_, in_=sb_, func=AF.Sqrt, bias=GN_EPS, scale=1.0), reads=[("st", 24)], writes=[("st", 24)])
                if cfg.get('yp_stop', 99) < 8:
                    return
                op("dve", lambda E: E.reciprocal(out=sb_, in_=sb_), reads=[("st", 24)], writes=[("st", 24)])
                if cfg.get('yp_stop', 99) < 9:
                    return
                op("dve", lambda E: E.tensor_tensor(out=v3(yc), in0=y3, in1=bc_last(sa, 64), op=ALU.subtract), reads=PSK(pyc) + [("st", 16)] + SHR, writes=[("ysq",)])
                if cfg.get('yp_stop', 99) < 10:
                    return
                op("dve", lambda E: E.tensor_tensor(out=v3(yh), in0=v3(yc), in1=bc_last(sb_, 64), op=ALU.mult), reads=[("ysq",), ("st", 24)] + SHR, writes=[("yh",), ("yh", 0), ("yh", 1)])

            def scores_inverse(hg, b, c, SAMP):
                rs = slice(c * 64, c * 64 + 64)
                ci = 8 if SAMP else 2 * b + c
                cc = b * 128 + c * 64
                msu, mu_, msl = (cs("s_su"), cs("s_u"), cs("s_sl")) if SAMP else (cs("m_su"), cs("m_u"), cs("m_sl"))
                pA = [None, None]; pBq = [None, None]; pC = [None, None]
                for hp in range(2):
                    hs = slice(hp * 64, hp * 64 + 64)
                    pA[hp], pBq[hp], pC[hp] = ps_take(), ps_take(), ps_take()
                    for kl in range(4):
                        op("pe", lambda E, kl=kl, hp=hp, hs=hs: E.matmul(psb[pA[hp]][rs, kl * 128:(kl + 1) * 128], lhsT=KtT[hs, kl, cc:cc + 64], rhs=QR[hs, kl, ci, :], start=True, stop=True), reads=GK + QK + SHR, writes=PSK(pA[hp]))
                        op("pe", lambda E, kl=kl, hp=hp, hs=hs: E.matmul(psb[pBq[hp]][rs, kl * 128:(kl + 1) * 128], lhsT=BtT[hs, kl, cc:cc + 64], rhs=QR[hs, kl, ci, :], start=True, stop=True), reads=GK + QK + SHR, writes=PSK(pBq[hp]))
                        op("pe", lambda E, kl=kl, hp=hp, hs=hs: E.matmul(psb[pC[hp]][rs, kl * 64:(kl + 1) * 64], lhsT=QR[hs, kl, ci, 0:64], rhs=BtT[hs, kl, cc:cc + 64], start=True, stop=True), reads=GK + QK + SHR, writes=PSK(pC[hp]))
                    h4 = lambda t, hp=hp: t[rs, :].rearrange("p (k h v) -> p k h v", k=4, h=2)[:, :, hp, :]
                    stA, stB, stC = Xs[c], Us[c], yh
                    op("act", lambda E, hp=hp: E.copy(out=stA[rs, :], in_=psb[pA[hp]][rs, :]), reads=PSK(pA[hp]) + ZP + SHR, writes=[("Xs", c)])
                    op("act", lambda E, hp=hp: E.copy(out=stB[rs, :], in_=psb[pBq[hp]][rs, :]), reads=PSK(pBq[hp]) + ZP + SHR, writes=[("Us", c)])
                    op("act", lambda E, hp=hp: E.copy(out=stC[rs, 0:256], in_=psb[pC[hp]][rs, 0:256]), reads=PSK(pC[hp]) + SHR, writes=[("yh", c)])
                    pa3 = stA[rs, :].rearrange("p (k q) -> p k q", k=4)
                    pb3 = stB[rs, :].rearrange("p (k q) -> p k q", k=4)
                    pc3 = stC[rs, 0:256].rearrange("p (k q) -> p k q", k=4)
                    op("pool", lambda E, h4=h4, pa3=pa3: E.tensor_tensor(out=h4(LkT[c]), in0=pa3[:, :, 0:64], in1=bc_mid(msu[rs, :], 4), op=ALU.mult), reads=[("Xs", c), ("cstb",)] + ZP + SHR, writes=[("LkT", c)])
                    op("pool", lambda E, h4=h4, pa3=pa3: E.tensor_tensor(out=h4(MkT[c]), in0=pa3[:, :, 64:128], in1=bc_mid(mu_[rs, :], 4), op=ALU.mult), reads=[("Xs", c), ("cstb",)] + ZP + SHR, writes=[("MkT", c)])
                    op("dve", lambda E, h4=h4, pb3=pb3: E.tensor_tensor(out=h4(NN[c][0]), in0=pb3[:, :, 0:64], in1=bc_mid(msu[rs, :], 4), op=ALU.mult), reads=[("Us", c), ("cstb",)] + SHR, writes=[("NN", c, 0)])
                    op("pool", lambda E, h4=h4, pb3=pb3: E.tensor_tensor(out=h4(MbT[c]), in0=pb3[:, :, 64:128], in1=bc_mid(mu_[rs, :], 4), op=ALU.mult), reads=[("Us", c), ("cstb",)] + ZP + SHR, writes=[("MbT", c)])
                    op("dve", lambda E, h4=h4, pc3=pc3: E.tensor_tensor(out=h4(AA[c][0]), in0=pc3, in1=bc_mid(msl[rs, :], 4), op=ALU.mult), reads=[("yh", c), ("cstb",)] + SHR, writes=[("AA", c, 0)])
                op("pool", lambda E: E.tensor_tensor(out=h8(ZZ[c][0][rs, :]), in0=h8(NN[c][0][rs, :]), in1=bc_mid(eqm[rs, :], 8), op=ALU.add), reads=[("NN", c, 0), ("cstb",)] + SHR, writes=[("ZZ", c, 0)])
                nlev = 1 if SAMP else 5
                for st_ in range(nlev + 1):
                    s_, d_ = st_ % 2, (st_ + 1) % 2
                    do_sq = st_ < nlev
                    lastsq = (st_ == nlev - 1)
                    do_prod = st_ >= 1
                    if do_sq:
                        pa_ = ps_take()
                        for hi in range(8):
                            op("pe", lambda E, hi=hi, s_=s_, pa_=pa_: E.matmul(psb[pa_][rs, hi * 64:(hi + 1) * 64], lhsT=h8(NN[c][s_])[rs, hi, :], rhs=h8(AA[c][s_])[rs, hi, :], start=True, stop=True), reads=[("NN", c, s_), ("AA", c, s_)] + SHR, writes=PSK(pa_))
                        if not lastsq:
                            pn_ = ps_take()
                            for hi in range(8):
                                op("pe", lambda E, hi=hi, s_=s_, pn_=pn_: E.matmul(psb[pn_][rs, hi * 64:(hi + 1) * 64], lhsT=h8(AA[c][s_])[rs, hi, :], rhs=h8(NN[c][s_])[rs, hi, :], start=True, stop=True), reads=[("NN", c, s_), ("AA", c, s_)] + SHR, writes=PSK(pn_))
                    if do_prod:
                        zs, zd = (st_ - 1) % 2, st_ % 2
                        pz_ = ps_take()
                        for hi in range(8):
                            op("pe", lambda E, hi=hi, s_=s_, zs=zs, pz_=pz_: E.matmul(psb[pz_][rs, hi * 64:(hi + 1) * 64], lhsT=h8(AA[c][s_])[rs, hi, :], rhs=h8(ZZ[c][zs])[rs, hi, :], start=True, stop=True), reads=[("AA", c, s_), ("ZZ", c, zs)] + SHR, writes=PSK(pz_))
                    if do_sq:
                        op("act", lambda E, d_=d_, pa_=pa_: E.copy(out=AA[c][d_][rs, :], in_=psb[pa_][rs, :]), reads=PSK(pa_) + SHR, writes=[("AA", c, d_)])
                        if not lastsq:
                            op("act", lambda E, d_=d_, pn_=pn_: E.copy(out=NN[c][d_][rs, :], in_=psb[pn_][rs, :]), reads=PSK(pn_) + SHR, writes=[("NN", c, d_)])
                    if do_prod:
                        op("dve", lambda E, zs=zs, zd=zd, pz_=pz_: E.tensor_tensor(out=ZZ[c][zd][rs, :], in0=psb[pz_][rs, :], in1=ZZ[c][zs][rs, :], op=ALU.add), reads=PSK(pz_) + [("ZZ", c, zs)] + ZP + SHR, writes=[("ZZ", c, zd)])

            def chunk(hg, b, c):
                rs = slice(c * 64, c * 64 + 64)
                ci = 2 * b + c
                TT_ = ZZ[c][1]
                px, py, pu, ph = ps_take(), ps_take(), ps_take(), ps_take()
                pair = lambda pb, kl: pairv(psb[pb][rs, :], kl)
                for kl in range(4):
                    op("pe", lambda E, kl=kl: E.matmul(pair(px, kl), lhsT=QR[:, kl, ci, 0:64], rhs=Hbd[hg][c][:, kl, :], start=True, stop=False), reads=QK + [("Hbd", hg, c, 0), ("Hbd", hg, c, 1)] + SHR, writes=PSK(px))
                    for hi in (2 * kl, 2 * kl + 1):
                        op("pe", lambda E, hi=hi: E.matmul(psb[px][rs, hi * 64:(hi + 1) * 64], lhsT=h8(LkT[c])[:, hi, :], rhs=Vc[c][:, hi * 64:(hi + 1) * 64], start=False, stop=(hi % 2 == 1)), reads=[("LkT", c), ("Vc", c)] + ZP + SHR, writes=PSK(px))
                op("act", lambda E: E.copy(out=Xs[c][rs, :], in_=psb[px][rs, :]), reads=PSK(px) + ZP + SHR, writes=[("Xs", c)])
                if cfg.get("chain_stop", 9) <= 1:
                    return
                for hi in range(8):
                    op("pe", lambda E, hi=hi: E.matmul(psb[pu][rs, hi * 64:(hi + 1) * 64], lhsT=h8(TT_)[:, hi, :], rhs=Xs[c][:, hi * 64:(hi + 1) * 64], start=True, stop=True), reads=[("ZZ", c, 1), ("Xs", c)] + ZP + SHR, writes=PSK(pu))
                op("act", lambda E: E.copy(out=Us[c][rs, :], in_=psb[pu][rs, :]), reads=PSK(pu) + ZP + SHR, writes=[("Us", c)])
                if cfg.get("chain_stop", 9) <= 2:
                    return
                for kl in range(4):
                    op("pe", lambda E, kl=kl: E.matmul(psb[ph][:, kl * 128:(kl + 1) * 128], lhsT=Ktok[:, kl * 128:(kl + 1) * 128], rhs=pairv(Vc[c], kl), start=True, stop=False), reads=[("Ktok",), ("Vc", c)] + ZP + SHR, writes=PSK(ph))
                    op("pe", lambda E, kl=kl: E.matmul(psb[ph][:, kl * 128:(kl + 1) * 128], lhsT=Btok[:, kl * 128:(kl + 1) * 128], rhs=pairv(Us[c], kl), start=False, stop=True), reads=[("Btok",), ("Us", c)] + ZP + SHR, writes=PSK(ph))
                for hp in range(2):
                    hs = slice(hp * 64, hp * 64 + 64)
                    phv = psb[ph][hs, :].rearrange("p (k q) -> p k q", k=4)[:, :, hp * 64:(hp + 1) * 64]
                    op("dve", lambda E, hs=hs, phv=phv: E.tensor_tensor(out=Hst[hg][hs, :, :], in0=phv, in1=Hst[hg][hs, :, :], op=ALU.add), reads=PSK(ph) + [("H32", hg, hp)], writes=[("H32", hg, hp)])
                    op("dve", lambda E, hs=hs: E.tensor_tensor(out=Hst[hg][hs, :, :], in0=Hst[hg][hs, :, :], in1=bc_last(WC[hs, :, ci], 64), op=ALU.mult), reads=[("H32", hg, hp)] + [("WC", k) for k in range(4)] + SHR, writes=[("H32", hg, hp)])
                    op("act", lambda E, hs=hs, hp=hp: E.copy(out=Hbd[hg][1 - c][hs, :, hp * 64:(hp + 1) * 64], in_=Hst[hg][hs, :, :]), reads=[("H32", hg, hp)], writes=[("Hbd", hg, 1 - c, hp)])
                for kl in range(4):
                    op("pe", lambda E, kl=kl: E.matmul(pair(py, kl), lhsT=QR[:, kl, ci, 64:128], rhs=Hbd[hg][c][:, kl, :], start=True, stop=False), reads=QK + [("Hbd", hg, c, 0), ("Hbd", hg, c, 1)] + SHR, writes=PSK(py))
                    for hi in (2 * kl, 2 * kl + 1):
                        op("pe", lambda E, hi=hi: E.matmul(psb[py][rs, hi * 64:(hi + 1) * 64], lhsT=h8(MkT[c])[:, hi, :], rhs=Vc[c][:, hi * 64:(hi + 1) * 64], start=False, stop=False), reads=[("MkT", c), ("Vc", c)] + ZP + SHR, writes=PSK(py))
                    for hi in (2 * kl, 2 * kl + 1):
                        op("pe", lambda E, hi=hi: E.matmul(psb[py][rs, hi * 64:(hi + 1) * 64], lhsT=h8(MbT[c])[:, hi, :], rhs=Us[c][:, hi * 64:(hi + 1) * 64], start=False, stop=(hi % 2 == 1)), reads=[("MbT", c), ("Us", c)] + ZP + SHR, writes=PSK(py))
                ypost(py, rs, 0, 8)

            def sample_chain(hg):
                rs = slice(0, 64)
                c = 0
                WK = lambda i: [("w", i), ("wB", i)]
                scores_inverse(hg, 4, 0, True)
                TT_ = ZZ[0][1]
                qmk = cs("qmask").rearrange("p (j t) -> p j t", j=16)
                rowm = cs("rowm")
                w4b, w5b, w6b, w1b = w[4].bitcast(BF16), w[5].bitcast(BF16), w[3].bitcast(BF16), w[1].bitcast(BF16)
                Qm = w4b[:, 0:1024].rearrange("p (j t) -> p j t", j=16)
                Rm = w4b[:, 1024:2048].rearrange("p (j t) -> p j t", j=16)
                Vm = w5b[:, 0:2048].rearrange("p (j c) -> p j c", j=16)
                Um = w6b[:, 0:2048].rearrange("p (j c) -> p j c", j=16)
                Sn32 = w[0][:, 0:1024].rearrange("p (j k) -> p j k", j=16)
                Snb = w1b[:, 0:1024].rearrange("p (j k) -> p j k", j=16)
                So = w[2][:, 0:1024].rearrange("p (j k) -> p j k", j=16)

                def load_s0(kc_):
                    op("sp", lambda E: E.dma_start(out=Sn32, in_=swkv[:, 2 * kc_:2 * kc_ + 2, :, :].rearrange("s h v k -> (h v) s k")), reads=SHR, writes=WK(0), dma=True)

                def pairk(kl):
                    kc = hg * 4 + kl
                    op("dve", lambda E: E.tensor_tensor(out=Qm, in0=bc_mid(QR[:, kl, 8, 0:64], 16), in1=qmk, op=ALU.mult), reads=QK + [("cstb",)] + SHR, writes=WK(4))
                    op("dve", lambda E: E.tensor_tensor(out=Rm, in0=bc_mid(QR[:, kl, 8, 64:128], 16), in1=qmk, op=ALU.mult), reads=QK + [("cstb",)] + SHR, writes=WK(4))
                    v4 = lambda t: pairv(t, kl).unsqueeze(1).to_broadcast([128, 16, 128])
                    rm4 = rowm.unsqueeze(2).to_broadcast([128, 16, 128])
                    op("dve", lambda E: E.tensor_tensor(out=Vm, in0=v4(Vc[0]), in1=rm4, op=ALU.mult), reads=[("Vc", 0), ("cstb",)] + ZP + SHR, writes=WK(5))
                    if kl == 0:
                        load_s0(kc)
                    op("pool", lambda E: E.tensor_copy(out=Snb, in_=Sn32), reads=WK(0) + SHR, writes=WK(1))
                    if kl + 1 < 4:
                        load_s0(kc + 1)
                    op("pool", lambda E: E.memset(H0bd, 0.0), reads=SHR, writes=[("H0bd",)])
                    pT = [ps_take(), ps_take()]
                    for hp in range(2):
                        hs = slice(hp * 64, hp * 64 + 64)
                        vT_ = psb[pT[hp]][:].bitcast(BF16)
                        for j in range(16):
                            op("pe", lambda E, j=j, hs=hs, vT_=vT_: E.transpose(out=vT_[hs, j * 64:(j + 1) * 64], in_=Snb[hs, j, :], identity=identb[hs, hs]),
                               reads=WK(1) + [("cstb",)] + SHR, writes=PSK(pT[hp]))
                        op("act", lambda E, hs=hs, hp=hp, vT_=vT_: E.copy(out=H0bd[hs, :, hp * 64:(hp + 1) * 64], in_=vT_[hs, :].rearrange("p (j c) -> p j c", j=16)),
                           reads=PSK(pT[hp]) + SHR, writes=[("H0bd",)])
                    px, py, pu = ps_take(), ps_take(), ps_take()
                    pair = lambda pb: pairv(psb[pb][rs, :], kl)
                    for j in range(16):
                        op("pe", lambda E, j=j: E.matmul(pair(px), lhsT=Qm[:, j, :], rhs=H0bd[:, j, :], start=(j == 0), stop=False), reads=WK(4) + [("H0bd",)] + SHR, writes=PSK(px))
                        op("pe", lambda E, j=j: E.matmul(pair(py), lhsT=Rm[:, j, :], rhs=H0bd[:, j, :], start=(j == 0), stop=False), reads=WK(4) + [("H0bd",)] + SHR, writes=PSK(py))
                    his = [2 * kl, 2 * kl + 1]
                    for hi in his:
                        op("pe", lambda E, hi=hi: E.matmul(psb[px][rs, hi * 64:(hi + 1) * 64], lhsT=h8(LkT[0])[:, hi, :], rhs=Vc[0][:, hi * 64:(hi + 1) * 64], start=False, stop=(hi % 2 == 1)), reads=[("LkT", 0), ("Vc", 0)] + ZP + SHR, writes=PSK(px))
                        op("pe", lambda E, hi=hi: E.matmul(psb[py][rs, hi * 64:(hi + 1) * 64], lhsT=h8(MkT[0])[:, hi, :], rhs=Vc[0][:, hi * 64:(hi + 1) * 64], start=False, stop=False), reads=[("MkT", 0), ("Vc", 0)] + ZP + SHR, writes=PSK(py))
                    for hi in his:
                        op("act", lambda E, hi=hi: E.copy(out=Xs[0][rs, hi * 64:(hi + 1) * 64], in_=psb[px][rs, hi * 64:(hi + 1) * 64]), reads=PSK(px) + ZP + SHR, writes=[("Xs", 0)])
                    for hi in his:
                        op("pe", lambda E, hi=hi: E.matmul(psb[pu][rs, hi * 64:(hi + 1) * 64], lhsT=h8(TT_)[:, hi, :], rhs=Xs[0][:, hi * 64:(hi + 1) * 64], start=True, stop=True), reads=[("ZZ", 0, 1), ("Xs", 0)] + ZP + SHR, writes=PSK(pu))
                    for hi in his:
                        op("act", lambda E, hi=hi: E.copy(out=Us[0][rs, hi * 64:(hi + 1) * 64], in_=psb[pu][rs, hi * 64:(hi + 1) * 64]), reads=PSK(pu) + ZP + SHR, writes=[("Us", 0)])
                    for hi in his:
                        op("pe", lambda E, hi=hi: E.matmul(psb[py][rs, hi * 64:(hi + 1) * 64], lhsT=h8(MbT[0])[:, hi, :], rhs=Us[0][:, hi * 64:(hi + 1) * 64], start=False, stop=(hi % 2 == 1)), reads=[("MbT", 0), ("Us", 0)] + ZP + SHR, writes=PSK(py))
                    for hi in his:
                        ypost(py, rs, hi, 1)
                    op("pool", lambda E: E.tensor_tensor(out=H0bd, in0=H0bd, in1=bc_last(WCs[:, kl, :], 128), op=ALU.mult), reads=[("H0bd",), ("WCs", kl)] + SHR, writes=[("H0bd",)])
                    op("dve", lambda E: E.tensor_tensor(out=Um, in0=v4(Us[0]), in1=rm4, op=ALU.mult), reads=[("Us", 0), ("cstb",)] + ZP + SHR, writes=WK(3))
                    for jb in range(4):
                        pS = ps_take()
                        for jj in range(4):
                            j = jb * 4 + jj
                            o_ = psb[pS][:, jj * 128:(jj + 1) * 128]
                            op("pe", lambda E, j=j, o_=o_: E.matmul(o_, lhsT=H0bd[:, j, :], rhs=identb, start=True, stop=False), reads=[("H0bd",), ("cstb",)] + SHR, writes=PSK(pS))
                            op("pe", lambda E, j=j, o_=o_: E.matmul(o_, lhsT=Vm[:, j, :], rhs=Khtok[:, kl * 128:(kl + 1) * 128], start=False, stop=False), reads=WK(5) + [("Khtok",)] + ZP + SHR, writes=PSK(pS))
                            op("pe", lambda E, j=j, o_=o_: E.matmul(o_, lhsT=Um[:, j, :], rhs=Bhtok[:, kl * 128:(kl + 1) * 128], start=False, stop=True), reads=WK(3) + [("Bhtok",)] + ZP + SHR, writes=PSK(pS))
                        for hp in range(2):
                            hs = slice(hp * 64, hp * 64 + 64)
                            op("act", lambda E, jb=jb, hs=hs, hp=hp, pS=pS: E.copy(out=So[hs, jb * 4:(jb + 1) * 4, :], in_=psb[pS][hs, :].rearrange("p (j c) -> p j c", j=4)[:, :, hp * 64:(hp + 1) * 64]),
                               reads=PSK(pS) + SHR, writes=WK(2))
                    op("sp", lambda E: E.dma_start(out=wkv_s[:, 2 * kc:2 * kc + 2, :, :].rearrange("s h v k -> (h v) s k"), in_=So), reads=WK(2) + SHR, writes=[("o_wkv_s", kc)], dma=True)
                for kl in range(4):
                    pairk(kl)

            def do_block(hg, b):
                SAMP = (b == 4)
                r = 64 if SAMP else 128
                bc0 = b * 128
                pK, pB, pV = ps_take(), ps_take(), ps_take()
                vK, vB, vV = [psb[x][:].bitcast(BF16) for x in (pK, pB, pV)]
                for kl in range(4):
                    op("pe", lambda E, kl=kl: E.transpose(out=vK[0:r, kl * 128:(kl + 1) * 128], in_=KtT[:, kl, bc0:bc0 + r], identity=identb), reads=[("gbuf", kl), ("cstb",)], writes=PSK(pK))
                    op("pe", lambda E, kl=kl: E.transpose(out=vB[0:r, kl * 128:(kl + 1) * 128], in_=BtT[:, kl, bc0:bc0 + r], identity=identb), reads=[("gbuf", 4 + kl), ("cstb",)], writes=PSK(pB))
                    op("pe", lambda E, kl=kl: E.transpose(out=vV[0:r, kl * 128:(kl + 1) * 128], in_=RKV[2][:, kl, bc0:bc0 + r], identity=identb), reads=[("rkv", 2, kl), ("cstb",)] + SHR, writes=PSK(pV))
                op("act", lambda E: E.copy(out=Ktok[0:r, :], in_=vK[0:r, 0:512]), reads=PSK(pK) + SHR, writes=[("Ktok",)])
                op("act", lambda E: E.copy(out=Btok[0:r, :], in_=vB[0:r, 0:512]), reads=PSK(pB) + SHR, writes=[("Btok",)])
                for c in ([0] if SAMP else [0, 1]):
                    rs = slice(c * 64, c * 64 + 64)
                    op("dve", lambda E, c=c, rs=rs: E.tensor_copy(out=Vc[c][rs, :], in_=vV[rs, 0:512]), reads=PSK(pV) + ZP + SHR, writes=[("Vc", c)])
                if SAMP:
                    pK2, pB2 = ps_take(), ps_take()
                    vK2, vB2 = psb[pK2][:].bitcast(BF16), psb[pB2][:].bitcast(BF16)
                    for kl in range(4):
                        op("pe", lambda E, kl=kl: E.transpose(out=vK2[0:64, kl * 128:(kl + 1) * 128], in_=Khs[:, kl, :], identity=identb), reads=[("Khs", kl), ("cstb",)] + SHR, writes=PSK(pK2))
                        op("pe", lambda E, kl=kl: E.transpose(out=vB2[0:64, kl * 128:(kl + 1) * 128], in_=Bhs[:, kl, :], identity=identb), reads=[("Bhs", kl), ("cstb",)] + SHR, writes=PSK(pB2))
                    op("act", lambda E: E.copy(out=Khtok[0:64, :], in_=vK2[0:64, 0:512]), reads=PSK(pK2) + ZP + SHR, writes=[("Khtok",)])
                    op("act", lambda E: E.copy(out=Bhtok[0:64, :], in_=vB2[0:64, 0:512]), reads=PSK(pB2) + ZP + SHR, writes=[("Bhtok",)])
                    if cfg.get('rw_level', 4) >= 4:
                        sample_chain(hg)
                else:
                    interleave([(lambda: scores_inverse(hg, b, 0, False), [0, 1, 2, 3]), (lambda: scores_inverse(hg, b, 1, False), [4, 5, 6, 7])])
                    for c in (0, 1):
                        chunk(hg, b, c)
                pzt = ps_take()
                vz = psb[pzt][:].bitcast(BF16)
                for kl in range(4):
                    op("pe", lambda E, kl=kl: E.transpose(out=vz[:, kl * 128:kl * 128 + r], in_=pairv(yh[0:r, :], kl), identity=identb[0:r, 0:r]), reads=[("yh",), ("yh", 0), ("yh", 1), ("cstb",)] + SHR, writes=PSK(pzt))
                for kl in range(4):
                    kc = hg * 4 + kl
                    op("dve", lambda E, kl=kl, kc=kc: E.tensor_scalar(out=zt[:, kl, 0:r], in0=vz[:, kl * 128:kl * 128 + r], scalar1=pc(PI["lnx_g"], kc), scalar2=pc(PI["lnx_b"], kc), op0=ALU.mult, op1=ALU.add), reads=PSK(pzt) + [("pcol",)] + SHR, writes=[("zt",)])
                op("dve", lambda E: E.tensor_tensor(out=zt[:, :, 0:r], in0=zt[:, :, 0:r], in1=bon[:, :, bc0:bc0 + r], op=ALU.add), reads=[("zt",)] + [("bon", k) for k in range(4)] + SHR, writes=[("zt",)])
                op("dve", lambda E: E.tensor_tensor(out=zb[:, hg * 4:(hg + 1) * 4, bc0:bc0 + r], in0=zt[:, :, 0:r], in1=gT[:, :, bc0:bc0 + r], op=ALU.mult), reads=[("zt",)] + [("gT", k) for k in range(4)] + SHR, writes=[("zb", hg, b)])

            def do_hg(hg):
                for kind in range(3):
                    sw = wnext()
                    def ev_mix(oc, pvs, kind=kind):
                        mix(kind * 8 + hg * 4 + oc, pvs, RKV[kind][:, oc, :], ("rkv", kind, oc))
                    ws_mm(sw, 4, hT, HTK, segs_r, KC, ev_mix)
                for kl2 in range(0, 4, 2):
                    interleave([(lambda kl=kl2: elem(hg, kl, 0), [0, 1, 2, 3]), (lambda kl=kl2 + 1: elem(hg, kl, 1), [4, 5, 6, 7])])
                for b in range(nblk):
                    if cfg.get('rw_level', 4) >= 2:
                        do_block(hg, b)
                if LAST:
                    pw = ps_take()
                    Ho = w[3][:, 0:512].rearrange("p (k c) -> p k c", k=4)
                    for kl in range(4):
                        op("pe", lambda E, kl=kl: E.transpose(out=psb[pw][0:64, kl * 128:(kl + 1) * 128], in_=Hst[hg][:, kl, :], identity=identf), reads=[("H32", hg, 0), ("H32", hg, 1), ("cst",)], writes=PSK(pw))
                    op("act", lambda E: E.copy(out=Ho[0:64, :, :], in_=psb[pw][0:64, :].rearrange("p (k c) -> p k c", k=4)), reads=PSK(pw) + SHR, writes=WK(3) + [("wB", 3)])
                    dstv = wkv_p.rearrange("(kc hp) v k -> v kc hp k", hp=2)[:, hg * 4:(hg + 1) * 4, :, :]
                    op("sp", lambda E: E.dma_start(out=dstv, in_=Ho[0:64, :, :].rearrange("p k (hp c) -> p k hp c", hp=2)), reads=WK(3) + [("wB", 3)] + SHR, writes=[("o_wkv_p", hg)], dma=True)
            for hg in range(2):
                do_hg(hg)
            tap("zb", zb[:, :, :], [128, KC, 576], [("zb", hg, b) for hg in range(2) for b in range(nblk)])
            for g in range(2):
                sw = wnext()
                def ev_gb(oc, pvs, g=g):
                    kc = g * 4 + oc
                    for (pv, pb, c0, c1) in pvs:
                        op("act", lambda E, pv=pv, kc=kc, c0=c0, c1=c1: E.activation(out=gbuf[:, kc, c0:c1], in_=pv, func=AF.Sigmoid), reads=PSK(pb), writes=[("gbuf", kc)])
                ws_mm(sw, 4, hT, HTK, segs, KC, ev_gb)
            ZBK = [("zb", hg, b) for hg in range(2) for b in range(nblk)]
            for g in range(2):
                sw = wnext()
                def ev_ob(oc, pvs, g=g):
                    kc = g * 4 + oc
                    for (pv, pb, c0, c1) in pvs:
                        op("dve", lambda E, pv=pv, kc=kc, c0=c0, c1=c1: E.tensor_tensor(out=merged[:, kc, c0:c1], in0=pv, in1=gbuf[:, kc, c0:c1], op=ALU.mult), reads=PSK(pb) + [("gbuf", kc)], writes=[("merged", kc)])
                ws_mm(sw, 4, zb, ZBK + SHR, segs, KC, ev_ob)

        def phase_n(p):
            P0 = (p == 0)
            LAST = (p == NPASS - 1)
            nblk = 5 if P0 else 4
            brows = lambda b: 64 if b == 4 else 128
            bcol = lambda b: b * 128
            load_gain(0, 0)
            for b in range(nblk):
                r = brows(b)
                xi = b % 2
                src = x_s[0:64, :] if b == 4 else x_p[p * TT + b * 128: p * TT + (b + 1) * 128, :]
                op("sp", lambda E, xi=xi, r=r, src=src: E.dma_start(out=xs[xi][0:r, :], in_=src), writes=[("xs", xi)], dma=True)
                op("act", lambda E, xi=xi, r=r: E.activation(out=junk[0:r, :], in_=xs[xi][0:r, :], func=AF.Square, accum_out=st[0:r, 0:1]),
                   reads=[("xs", xi)], writes=[("junk",), ("st", 0)])
                rstd_from_ssq(st[0:r, 0:1], st[0:r, 1:2], D, 1e-6, [("st", 0)], [("st", 1)])
                need32 = (b == 4) or (LAST and b == 3)
                if need32:
                    op("dve", lambda E, xi=xi, r=r: E.scalar_tensor_tensor(out=junk[0:r, :], in0=xs[xi][0:r, :], scalar=st[0:r, 1:2], in1=gain[0][0:r, :], op0=ALU.mult, op1=ALU.mult),
                       reads=[("xs", xi), ("st", 1), ("gain", 0)], writes=[("junk",)])
                    if b == 4:
                        dst = shift_s.rearrange("(s o) d -> s o d", o=1)
                        op("sp", lambda E, dst=dst: E.dma_start(out=dst[:, 0, :], in_=junk[3:64:4, :]), reads=[("junk",)], writes=[("o_shift_s",)], dma=True)
                    else:
                        op("sp", lambda E: E.dma_start(out=shift_p[0:1, :], in_=junk[127:128, :]), reads=[("junk",)], writes=[("o_shift_p",)], dma=True)
                op("dve", lambda E, xi=xi, r=r: E.scalar_tensor_tensor(out=hb[0:r, :], in0=xs[xi][0:r, :], scalar=st[0:r, 1:2], in1=gain[0][0:r, :], op0=ALU.mult, op1=ALU.mult),
                   reads=[("xs", xi), ("st", 1), ("gain", 0)], writes=[("hb",)])
                pb = ps_take()
                pv = psb[pb][:].bitcast(BF16)
                for kc in range(KC):
                    op("pe", lambda E, kc=kc, r=r, pv=pv: E.transpose(out=pv[:, kc * 128: kc * 128 + r], in_=hb[0:r, kc * 128:(kc + 1) * 128], identity=identb[0:r, 0:r]),
                       reads=[("hb",), ("cstb",)], writes=PSK(pb))
                op("act", lambda E, b=b, r=r, pv=pv: E.copy(out=hT[:, :, bcol(b): bcol(b) + r], in_=pv.rearrange("p (k c) -> p k c", k=KC)[:, :, 0:r]),
                   reads=PSK(pb), writes=[("hT", b)])
            if P0:
                op("sp", lambda E: E.dma_start(out=xs[0][0:16, :], in_=sshift[:, :]), writes=[("xs", 0)], dma=True)
                op("dve", lambda E: E.tensor_copy(out=hb[0:16, :], in_=xs[0][0:16, :]), reads=[("xs", 0)], writes=[("hb",)])
                pb = ps_take()
                pv = psb[pb][:].bitcast(BF16)
                for kc in range(KC):
                    op("pe", lambda E, kc=kc, pv=pv: E.transpose(out=pv[:, kc * 128: kc * 128 + 16], in_=hb[0:16, kc * 128:(kc + 1) * 128], identity=identb[0:16, 0:16]),
                       reads=[("hb",), ("cstb",)], writes=PSK(pb))
                op("act", lambda E, pv=pv: E.copy(out=hT[:, :, 576:592], in_=pv.rearrange("p (k c) -> p k c", k=KC)[:, :, 0:16]),
                   reads=PSK(pb), writes=[("hT", 5)])
            HTK = [("hT", b) for b in range(nblk)] + ([("hT", 5)] if P0 else [])
            tap("hT", hT[:, :, :], [128, KC, NCOL], HTK)


        def do_pass(p):
            P0 = (p == 0)
            LAST = (p == NPASS - 1)
            nblk = 5 if P0 else 4
            segs = [(0, 512)] + ([(512, 576)] if P0 else [])
            segs_r = [(0, 512)] + ([(512, 592)] if P0 else [])
            ncolu = 576 if P0 else 512
            brows = lambda b: 64 if b == 4 else 128
            bcol = lambda b: b * 128

            if p == 0:
                phase_n(0)
            HTK = [("hT", b) for b in range(nblk)] + ([("hT", 5)] if P0 else [])
            def ws_mm(ws, noc, act_t, act_keys, sg, nk, evac, lhs=None):
                for oc in range(noc):
                    pbs = [ps_take() for _ in sg]
                    for si, (c0, c1) in enumerate(sg):
                        for k in range(nk):
                            lh = (lhs(k, oc) if lhs else wb[ws][:, k, oc * 128:(oc + 1) * 128])
                            op("pe", lambda E, lh=lh, k=k, c0=c0, c1=c1, pb=pbs[si]: E.matmul(psb[pb][:, 0:c1 - c0], lhsT=lh, rhs=act_t[:, k, c0:c1], start=(k == 0), stop=(k == nk - 1)),
                               reads=([("wb", ws)] if ws is not None else []) + act_keys, writes=PSK(pbs[si]))
                    evac(oc, [(psb[pb][:, 0:c1 - c0], pb, c0, c1) for pb, (c0, c1) in zip(pbs, sg)])

            if do_rwkv:
                rwkv_phase(p, P0, LAST, nblk, segs, segs_r, HTK, ws_mm)
            else:
                op("pool", lambda E: E.memset(merged[:], 0.0), writes=[("merged", k) for k in range(KC)])

            phase_barrier()
            AR.reset()
            UW = 30 + 576
            uext = AR.take(KC * UW, BF16).rearrange("p (k c) -> p k c", k=KC)
            uexs = AR.take(KC * NSEQ * 34, BF16).rearrange("p (k s c) -> p k s c", k=KC, s=NSEQ)
            cT = AR.take(KC * 576, F32).rearrange("p (k c) -> p k c", k=KC)
            cbf = AR.take(KC * 576, BF16).rearrange("p (k c) -> p k c", k=KC)
            csq = AR.take(KC * 576, BF16).rearrange("p (k c) -> p k c", k=KC)
            csl = AR.take(KC * 576, BF16).rearrange("p (k c) -> p k c", k=KC)
            sgt = AR.take(4 * 576, F32).rearrange("p (k c) -> p k c", k=4)
            mean_sb = AR.take(576, F32)
            rstd_sb = AR.take(576, F32)
            u32 = AR.take(KC * 64, F32).rearrange("p (k c) -> p k c", k=KC)
            tmpb = AR.take(576, BF16)
            dtmp = AR.take(576, F32)

            op("pool", lambda E: E.tensor_copy(out=uext[:, :, 0:30], in_=uhist[:, :, :]), reads=[("uhist",)] + SHR, writes=[("uext", "h")])
            for g in range(2):
                sw = wnext()
                def ev_gate(oc, pvs):
                    for (pv, pb, c0, c1) in pvs:
                        op("act", lambda E, pv=pv, oc=oc, c0=c0, c1=c1: E.activation(out=sgt[:, oc, c0:c1], in_=pv, func=AF.Sigmoid),
                           reads=PSK(pb) + SHR, writes=[("sgt", oc)])
                ws_mm(sw, 4, hT, HTK, segs, KC, ev_gate)
                sw = wnext()
                def ev_val(oc, pvs, g=g):
                    kc = g * 4 + oc
                    for (pv, pb, c0, c1) in pvs:
                        if c0 == 0:
                            op("dve", lambda E, pv=pv, oc=oc, kc=kc: E.tensor_tensor(out=uext[:, kc, 30:542], in0=pv, in1=sgt[:, oc, 0:512], op=ALU.mult),
                               reads=PSK(pb) + [("sgt", oc)] + SHR, writes=[("uext", kc)])
                            if LAST:
                                op("dve", lambda E, pv=pv, oc=oc, kc=kc: E.tensor_tensor(out=u32[:, kc, 0:30], in0=pv[:, 482:512], in1=sgt[:, oc, 482:512], op=ALU.mult),
                                   reads=PSK(pb) + [("sgt", oc)] + SHR, writes=[("u32", kc)])
                        else:
                            op("dve", lambda E, pv=pv, oc=oc, kc=kc: E.tensor_tensor(out=uexs[:, kc, :, 30:34], in0=pv.rearrange("p (s t) -> p s t", t=4), in1=sgt[:, oc, 512:576].rearrange("p (s t) -> p s t", t=4), op=ALU.mult),
                               reads=PSK(pb) + [("sgt", oc)] + SHR, writes=[("uexs", kc)])
                            op("dve", lambda E, pv=pv, oc=oc, kc=kc: E.tensor_tensor(out=u32[:, kc, 0:64], in0=pv, in1=sgt[:, oc, 512:576], op=ALU.mult),
                               reads=PSK(pb) + [("sgt", oc)] + SHR, writes=[("u32", kc)])
                ws_mm(sw, 4, hT, HTK, segs, KC, ev_val)
            UK = [("uext", kc) for kc in range(KC)] + [("uext", "h")]
            op("pool", lambda E: E.tensor_copy(out=uhist[:, :, :], in_=uext[:, :, 512:542]), reads=UK + SHR, writes=[("uhist",)])
            def emit_u32(nrows, dst_fn):
                pbA, pbB = ps_take(), ps_take()
                for kc in range(KC):
                    pb = pbA if kc < 4 else pbB
                    op("pe", lambda E, kc=kc, pb=pb: E.transpose(out=psb[pb][0:nrows, (kc % 4) * 128:(kc % 4 + 1) * 128], in_=u32[:, kc, 0:nrows], identity=identf),
                       reads=[("u32", kc), ("cst",)] + SHR, writes=PSK(pb))
                op("act", lambda E: E.copy(out=junk[0:nrows, 0:512], in_=psb[pbA][0:nrows, :]), reads=PSK(pbA), writes=[("junk",)])
                op("act", lambda E: E.copy(out=junk[0:nrows, 512:1024], in_=psb[pbB][0:nrows, :]), reads=PSK(pbB), writes=[("junk",)])
                dst_fn()
            if LAST:
                emit_u32(30, lambda: op("sp", lambda E: E.dma_start(out=conv_p[:, :], in_=junk[0:30, :]), reads=[("junk",)], writes=[("o_conv_p",)], dma=True))
            if P0:
                def dst():
                    for t in range(4):
                        op("sp", lambda E, t=t: E.dma_start(out=conv_s[:, 26 + t, :], in_=junk[t:64:4, :]), reads=[("junk",)], writes=[("o_conv_s", t)], dma=True)
                    op("sp", lambda E: E.dma_start(out=conv_s[:, 0:26, :], in_=sconv[:, 4:30, :]), writes=[("o_conv_s", 9)], dma=True)
                emit_u32(64, dst)
                for q in range(4):
                    op("pool", lambda E, q=q: E.dma_start(out=hb[0:120, :], in_=sconv[4 * q:4 * q + 4, :, :].rearrange("s r d -> (s r) d")), writes=[("hb",)], dma=True)
                    pb = ps_take()
                    pv = psb[pb][:].bitcast(BF16)
                    for kc in range(KC):
                        op("pe", lambda E, kc=kc, pv=pv: E.transpose(out=pv[:, kc * 120:(kc + 1) * 120], in_=hb[0:120, kc * 128:(kc + 1) * 128], identity=identb[0:120, 0:120]),
                           reads=[("hb",), ("cstb",)], writes=PSK(pb))
                    for kc in range(KC):
                        op("act", lambda E, kc=kc, q=q, pv=pv: E.copy(out=uexs[:, kc, 4 * q:4 * q + 4, 0:30], in_=pv[:, kc * 120:(kc + 1) * 120].rearrange("p (s r) -> p s r", s=4)),
                           reads=PSK(pb) + SHR, writes=[("uexs", kc, q)])
            dcur = 0
            for kc in range(KC):
                pbm = ps_take()
                pbs = ps_take() if P0 else None
                for j in range(31):
                    d = dcur % 8
                    dcur += 1
                    op("dve", lambda E, d=d, kc=kc, j=j: E.tensor_scalar(out=diag[d][:], in0=identb, scalar1=pcol[:, PC_CW + kc * 31 + j: PC_CW + kc * 31 + j + 1], scalar2=None, op0=ALU.mult),
                       reads=[("cstb",), ("pcol",)], writes=[("diag", d)])
                    op("pe", lambda E, d=d, kc=kc, j=j, pbm=pbm: E.matmul(psb[pbm][:, 0:512], lhsT=diag[d][:], rhs=uext[:, kc, j:j + 512], start=(j == 0), stop=(j == 30)),
                       reads=[("diag", d), ("uext", kc), ("uext", "h")] + SHR, writes=PSK(pbm))
                    if P0:
                        op("pe", lambda E, d=d, kc=kc, j=j, pbs=pbs: E.matmul(psb[pbs][:, 0:64].rearrange("p (s t) -> p s t", t=4), lhsT=diag[d][:], rhs=uexs[:, kc, :, j:j + 4], start=(j == 0), stop=(j == 30)),
                           reads=[("diag", d), ("uexs", kc)] + [("uexs", kc, q) for q in range(4)] + SHR, writes=PSK(pbs))
                for (pb, c0, c1) in [(pbm, 0, 512)] + ([(pbs, 512, 576)] if P0 else []):
                    op("act", lambda E, pb=pb, kc=kc, c0=c0, c1=c1: E.activation(out=cT[:, kc, c0:c1], in_=psb[pb][:, 0:c1 - c0], func=AF.Identity, bias=pc(PI["conv_b"], kc), scale=1.0),
                       reads=PSK(pb) + [("pcol",)] + SHR, writes=[("cT", kc)])
                op("pool", lambda E, kc=kc: E.tensor_copy(out=cbf[:, kc, 0:ncolu], in_=cT[:, kc, 0:ncolu]), reads=[("cT", kc)] + SHR, writes=[("cbf", kc)])
                op("act", lambda E, kc=kc: E.activation(out=csq[:, kc, 0:ncolu], in_=cT[:, kc, 0:ncolu], func=AF.Square), reads=[("cT", kc)] + SHR, writes=[("csq", kc)])
            odv = cs("odiv", BF16)
            for (c0, c1) in segs:
                pm_, pq_ = ps_take(), ps_take()
                for kc in range(KC):
                    op("pe", lambda E, kc=kc, c0=c0, c1=c1, pm_=pm_: E.matmul(psb[pm_][:, 0:c1 - c0], lhsT=odv, rhs=cbf[:, kc, c0:c1], start=(kc == 0), stop=(kc == KC - 1)),
                       reads=[("cbf", kc), ("cstb",)] + SHR, writes=PSK(pm_))
                for kc in range(KC):
                    op("pe", lambda E, kc=kc, c0=c0, c1=c1, pq_=pq_: E.matmul(psb[pq_][:, 0:c1 - c0], lhsT=odv, rhs=csq[:, kc, c0:c1], start=(kc == 0), stop=(kc == KC - 1)),
                       reads=[("csq", kc), ("cstb",)] + SHR, writes=PSK(pq_))
                op("act", lambda E, c0=c0, c1=c1, pm_=pm_: E.copy(out=mean_sb[:, c0:c1], in_=psb[pm_][:, 0:c1 - c0]), reads=PSK(pm_) + SHR, writes=[("mean_sb", c0)])
                op("dve", lambda E, c0=c0, c1=c1: E.tensor_tensor(out=dtmp[:, c0:c1], in0=mean_sb[:, c0:c1], in1=mean_sb[:, c0:c1], op=ALU.mult), reads=[("mean_sb", c0)] + SHR, writes=[("dtmp",)])
                op("dve", lambda E, c0=c0, c1=c1, pq_=pq_: E.tensor_tensor(out=rstd_sb[:, c0:c1], in0=psb[pq_][:, 0:c1 - c0], in1=dtmp[:, c0:c1], op=ALU.subtract), reads=PSK(pq_) + [("dtmp",)] + SHR, writes=[("rstd_sb", c0)])
                op("act", lambda E, c0=c0, c1=c1: E.activation(out=rstd_sb[:, c0:c1], in_=rstd_sb[:, c0:c1], func=AF.Sqrt, bias=1e-5, scale=1.0), reads=[("rstd_sb", c0)] + SHR, writes=[("rstd_sb", c0)])
                op("dve", lambda E, c0=c0, c1=c1: E.reciprocal(out=rstd_sb[:, c0:c1], in_=rstd_sb[:, c0:c1]), reads=[("rstd_sb", c0)] + SHR, writes=[("rstd_sb", c0)])
            for g in range(2):
                sw = wnext()
                def ev_ga(oc, pvs, g=g):
                    kc = g * 4 + oc
                    for (pv, pb, c0, c1) in pvs:
                        op("act", lambda E, pv=pv, kc=kc, c0=c0, c1=c1: E.activation(out=gbuf[:, kc, c0:c1], in_=pv, func=AF.Sigmoid), reads=PSK(pb), writes=[("gbuf", kc)])
                ws_mm(sw, 4, hT, HTK, segs, KC, ev_ga)
            LNK = [("mean_sb", c0) for c0, _ in segs] + [("rstd_sb", c0) for c0, _ in segs]
            for kc in range(KC):
                op("dve", lambda E, kc=kc: E.tensor_tensor(out=dtmp[:, 0:ncolu], in0=cT[:, kc, 0:ncolu], in1=mean_sb[:, 0:ncolu], op=ALU.subtract), reads=[("cT", kc)] + LNK + SHR, writes=[("dtmp",)])
                op("dve", lambda E, kc=kc: E.tensor_tensor(out=dtmp[:, 0:ncolu], in0=dtmp[:, 0:ncolu], in1=rstd_sb[:, 0:ncolu], op=ALU.mult), reads=[("dtmp",)] + LNK + SHR, writes=[("dtmp",)])
                op("act", lambda E, kc=kc: E.activation(out=csl[:, kc, 0:ncolu], in_=dtmp[:, 0:ncolu], func=AF.Silu, bias=pc(PI["conv_ln_b"], kc), scale=pc(PI["conv_ln_g"], kc)),
                   reads=[("dtmp",), ("pcol",)] + SHR, writes=[("csl", kc)])
            tap("csl", csl[:, :, :], [128, KC, 576], [("csl", kc) for kc in range(KC)])
            for g in range(2):
                sw = wnext()
                def ev_oa(oc, pvs, g=g):
                    kc = g * 4 + oc
                    for (pv, pb, c0, c1) in pvs:
                        op("dve", lambda E, pv=pv, kc=kc, c0=c0, c1=c1: E.tensor_tensor(out=tmpb[:, c0:c1], in0=pv, in1=gbuf[:, kc, c0:c1], op=ALU.mult), reads=PSK(pb) + [("gbuf", kc)] + SHR, writes=[("tmpb",)])
                        op("pool", lambda E, kc=kc, c0=c0, c1=c1: E.tensor_tensor(out=merged[:, kc, c0:c1], in0=merged[:, kc, c0:c1], in1=tmpb[:, c0:c1], op=ALU.add), reads=[("tmpb",), ("merged", kc)] + SHR, writes=[("merged", kc)])
                ws_mm(sw, 4, csl, [("csl", kc) for kc in range(KC)] + SHR, segs, KC, ev_oa)
            MK = [("merged", kc) for kc in range(KC)]
            tap("merged", merged[:, :, :], [128, KC, NCOL], MK)

            phase_barrier()
            AR.reset()
            X1 = AR.take(5 * D, F32).rearrange("p (b d) -> p b d", b=5)
            load_gain(1, 1)
            load_gain(0, 2)
            swo = [wnext(ahead=1), wnext(ahead=1)]
            junk2 = AR.take(D, F32); hb2 = AR.take(D, BF16)
            def wout_block(b, junk, hb, so, JK, HK):
                r = brows(b)
                xi = b % 2
                pbh = [ps_take(), ps_take()]
                for hf in range(2):
                    for kc in range(KC):
                        op("pe", lambda E, hf=hf, kc=kc, b=b, r=r, pb=pbh[hf]: E.matmul(psb[pb][0:r, :], lhsT=merged[:, kc, bcol(b):bcol(b) + r], rhs=wb[swo[hf]][:, kc, :], start=(kc == 0), stop=(kc == KC - 1)),
                           reads=[("merged", kc), ("wb", swo[hf])], writes=PSK(pbh[hf]))
                    op("act", lambda E, hf=hf, r=r, pb=pbh[hf]: E.activation(out=junk[0:r, hf * 512:(hf + 1) * 512], in_=psb[pb][0:r, :], func=AF.Square, accum_out=st[0:r, so + 2 + hf:so + 3 + hf]),
                       reads=PSK(pbh[hf]) + SHR, writes=[JK, ("st", so + 2 + hf)])
                op("dve", lambda E, r=r: E.tensor_tensor(out=st[0:r, so + 4:so + 5], in0=st[0:r, so + 2:so + 3], in1=st[0:r, so + 3:so + 4], op=ALU.add), reads=[("st", so + 2), ("st", so + 3)], writes=[("st", so + 4)])
                rstd_from_ssq(st[0:r, so + 4:so + 5], st[0:r, so + 5:so + 6], D, 1e-6, [("st", so + 4)], [("st", so + 5)])
                src = x_s[0:64, :] if b == 4 else x_p[p * TT + b * 128: p * TT + (b + 1) * 128, :]
                op("sp", lambda E, xi=xi, r=r, src=src: E.dma_start(out=xs[xi][0:r, :], in_=src), writes=[("xs", xi)], dma=True)
                for hf in range(2):
                    op("dve", lambda E, hf=hf, r=r, pb=pbh[hf]: E.scalar_tensor_tensor(out=junk[0:r, hf * 512:(hf + 1) * 512], in0=psb[pb][0:r, :], scalar=st[0:r, so + 5:so + 6], in1=gain[1][0:r, hf * 512:(hf + 1) * 512], op0=ALU.mult, op1=ALU.mult),
                       reads=PSK(pbh[hf]) + [("st", so + 5), ("gain", 1)] + SHR, writes=[JK])
                op("pool", lambda E, b=b, r=r, xi=xi: E.tensor_tensor(out=X1[0:r, b, :], in0=xs[xi][0:r, :], in1=junk[0:r, :], op=ALU.add), reads=[("xs", xi), JK] + SHR, writes=[("X1", b)])
                op("act", lambda E, b=b, r=r: E.activation(out=junk[0:r, :], in_=X1[0:r, b, :], func=AF.Square, accum_out=st[0:r, so + 6:so + 7]), reads=[("X1", b)] + SHR, writes=[JK, ("st", so + 6)])
                rstd_from_ssq(st[0:r, so + 6:so + 7], st[0:r, so + 7:so + 8], D, 1e-6, [("st", so + 6)], [("st", so + 7)])
                op("dve", lambda E, b=b, r=r: E.scalar_tensor_tensor(out=hb[0:r, :], in0=X1[0:r, b, :], scalar=st[0:r, so + 7:so + 8], in1=gain[0][0:r, :], op0=ALU.mult, op1=ALU.mult),
                   reads=[("X1", b), ("st", so + 7), ("gain", 0)] + SHR, writes=[HK])
                pb = ps_take()
                pv = psb[pb][:].bitcast(BF16)
                for kc in range(KC):
                    op("pe", lambda E, kc=kc, r=r, pv=pv: E.transpose(out=pv[:, kc * 128: kc * 128 + r], in_=hb[0:r, kc * 128:(kc + 1) * 128], identity=identb[0:r, 0:r]),
                       reads=[HK, ("cstb",)] + SHR, writes=PSK(pb))
                op("act", lambda E, b=b, r=r, pv=pv: E.copy(out=hT[:, :, bcol(b): bcol(b) + r], in_=pv.rearrange("p (k c) -> p k c", k=KC)[:, :, 0:r]),
                   reads=PSK(pb), writes=[("hT", b)])
            for b0 in range(0, nblk, 2):
                gens = [(lambda b=b0: wout_block(b, junk, hb, 0, ("junk",), ("hb",)), [0, 1, 2, 3])]
                if b0 + 1 < nblk:
                    gens.append((lambda b=b0 + 1: wout_block(b, junk2, hb2, 40, ("junk2",), ("hb2",)), [4, 5, 6, 7]))
                interleave(gens)
            H2K = [("hT", b) for b in range(nblk)]

            aT = AR.take(32 * 576, BF16).rearrange("p (f c) -> p f c", f=32)
            fT = AR.take(KC * 576, F32).rearrange("p (k c) -> p k c", k=KC)
            rtmp = AR.take(576, F32)
            load_gain(1, 3)
            for g in range(8):
                sw = wnext()
                def ev_up(oc, pvs, g=g):
                    fc = g * 4 + oc
                    for (pv, pb, c0, c1) in pvs:
                        op("act", lambda E, pv=pv, c0=c0, c1=c1: E.activation(out=rtmp[:, c0:c1], in_=pv, func=AF.Relu), reads=PSK(pb) + SHR, writes=[("rtmp",)])
                        op("dve", lambda E, pv=pv, fc=fc, c0=c0, c1=c1: E.tensor_tensor(out=aT[:, fc, c0:c1], in0=pv, in1=rtmp[:, c0:c1], op=ALU.mult), reads=PSK(pb) + [("rtmp",)] + SHR, writes=[("aT", fc)])
                ws_mm(sw, 4, hT, H2K, segs, KC, ev_up)
            ATK = [("aT", fc) for fc in range(32)]
            if p + 1 < npass:
                phase_n(p + 1)
            for oc in range(KC):
                s_ = wnext()
                wv = wb[s_][:].rearrange("p k c -> p (k c)").rearrange("p (f c) -> p f c", f=32)
                def ev_dn(oc_, pvs, oc=oc):
                    for (pv, pb, c0, c1) in pvs:
                        op("act", lambda E, pv=pv, c0=c0, c1=c1: E.copy(out=fT[:, oc, c0:c1], in_=pv), reads=PSK(pb) + SHR, writes=[("fT", oc)])
                ws_mm(s_, 1, aT, ATK + SHR, segs, 32, ev_dn, lhs=lambda k, oc_, wv=wv: wv[:, k, :])
            FTK = [("fT", kc) for kc in range(KC)]
            def fout_block(b, junk, so, JK):
                r = brows(b)
                xi = b % 2
                pbh = [ps_take(), ps_take()]
                for kc in range(KC):
                    pb = pbh[kc // 4]
                    op("pe", lambda E, kc=kc, b=b, r=r, pb=pb: E.transpose(out=psb[pb][0:r, (kc % 4) * 128:(kc % 4 + 1) * 128], in_=fT[:, kc, bcol(b):bcol(b) + r], identity=identf),
                       reads=[("fT", kc), ("cst",)] + SHR, writes=PSK(pb))
                for hf in range(2):
                    op("act", lambda E, hf=hf, r=r, pb=pbh[hf]: E.activation(out=junk[0:r, hf * 512:(hf + 1) * 512], in_=psb[pb][0:r, :], func=AF.Square, accum_out=st[0:r, so + 8 + hf:so + 9 + hf]),
                       reads=PSK(pbh[hf]) + SHR, writes=[JK, ("st", so + 8 + hf)])
                op("dve", lambda E, r=r: E.tensor_tensor(out=st[0:r, so + 10:so + 11], in0=st[0:r, so + 8:so + 9], in1=st[0:r, so + 9:so + 10], op=ALU.add), reads=[("st", so + 8), ("st", so + 9)], writes=[("st", so + 10)])
                rstd_from_ssq(st[0:r, so + 10:so + 11], st[0:r, so + 11:so + 12], D, 1e-6, [("st", so + 10)], [("st", so + 11)])
                for hf in range(2):
                    op("dve", lambda E, hf=hf, r=r, pb=pbh[hf]: E.scalar_tensor_tensor(out=junk[0:r, hf * 512:(hf + 1) * 512], in0=psb[pb][0:r, :], scalar=st[0:r, so + 11:so + 12], in1=gain[1][0:r, hf * 512:(hf + 1) * 512], op0=ALU.mult, op1=ALU.mult),
                       reads=PSK(pbh[hf]) + [("st", so + 11), ("gain", 1)] + SHR, writes=[JK])
                op("pool", lambda E, b=b, r=r, xi=xi: E.tensor_tensor(out=xs[xi][0:r, :], in0=X1[0:r, b, :], in1=junk[0:r, :], op=ALU.add), reads=[("X1", b), JK] + SHR, writes=[("xs", xi)])
                dst = y_s[0:64, :] if b == 4 else y_p[p * TT + b * 128: p * TT + (b + 1) * 128, :]
                op("sp", lambda E, xi=xi, r=r, dst=dst: E.dma_start(out=dst, in_=xs[xi][0:r, :]), reads=[("xs", xi)], writes=[("o_y", p, b)], dma=True)
            for b0 in range(0, nblk, 2):
                gens = [(lambda b=b0: fout_block(b, junk, 0, ("junk",)), [0, 1, 2, 3])]
                if b0 + 1 < nblk:
                    gens.append((lambda b=b0 + 1: fout_block(b, junk2, 40, ("junk2",)), [4, 5, 6, 7]))
                interleave(gens)

        for p_ in range(npass):
            do_pass(p_)

        outk = [k for k in S.last_w if isinstance(k, tuple) and (str(k[0]).startswith("o_") or k[0] == "tap")]
        op("sp", None, reads=outk)
        S.emit(lambda name: es.enter_context(nc.semaphore(name)))
    nc_sched[0] = S
    return nc, tapd


def _in_maps(inp, cores):
    f = lambda a: np.ascontiguousarray(np.asarray(a, np.float32))
    pcolv = _pack_pcol(inp)
    gains = f(np.stack([inp["pre_mix_g"][0], inp["post_mix_g"][0], inp["pre_ffn_g"][0], inp["post_ffn_g"][0]]))
    shared = dict(gains=gains, pcol=pcolv, cst=_CST, w_in=f(inp["w_in"][0]), w_co=f(inp["w_conv_out"][0]),
                  w_ro=f(inp["w_rwkv_out"][0]), w_out=f(inp["w_out"][0]), w_up=f(inp["w_ff_up"][0]), w_dn=f(inp["w_ff_down"][0]),
                  w_dec=f(inp["w_decay_up"][0]), w_icl=f(inp["w_iclr_up"][0]), w_g=f(inp["w_gate_up"][0]))
    maps = []
    for c in cores:
        sl = slice(16 * c, 16 * c + 16)
        m = dict(shared)
        m.update(x_p=f(inp["x_prompt"][c]), x_s=f(inp["x_sample"][sl]).reshape(NS, D), sconv=f(inp["state_conv"][0][sl]),
                 sshift=f(inp["state_shift"][0][sl]), swkv=f(inp["state_wkv"][0][sl]))
        maps.append(m)
    return maps


_NC_CACHE = {}
nc_sched = [None]


def kernel(**inp):
    if "nc" not in _NC_CACHE:
        _NC_CACHE["nc"] = build_nc()[0]
    nc = _NC_CACHE["nc"]
    res = run_bass_kernel_spmd(nc, _in_maps(inp, range(NCORES)), core_ids=list(range(NCORES))).results
    g = lambda n: [np.asarray(r[n], np.float32) for r in res]
    y_p = np.stack(g("y_p"))
    y_s = np.concatenate(g("y_s")).reshape(128, 4, D)
    conv_p = np.stack(g("conv_p"))[None]
    shift_p = np.concatenate(g("shift_p"))[None]
    wkv_p = np.stack(g("wkv_p"))[None]
    conv_s = np.concatenate(g("conv_s"))[None]
    shift_s = np.concatenate(g("shift_s"))[None]
    wkv_s = np.concatenate(g("wkv_s"))[None]
    return (y_p, y_s, conv_p, shift_p, wkv_p, conv_s, shift_s, wkv_s)
```

```python
import numpy as np
from contextlib import ExitStack
import concourse.bass as bass
import concourse.mybir as mybir
from concourse.bass_utils import run_bass_kernel_spmd
from concourse.alu_op_type import AluOpType as ALU

F32 = mybir.dt.float32
BF16 = mybir.dt.bfloat16
AF = mybir.ActivationFunctionType
AX = mybir.AxisListType

NCORES = 8
D = 1024
SEQ = 2048
TT = 512
NPASS = 4
NS = 64
NSEQ = 16
NCOL = 592
KC = 8
DFF = 4096
NIN = 7424
C0 = float(np.exp(-0.5))


class Sched:
    COMPUTE = ("pe", "act", "dve", "pool")

    def __init__(self, nc, n_dma_sems=10, epoch_cap=4000):
        self.nc = nc
        self.ops = []
        self.last_w = {}
        self.readers = {}
        self.n_dma_sems = n_dma_sems
        self.epoch_cap = epoch_cap
        self.eng = {"pe": nc.tensor, "act": nc.scalar, "dve": nc.vector,
                    "pool": nc.gpsimd, "sp": nc.sync}

    def add(self, eng, fn, reads=(), writes=(), dma=False):
        i = len(self.ops)
        deps = set()
        for k in reads:
            w = self.last_w.get(k)
            if w is not None:
                deps.add(w)
        for k in writes:
            w = self.last_w.get(k)
            if w is not None:
                deps.add(w)
            for r in self.readers.get(k, ()):
                deps.add(r)
        for k in reads:
            self.readers.setdefault(k, []).append(i)
        for k in writes:
            self.last_w[k] = i
            self.readers[k] = []
        deps.discard(i)
        self.ops.append(dict(eng=eng, fn=fn, deps=deps, dma=dma, sig=False))
        return i

    def emit(self, sem_ctx):
        ops = self.ops

        def skip(p, o):
            return (not p["dma"]) and (not o["dma"]) and p["eng"] == "pe" and o["eng"] == "pe"

        for o in ops:
            for d in o["deps"]:
                p = ops[d]
                if p["dma"] or skip(p, o):
                    continue
                p["sig"] = True
        cnt = {e: 0 for e in self.COMPUTE}
        epoch = {e: 0 for e in self.COMPUTE}
        sems = {}

        def get_sem(name):
            if name not in sems:
                sems[name] = sem_ctx(name)
            return sems[name]

        for o in ops:
            if o["dma"] or o["eng"] not in self.COMPUTE:
                continue
            e = o["eng"]
            if o["sig"]:
                if cnt[e] >= self.epoch_cap:
                    epoch[e] += 1
                    cnt[e] = 0
                cnt[e] += 1
                o["semname"] = f"s_{e}_{epoch[e]}"
                o["semval"] = cnt[e]
        dcount, dlast, dlast_idx = {}, {}, {}
        for _i, _o in enumerate(ops):
            _o["_i"] = _i
        for o in ops:
            if not o["dma"]:
                continue
            q = o["eng"]
            n = dcount.get(q, 0)
            dcount[q] = n + 1
            name = f"d_{q}_{n % self.n_dma_sems}"
            prev = dlast.get(name, 0)
            o["semname"], o["semval"], o["prev_val"] = name, prev + 16, prev
            o["prev_idx"] = dlast_idx.get(name, -1)
            dlast[name] = prev + 16
            dlast_idx[name] = ops.index(o) if False else o["_i"]
        eclock = {e: {} for e in self.eng}
        oclock = {}
        for idx, o in enumerate(ops):
            e = o["eng"]
            E = self.eng[e]
            ck = eclock[e]
            need = {}
            for d in o["deps"]:
                p = ops[d]
                if skip(p, o):
                    continue
                cur = need.get(p["semname"])
                if cur is None or p["semval"] > cur[0]:
                    need[p["semname"]] = (p["semval"], d)
            if o["dma"] and o["prev_val"] > 0:
                cur = need.get(o["semname"])
                if cur is None or o["prev_val"] > cur[0]:
                    need[o["semname"]] = (o["prev_val"], o.get("prev_idx", -1))
            for sn, (v, pidx) in sorted(need.items(), key=lambda kv: -kv[1][1]):
                if ck.get(sn, 0) >= v:
                    continue
                E.wait_ge(get_sem(sn), v)
                pc_ = oclock.get(pidx)
                if pc_:
                    for k2, v2 in pc_.items():
                        if ck.get(k2, 0) < v2:
                            ck[k2] = v2
                if ck.get(sn, 0) < v:
                    ck[sn] = v
            if o["fn"] is None:
                continue
            ins = o["fn"](E)
            if o["dma"]:
                ins.then_inc(get_sem(o["semname"]), 16)
                c2 = dict(ck)
                c2[o["semname"]] = o["semval"]
                oclock[idx] = c2
            elif o["sig"]:
                ins.then_inc(get_sem(o["semname"]), 1)
                c2 = dict(ck)
                c2[o["semname"]] = o["semval"]
                oclock[idx] = c2


def _consts():
    c = {}
    p = np.arange(128)
    col = np.arange(64)
    s = (p % 64)[:, None]
    t = col[None, :]
    c["ident"] = np.eye(128, dtype=np.float32)
    c["m_su"] = (t > s).astype(np.float32)
    c["m_u"] = (t >= s).astype(np.float32)
    c["m_sl"] = (t < s).astype(np.float32)
    same = ((s // 4) == (t // 4)).astype(np.float32)
    c["s_su"] = c["m_su"] * same
    c["s_u"] = c["m_u"] * same
    c["s_sl"] = c["m_sl"] * same
    c["eq"] = (t == s).astype(np.float32)
    c["bones"] = ((p[:, None] // 64) == (np.arange(128)[None, :] // 64)).astype(np.float32)
    c["odiv"] = np.full((128, 128), 1.0 / 1024.0, np.float32)
    sm = np.ones((128, 576), np.float32)
    sm[:, 0:512:64] = 0.0
    sm[:, 512:576:4] = 0.0
    c["scanm"] = sm
    qm = np.zeros((128, 16, 64), np.float32)
    for j in range(16):
        qm[:, j, 4 * j:4 * j + 4] = 1.0
    c["qmask"] = qm.reshape(128, 1024)
    rm = np.zeros((128, 16), np.float32)
    for r in range(64):
        rm[r, r // 4] = 1.0
    c["rowm"] = rm
    names = list(c.keys())
    offs, o = {}, 0
    for n in names:
        offs[n] = (o, c[n].shape[1])
        o += c[n].shape[1]
    return np.ascontiguousarray(np.concatenate([c[n] for n in names], axis=1)), offs


_CST, _COFF = _consts()
_PCN = ["conv_b", "conv_ln_g", "conv_ln_b", "decay_base", "iclr_base", "k_k", "k_a", "r_k", "lnx_g", "lnx_b"]
PC_MU = 80
PC_CW = 106
PC_N = 106 + 248


def _fm(v):
    return np.ascontiguousarray(np.asarray(v, np.float32).reshape(8, 128).T)


def _pack_pcol(inp):
    cols = [_fm(inp[n][0].reshape(-1)) for n in _PCN]
    mu = np.asarray(inp["shift_mu"][0], np.float32).reshape(26, 128).T
    cw = np.asarray(inp["conv_w"][0], np.float32)
    cwf = cw.reshape(31, 8, 128).transpose(2, 1, 0).reshape(128, 248)
    return np.ascontiguousarray(np.concatenate(cols + [mu, cwf], axis=1))


def build_nc(cfg=None):
    cfg = cfg or {}
    npass = cfg.get("npass", NPASS)
    do_rwkv = cfg.get("rwkv", True)
    taps = cfg.get("taps", ())
    nc = bass.Bass("TRN2", target_bir_lowering=False)
    dram = lambda n, s, kind="ExternalInput": nc.dram_tensor(n, list(s), F32, kind=kind).ap()
    x_p = dram("x_p", [SEQ, D]); x_s = dram("x_s", [NS, D])
    sconv = dram("sconv", [NSEQ, 30, D]); sshift = dram("sshift", [NSEQ, D]); swkv = dram("swkv", [NSEQ, 16, 64, 64])
    gains = dram("gains", [4, D]); pcol_d = dram("pcol", [128, PC_N]); cst_d = dram("cst", [128, _CST.shape[1]])
    w_in = dram("w_in", [D, NIN]); w_co = dram("w_co", [D, D]); w_ro = dram("w_ro", [D, D]); w_out = dram("w_out", [D, D])
    w_up = dram("w_up", [D, DFF]); w_dn = dram("w_dn", [DFF, D])
    w_dec = dram("w_dec", [64, D]); w_icl = dram("w_icl", [64, D]); w_g = dram("w_g", [128, D])
    OUT = "ExternalOutput"
    y_p = dram("y_p", [SEQ, D], OUT); y_s = dram("y_s", [NS, D], OUT)
    conv_p = dram("conv_p", [30, D], OUT); shift_p = dram("shift_p", [1, D], OUT); wkv_p = dram("wkv_p", [16, 64, 64], OUT)
    conv_s = dram("conv_s", [NSEQ, 30, D], OUT); shift_s = dram("shift_s", [NSEQ, D], OUT); wkv_s = dram("wkv_s", [NSEQ, 16, 64, 64], OUT)
    tapd = {}

    es = ExitStack()
    with es:
        S = Sched(nc, epoch_cap=cfg.get("epoch_cap", 4000))
        sbt = lambda n, s, d: es.enter_context(nc.sbuf_tensor(n, list(s), d))
        cstb = sbt("cstb_sb", [128, _CST.shape[1]], BF16)
        identf_t = sbt("identf_sb", [128, 128], F32)
        pcol = sbt("pcolt", [128, PC_N], F32)
        pneg = sbt("pneg", [128, 34], F32)
        gain = [sbt(f"gain{i}", [128, D], F32) for i in range(2)]
        wsm = sbt("wsm", [128, 3, D], BF16)
        wb = [sbt(f"wb{i}", [128, KC, 512], BF16) for i in range(3)]
        xs = [sbt(f"xs{i}", [128, D], F32) for i in range(2)]
        hb = sbt("hb", [128, D], BF16)
        junk = sbt("junk", [128, D], F32)
        st = sbt("st", [128, 64], F32)
        hT = sbt("hT", [128, KC, NCOL], BF16)
        merged = sbt("merged", [128, KC, NCOL], BF16)
        gbuf = sbt("gbuf", [128, KC, NCOL], BF16)
        uhist = sbt("uhist", [128, KC, 30], BF16)
        plast = sbt("plast", [128, 26], F32)
        diag = [sbt(f"diag{i}", [128, 128], BF16) for i in range(8)]
        Hst = [sbt(f"H32_{g}", [128, 4, 64], F32) for g in range(2)]
        Hbd = [[sbt(f"Hbd_{g}_{q_}", [128, 4, 128], BF16) for q_ in range(2)] for g in range(2)]
        SHN = cfg.get("shn", 56) * 1024
        SH = sbt("SH", [128, SHN], BF16)
        psb = [es.enter_context(nc.psum_tensor(f"ps{i}", [128, 512], F32)) for i in range(8)]

        def cs(name, dt=BF16):
            o, n = _COFF[name]
            return cstb[:, o:o + n]

        identf = identf_t[:]; identb = cs("ident", BF16)
        cst = SH[:, 0:2 * _CST.shape[1]].bitcast(F32)

        class Arena:
            def __init__(self):
                self.off = 0
            def reset(self):
                self.off = 0
            def take(self, nelem, dt):
                n16 = nelem * (2 if dt == F32 else 1)
                o = self.off
                self.off += (n16 + 1) // 2 * 2
                assert self.off <= SHN, ("arena overflow", self.off)
                v = SH[:, o:o + n16]
                return v.bitcast(F32) if dt == F32 else v
        AR = Arena()

        cur_stream = [None]

        def op(eng, fn, reads=(), writes=(), dma=False):
            r = [k for k in reads if k[0] != "ps"]
            w = list(writes) + [k for k in reads if k[0] == "ps"]
            if cur_stream[0] is not None:
                cur_stream[0].append((eng, fn, r, w, dma))
                return None
            return S.add(eng, fn, r, w, dma)

        def interleave(gens):
            streams = []
            for f, banks in gens:
                cur_stream[0] = []
                saved = (pspool[0], pscur[0])
                pspool[0], pscur[0] = banks, 0
                f()
                streams.append(cur_stream[0])
                pspool[0], pscur[0] = saved
                cur_stream[0] = None
            idx = [0] * len(streams)
            while any(idx[i] < len(streams[i]) for i in range(len(streams))):
                for i in range(len(streams)):
                    if idx[i] < len(streams[i]):
                        S.add(*streams[i][idx[i]])
                        idx[i] += 1

        def tap(name, view, shape, reads):
            if name not in taps:
                return
            t = dram("tap_" + name, shape, OUT)
            tapd[name] = t
            op("pool", lambda E: E.dma_start(out=t, in_=view), reads=list(reads) + [("SHE",)], writes=[("tap", name)], dma=True)

        pscur = [0]
        pspool = [list(range(8))]
        def ps_take():
            b = pspool[0][pscur[0] % len(pspool[0])]
            pscur[0] += 1
            return b
        PSK = lambda b: [("ps", b)]

        op("sp", lambda E: E.dma_start(out=cst, in_=cst_d[:, :]), writes=[("cst0",), ("SHE",)], dma=True)
        op("sp", lambda E: E.dma_start(out=pcol[:], in_=pcol_d[:, :]), writes=[("pcol",)], dma=True)
        op("dve", lambda E: E.tensor_copy(out=cstb[:], in_=cst), reads=[("cst0",)], writes=[("cstb",)])
        op("dve", lambda E: E.tensor_copy(out=identf_t[:], in_=cst[:, 0:128]), reads=[("cst0",)], writes=[("cst",)])
        op("dve", lambda E: E.tensor_scalar(out=pneg[:, 0:8], in0=pcol[:, 48:56], scalar1=-1.0, scalar2=1.0, op0=ALU.mult, op1=ALU.add),
           reads=[("pcol",)], writes=[("pneg",)])
        op("dve", lambda E: E.tensor_scalar(out=pneg[:, 8:34], in0=pcol[:, PC_MU:PC_MU + 26], scalar1=-1.0, scalar2=1.0, op0=ALU.mult, op1=ALU.add),
           reads=[("pcol",)], writes=[("pneg",)])
        op("pool", lambda E: E.memset(wsm[:], 0.0), writes=[("wsm",)])
        op("pool", lambda E: E.dma_start(out=wsm[0:64, 0, :], in_=w_dec[:, :]), reads=[("wsm",)], writes=[("wsm", 0)], dma=True)
        op("pool", lambda E: E.dma_start(out=wsm[64:128, 1, :], in_=w_icl[:, :]), reads=[("wsm",)], writes=[("wsm", 1)], dma=True)
        op("pool", lambda E: E.dma_start(out=wsm[:, 2, :], in_=w_g[:, :]), reads=[("wsm",)], writes=[("wsm", 2)], dma=True)
        op("pool", lambda E: E.memset(uhist[:], 0.0), writes=[("uhist",)])
        op("pool", lambda E: E.memset(plast[:], 0.0), writes=[("plast",)])
        for g in range(2):
            op("pool", lambda E, g=g: E.memset(Hst[g][:], 0.0), writes=[("H32", g, 0), ("H32", g, 1)])
            for q_ in range(2):
                op("pool", lambda E, g=g, q_=q_: E.memset(Hbd[g][q_][:], 0.0), writes=[("Hbd", g, q_, 0), ("Hbd", g, q_, 1)])
        NZ = 32
        for zi in range(NZ):
            z0, z1 = zi * SHN // NZ, (zi + 1) * SHN // NZ
            op("pool", lambda E, z0=z0, z1=z1: E.memset(SH[:, z0:z1], 0.0), reads=[("SHE",)] if zi == 0 else [], writes=[("SHE",), ("cst0",)])

        pc = lambda pi, kc: pcol[:, pi * 8 + kc: pi * 8 + kc + 1]
        PI = {n: i for i, n in enumerate(_PCN)}

        def load_gain(slot, gi):
            op("sp", lambda E: E.dma_start(out=gain[slot][:], in_=gains[gi:gi + 1, :].partition_broadcast(128)),
               writes=[("gain", slot)], dma=True)

        wcur = [0]
        def wload(dview, ncols=512):
            s = wcur[0] % 3
            wcur[0] += 1
            op("pool", lambda E: E.dma_start(out=wb[s][:, :, 0:ncols], in_=dview), writes=[("wb", s)], dma=True)
            return s

        def wload_v(dview):
            s_ = wcur[0] % 3
            wcur[0] += 1
            if dview.shape[1] == 32:
                outv = wb[s_][:].rearrange("p k c -> p (k c)").rearrange("p (f c) -> p f c", f=32)
            else:
                outv = wb[s_][:, :, 0:dview.shape[2]]
            op("pool", lambda E: E.dma_start(out=outv, in_=dview), writes=[("wb", s_)], dma=True)
            return s_

        def all_weight_views():
            v = []
            for p_ in range(npass):
                if do_rwkv:
                    v.append(wview(w_in, 2048 + 3072, 256))
                    for hg in range(2):
                        for kind in range(3):
                            v.append(wview(w_in, 2048 + kind * 1024 + hg * 512, 512))
                        if hg == 0:
                            v += [wview(w_in, 6400, 512), wview(w_in, 6400 + 512, 512)]
                    v += [wview(w_ro, 0, 512), wview(w_ro, 512, 512)]
                v += [wview(w_in, 1024, 512), wview(w_in, 0, 512), wview(w_in, 1024 + 512, 512), wview(w_in, 512, 512)]
                v += [wview(w_in, 5376, 512), wview(w_in, 5376 + 512, 512), wview(w_co, 0, 512), wview(w_co, 512, 512)]
                v += [wview(w_out, 0, 512), wview(w_out, 512, 512)]
                v += [wview(w_up, g * 512, 512) for g in range(8)]
                v += [w_dn.rearrange("(f p) n -> p f n", p=128)[:, :, oc * 128:(oc + 1) * 128] for oc in range(KC)]
            return v

        wq = dict(views=None, issued=0, taken=0, slots={})
        def wnext(ahead=2):
            if wq["views"] is None:
                wq["views"] = all_weight_views()
            vs = wq["views"]
            while wq["issued"] < min(len(vs), wq["taken"] + 1 + ahead):
                i = wq["issued"]
                wq["slots"][i] = wload_v(vs[i])
                wq["issued"] += 1
            sl = wq["slots"].pop(wq["taken"])
            wq["taken"] += 1
            return sl

        def wstream(views, ahead=2):
            n = len(views)
            slots = {}
            for i in range(min(ahead, n)):
                slots[i] = wload(views[i])
            for i in range(n):
                if i + ahead < n:
                    slots[i + ahead] = wload(views[i + ahead])
                yield i, slots[i]

        def wview(W, c0, n):
            return W.rearrange("(kc p) n -> p kc n", p=128)[:, :, c0:c0 + n]

        def phase_barrier():
            op("pool", lambda E: E.memset(st[:, 60:64], 0.0), reads=[], writes=[("SHE",)])
        SHR = [("SHE",)]

        def rstd_from_ssq(ssq_ap, out_ap, n, eps, rk, wk, rows=128):
            op("act", lambda E: E.activation(out=out_ap, in_=ssq_ap, func=AF.Sqrt, bias=eps, scale=1.0 / n), reads=rk, writes=wk)
            op("dve", lambda E: E.reciprocal(out=out_ap, in_=out_ap), reads=wk, writes=wk)


        GN_EPS = 64e-5

        def rwkv_phase(p, P0, LAST, nblk, segs, segs_r, HTK, ws_mm):
            phase_barrier()
            AR.reset()
            n = 576 if P0 else 512
            nch = 9 if P0 else 8
            t3 = lambda k, c, dt: AR.take(k * c, dt).rearrange("p (k c) -> p k c", k=k)
            loraA = AR.take(592, BF16); sgl = AR.take(592, BF16)
            zb = t3(KC, 576, BF16)
            QR = AR.take(4 * 9 * 128, BF16).rearrange("p (k c q) -> p k c q", k=4, c=9)
            RKV = [t3(4, 592, BF16) for _ in range(3)]
            bon = t3(4, 576, BF16); gT = t3(4, 576, BF16)
            WC = t3(4, 9, F32); WCs = t3(4, 16, F32)
            w_full = [AR.take(1184, F32) for _ in range(6)]
            w = w_full
            sqb2 = [AR.take(576, BF16) for _ in range(2)]; rkb2 = [AR.take(576, BF16) for _ in range(2)]
            Ktok = AR.take(512, BF16); Btok = AR.take(512, BF16)
            Vc = [AR.take(512, BF16) for _ in range(2)]
            h8 = lambda t: t.rearrange("p (h v) -> p h v", h=8)
            LkT = [AR.take(512, BF16) for _ in range(2)]; MkT = [AR.take(512, BF16) for _ in range(2)]; MbT = [AR.take(512, BF16) for _ in range(2)]
            _nn = [AR.take(512, BF16) for _ in range(2)]; NN = [_nn, _nn]
            _aa = [AR.take(512, BF16) for _ in range(2)]; AA = [_aa, _aa]
            _z0 = AR.take(512, BF16); ZZ = [[_z0, AR.take(512, BF16)], [_z0, AR.take(512, BF16)]]
            Xs = [AR.take(512, BF16) for _ in range(2)]; Us = [AR.take(512, BF16) for _ in range(2)]
            ysq = AR.take(512, F32); yc = ysq; yh = AR.take(512, BF16)
            zt = AR.take(512, F32).rearrange("p (k c) -> p k c", k=4)
            Khs = t3(4, 64, BF16); Bhs = t3(4, 64, BF16)
            Khtok = AR.take(512, BF16); Bhtok = AR.take(512, BF16)
            H0bd = AR.take(2048, BF16).rearrange("p (j c) -> p j c", j=16)
            KtT = gbuf[:, 0:4, :]; BtT = gbuf[:, 4:8, :]
            a1, mixo = w[4], w[5]
            mu = lambda rc: pcol[:, PC_MU + rc: PC_MU + rc + 1]
            omu = lambda rc: pneg[:, 8 + rc: 9 + rc]
            WK = lambda i: [("w", i)]
            bc_mid = lambda ap, nmid: ap.unsqueeze(1).to_broadcast([ap.shape[0], nmid, ap.shape[1]])
            bc_last = lambda ap, nl: ap.unsqueeze(2).to_broadcast([ap.shape[0], ap.shape[1], nl])
            c64 = lambda ap: ap.rearrange("p (c q) -> p c q", q=64)
            s4 = lambda ap: ap.rearrange("p (s t) -> p s t", t=4)
            pairv = lambda t, kl: t[:, kl * 128:(kl + 1) * 128]
            bones = cs("bones"); eqm = cs("eq")
            GK = [("gbuf", k) for k in range(8)]
            QK = [("QR", k) for k in range(4)]
            for i, tz in enumerate(LkT + MkT + MbT + [ZZ[0][1], ZZ[1][1]] + Xs + Us + Vc + [Khtok, Bhtok]):
                op("pool", lambda E, tz=tz: E.memset(tz, 0.0), reads=SHR, writes=[("zp", i)])
            if P0:
                op("pool", lambda E: E.memset(H0bd, 0.0), reads=SHR, writes=[("H0bd",)])
            ZP = [("zp", i) for i in range(18)]

            def mix(rc, pvs, dst, dkey):
                for (pv, pb, c0, c1) in pvs:
                    if c0 == 0:
                        op("act", lambda E, pv=pv: E.activation(out=a1[:, 0:512], in_=pv, func=AF.Identity, scale=omu(rc)), reads=PSK(pb) + [("pneg",)] + SHR, writes=WK(4))
                        op("dve", lambda E, pv=pv: E.scalar_tensor_tensor(out=dst[:, 1:512], in0=pv[:, 0:511], scalar=mu(rc), in1=a1[:, 1:512], op0=ALU.mult, op1=ALU.add),
                           reads=PSK(pb) + WK(4) + [("pcol",)] + SHR, writes=[dkey])
                        op("dve", lambda E: E.scalar_tensor_tensor(out=dst[:, 0:1], in0=plast[:, rc:rc + 1], scalar=mu(rc), in1=a1[:, 0:1], op0=ALU.mult, op1=ALU.add),
                           reads=WK(4) + [("plast", rc), ("plast",), ("pcol",)] + SHR, writes=[dkey])
                        op("act", lambda E, pv=pv: E.copy(out=plast[:, rc:rc + 1], in_=pv[:, 511:512]), reads=PSK(pb) + [("plast",)], writes=[("plast", rc)])
                    else:
                        op("act", lambda E, pv=pv: E.activation(out=a1[:, 512:576], in_=pv[:, 0:64], func=AF.Identity, scale=omu(rc)), reads=PSK(pb) + [("pneg",)] + SHR, writes=WK(4))
                        d3, a3, p3 = s4(dst[:, 512:576]), s4(a1[:, 512:576]), s4(pv[:, 0:64])
                        op("dve", lambda E, d3=d3, a3=a3, p3=p3: E.scalar_tensor_tensor(out=d3[:, :, 1:4], in0=p3[:, :, 0:3], scalar=mu(rc), in1=a3[:, :, 1:4], op0=ALU.mult, op1=ALU.add),
                           reads=PSK(pb) + WK(4) + [("pcol",)] + SHR, writes=[dkey])
                        op("dve", lambda E, d3=d3, a3=a3, pv=pv: E.scalar_tensor_tensor(out=d3[:, :, 0], in0=pv[:, 64:80], scalar=mu(rc), in1=a3[:, :, 0], op0=ALU.mult, op1=ALU.add),
                           reads=PSK(pb) + WK(4) + [("pcol",)] + SHR, writes=[dkey])

            sw = wnext()
            def ev_lora(oc, pvs):
                mix(24 + oc, pvs, mixo, ("w", 5))
                if oc == 0:
                    op("act", lambda E: E.activation(out=loraA[0:64, 0:n], in_=mixo[0:64, 0:n], func=AF.Tanh), reads=WK(5) + SHR, writes=[("loraA", 0)])
                    op("act", lambda E: E.copy(out=loraA[64:128, 0:n], in_=mixo[64:128, 0:n]), reads=WK(5) + SHR, writes=[("loraA", 1)])
                else:
                    op("act", lambda E: E.activation(out=sgl[:, 0:n], in_=mixo[:, 0:n], func=AF.Sigmoid), reads=WK(5) + SHR, writes=[("sgl",)])
            ws_mm(sw, 2, hT, HTK, segs_r, KC, ev_lora)

            def elem(hg, kl, sid=0):
                kc = hg * 4 + kl
                w = [t[:, sid * 592:(sid + 1) * 592] for t in w_full]
                WK = lambda i: [("w", i) if sid == 0 else ("wB", i)]
                sqb, rkb = sqb2[sid], rkb2[sid]
                SQK, RKK = [("sqb", sid)], [("rkb", sid)]
                Rr, Kk, Vv = RKV[0][:, kl, :], RKV[1][:, kl, :], RKV[2][:, kl, :]
                RK = lambda i: [("rkv", i, kl)]
                pz = [None, None, None]
                for qi, (lh, rh, rkey) in enumerate([(wsm[:, 0, kc * 128:(kc + 1) * 128], loraA, [("loraA", 0), ("loraA", 1), ("wsm", 0), ("wsm",)]),
                                                     (wsm[:, 1, kc * 128:(kc + 1) * 128], loraA, [("loraA", 0), ("loraA", 1), ("wsm", 1), ("wsm",)]),
                                                     (wsm[:, 2, kc * 128:(kc + 1) * 128], sgl, [("sgl",), ("wsm", 2), ("wsm",)])]):
                    pz[qi] = [ps_take() for _ in segs]
                    for si, (c0, c1) in enumerate(segs):
                        op("pe", lambda E, lh=lh, rh=rh, c0=c0, c1=c1, pb=pz[qi][si]: E.matmul(psb[pb][:, 0:c1 - c0], lhsT=lh, rhs=rh[:, c0:c1], start=True, stop=True),
                           reads=rkey + SHR, writes=PSK(pz[qi][si]))
                    for si, (c0, c1) in enumerate(segs):
                        if qi == 0:
                            op("act", lambda E, c0=c0, c1=c1, pb=pz[0][si]: E.activation(out=w[0][:, c0:c1], in_=psb[pb][:, 0:c1 - c0], func=AF.Sigmoid, bias=pc(PI["decay_base"], kc), scale=1.0),
                               reads=PSK(pz[0][si]) + [("pcol",)] + SHR, writes=WK(0))
                        elif qi == 1:
                            op("act", lambda E, c0=c0, c1=c1, pb=pz[1][si]: E.activation(out=w[1][:, c0:c1], in_=psb[pb][:, 0:c1 - c0], func=AF.Sigmoid, bias=pc(PI["iclr_base"], kc), scale=1.0),
                               reads=PSK(pz[1][si]) + [("pcol",)] + SHR, writes=WK(1))
                        else:
                            op("act", lambda E, c0=c0, c1=c1, pb=pz[2][si]: E.copy(out=gT[:, kl, c0:c1], in_=psb[pb][:, 0:c1 - c0]), reads=PSK(pz[2][si]) + SHR, writes=[("gT", kl)])
                op("dve", lambda E: E.tensor_tensor_scan(out=w[2][:, 0:n], data0=cs("scanm")[:, 0:n], data1=w[0][:, 0:n], initial=0.0, op0=ALU.mult, op1=ALU.add),
                   reads=WK(0) + [("cstb",)] + SHR, writes=WK(2))
                op("pool", lambda E: E.tensor_tensor(out=w[3][:, 0:n], in0=w[2][:, 0:n], in1=w[0][:, 0:n], op=ALU.subtract), reads=WK(2) + WK(0) + SHR, writes=WK(3))
                op("act", lambda E: E.activation(out=w[3][:, 0:n], in_=w[3][:, 0:n], func=AF.Exp, scale=-C0), reads=WK(3) + SHR, writes=WK(3))
                op("act", lambda E: E.activation(out=w[4][:, 0:n], in_=w[2][:, 0:n], func=AF.Exp, scale=C0), reads=WK(2) + SHR, writes=WK(4))
                op("act", lambda E: E.activation(out=w[2][:, 0:n], in_=w[2][:, 0:n], func=AF.Exp, scale=-C0), reads=WK(2) + WK(4) + SHR, writes=WK(2))
                op("pool", lambda E: E.tensor_copy(out=WC[:, kl, 0:8], in_=w[2][:, 63:512:64]), reads=WK(2) + SHR, writes=[("WC", kl)])
                if P0:
                    op("pool", lambda E: E.tensor_copy(out=WCs[:, kl, :], in_=w[2][:, 515:576:4]), reads=WK(2) + SHR, writes=[("WCs", kl)])
                op("dve", lambda E: E.tensor_scalar(out=w[5][:, 0:n], in0=Kk[:, 0:n], scalar1=pc(PI["k_k"], kc), scalar2=None, op0=ALU.mult), reads=RK(1) + [("pcol",)] + SHR, writes=WK(5))
                op("act", lambda E: E.activation(out=sqb[:, 0:n], in_=w[5][:, 0:n], func=AF.Square), reads=WK(5) + SHR, writes=SQK)
                pss = [ps_take() for _ in segs]
                for si, (c0, c1) in enumerate(segs):
                    op("pe", lambda E, c0=c0, c1=c1, pb=pss[si]: E.matmul(psb[pb][:, 0:c1 - c0], lhsT=bones, rhs=sqb[:, c0:c1], start=True, stop=True), reads=SQK + [("cstb",)] + SHR, writes=PSK(pss[si]))
                    op("act", lambda E, c0=c0, c1=c1, pb=pss[si]: E.activation(out=w[0][:, c0:c1], in_=psb[pb][:, 0:c1 - c0], func=AF.Sqrt, bias=1e-24, scale=1.0), reads=PSK(pss[si]) + WK(3) + SHR, writes=WK(0))
                op("dve", lambda E: E.reciprocal(out=w[0][:, 0:n], in_=w[0][:, 0:n]), reads=WK(0) + SHR, writes=WK(0))
                op("dve", lambda E: E.tensor_tensor(out=w[5][:, 0:n], in0=w[5][:, 0:n], in1=w[0][:, 0:n], op=ALU.mult), reads=WK(5) + WK(0) + SHR, writes=WK(5))
                op("dve", lambda E: E.tensor_scalar(out=w[0][:, 0:n], in0=w[1][:, 0:n], scalar1=pc(PI["k_a"], kc), scalar2=pneg[:, kc:kc + 1], op0=ALU.mult, op1=ALU.add),
                   reads=WK(1) + WK(0) + [("pcol",), ("pneg",)] + SHR, writes=WK(0))
                op("dve", lambda E: E.tensor_tensor(out=w[0][:, 0:n], in0=Kk[:, 0:n], in1=w[0][:, 0:n], op=ALU.mult), reads=RK(1) + WK(0) + SHR, writes=WK(0))
                op("dve", lambda E: E.tensor_tensor(out=QR[:, kl, 0:nch, 0:64], in0=c64(w[5][:, 0:n]), in1=c64(w[3][:, 0:n]), op=ALU.mult), reads=WK(5) + WK(3) + SHR, writes=[("QR", kl)])
                op("dve", lambda E: E.tensor_tensor(out=w[5][:, 0:n], in0=w[5][:, 0:n], in1=w[1][:, 0:n], op=ALU.mult), reads=WK(5) + WK(1) + [("QR", kl)] + SHR, writes=WK(5))
                op("dve", lambda E: E.scalar_tensor_tensor(out=BtT[:, kl, 0:n], in0=w[5][:, 0:n], scalar=-1.0, in1=w[4][:, 0:n], op0=ALU.mult, op1=ALU.mult), reads=WK(5) + WK(4) + SHR, writes=[("gbuf", 4 + kl)])
                op("pool", lambda E: E.tensor_tensor(out=KtT[:, kl, 0:n], in0=w[0][:, 0:n], in1=w[4][:, 0:n], op=ALU.mult), reads=WK(0) + WK(4) + SHR, writes=[("gbuf", kl)])
                op("pool", lambda E: E.tensor_tensor(out=QR[:, kl, 0:nch, 64:128], in0=c64(Rr[:, 0:n]), in1=c64(w[2][:, 0:n]), op=ALU.mult), reads=RK(0) + WK(2) + SHR, writes=[("QR", kl)])
                op("dve", lambda E: E.scalar_tensor_tensor(out=rkb[:, 0:n], in0=Rr[:, 0:n], scalar=pc(PI["r_k"], kc), in1=w[0][:, 0:n], op0=ALU.mult, op1=ALU.mult), reads=RK(0) + WK(0) + [("pcol",)] + SHR, writes=RKK)
                psq = [ps_take() for _ in segs]
                for si, (c0, c1) in enumerate(segs):
                    op("pe", lambda E, c0=c0, c1=c1, pb=psq[si]: E.matmul(psb[pb][:, 0:c1 - c0], lhsT=bones, rhs=rkb[:, c0:c1], start=True, stop=True), reads=RKK + [("cstb",)] + SHR, writes=PSK(psq[si]))
                    op("dve", lambda E, c0=c0, c1=c1, pb=psq[si]: E.tensor_tensor(out=bon[:, kl, c0:c1], in0=psb[pb][:, 0:c1 - c0], in1=Vv[:, c0:c1], op=ALU.mult), reads=PSK(psq[si]) + RK(2) + SHR, writes=[("bon", kl)])
                if P0:
                    op("dve", lambda E: E.tensor_tensor(out=s4(Khs[:, kl, :]), in0=s4(KtT[:, kl, 512:576]), in1=bc_last(WCs[:, kl, :], 4), op=ALU.mult), reads=[("gbuf", kl), ("WCs", kl)] + SHR, writes=[("Khs", kl)])
                    op("dve", lambda E: E.tensor_tensor(out=s4(Bhs[:, kl, :]), in0=s4(BtT[:, kl, 512:576]), in1=bc_last(WCs[:, kl, :], 4), op=ALU.mult), reads=[("gbuf", 4 + kl), ("WCs", kl)] + SHR, writes=[("Bhs", kl)])

            def ypost(pyc, rs, h0, nh):
                hsl = slice(h0 * 64, (h0 + nh) * 64)
                y3 = psb[pyc][rs, hsl].rearrange("p (h v) -> p h v", h=nh)
                v3 = lambda t: t[rs, hsl].rearrange("p (h v) -> p h v", h=nh)
                sa, sb_, sc_ = st[rs, 16 + h0:16 + h0 + nh], st[rs, 24 + h0:24 + h0 + nh], st[rs, 32 + h0:32 + h0 + nh]
                if cfg.get('yp_stop', 99) < 1:
                    return
                op("act", lambda E: E.activation(out=ysq[rs, hsl], in_=psb[pyc][rs, hsl], func=AF.Square), reads=PSK(pyc) + SHR, writes=[("ysq",)])
                if cfg.get('yp_stop', 99) < 2:
                    return
                op("dve", lambda E: E.tensor_reduce(out=sa, in_=y3, axis=AX.X, op=ALU.add), reads=PSK(pyc), writes=[("st", 16)])
                if cfg.get('yp_stop', 99) < 3:
                    return
                op("dve", lambda E: E.tensor_reduce(out=sb_, in_=v3(ysq), axis=AX.X, op=ALU.add), reads=[("ysq",)] + SHR, writes=[("st", 24)])
                if cfg.get('yp_stop', 99) < 4:
                    return
                op("dve", lambda E: E.tensor_scalar(out=sa, in0=sa, scalar1=1.0 / 64, scalar2=None, op0=ALU.mult), reads=[("st", 16)], writes=[("st", 16)])
                if cfg.get('yp_stop', 99) < 5:
                    return
                op("dve", lambda E: E.tensor_tensor(out=sc_, in0=sa, in1=sa, op=ALU.mult), reads=[("st", 16)], writes=[("st", 32)])
                if cfg.get('yp_stop', 99) < 6:
                    return
                op("dve", lambda E: E.scalar_tensor_tensor(out=sb_, in0=sb_, scalar=1.0 / 64, in1=sc_, op0=ALU.mult, op1=ALU.subtract), reads=[("st", 24), ("st", 32)], writes=[("st", 24)])
                if cfg.get('yp_stop', 99) < 7:
                    return
                op("act", lambda E: E.activation(out=sb_, in_=sb_, func=AF.Sqrt, bias=GN_EPS, scale=1.0), reads=[("st", 24)], writes=[("st", 24)])
                if cfg.get('yp_stop', 99) < 8:
                    return
                op("dve", lambda E: E.reciprocal(out=sb_, in_=sb_), reads=[("st", 24)], writes=[("st", 24)])
                if cfg.get('yp_stop', 99) < 9:
                    return
                op("dve", lambda E: E.tensor_tensor(out=v3(yc), in0=y3, in1=bc_last(sa, 64), op=ALU.subtract), reads=PSK(pyc) + [("st", 16)] + SHR, writes=[("ysq",)])
                if cfg.get('yp_stop', 99) < 10:
                    return
                op("dve", lambda E: E.tensor_tensor(out=v3(yh), in0=v3(yc), in1=bc_last(sb_, 64), op=ALU.mult), reads=[("ysq",), ("st", 24)] + SHR, writes=[("yh",), ("yh", 0), ("yh", 1)])

            def scores_inverse(hg, b, c, SAMP):
                rs = slice(c * 64, c * 64 + 64)
                ci = 8 if SAMP else 2 * b + c
                cc = b * 128 + c * 64
                msu, mu_, msl = (cs("s_su"), cs("s_u"), cs("s_sl")) if SAMP else (cs("m_su"), cs("m_u"), cs("m_sl"))
                pA = [None, None]; pBq = [None, None]; pC = [None, None]
                for hp in range(2):
                    hs = slice(hp * 64, hp * 64 + 64)
                    pA[hp], pBq[hp], pC[hp] = ps_take(), ps_take(), ps_take()
                    for kl in range(4):
                        op("pe", lambda E, kl=kl, hp=hp, hs=hs: E.matmul(psb[pA[hp]][rs, kl * 128:(kl + 1) * 128], lhsT=KtT[hs, kl, cc:cc + 64], rhs=QR[hs, kl, ci, :], start=True, stop=True), reads=GK + QK + SHR, writes=PSK(pA[hp]))
                        op("pe", lambda E, kl=kl, hp=hp, hs=hs: E.matmul(psb[pBq[hp]][rs, kl * 128:(kl + 1) * 128], lhsT=BtT[hs, kl, cc:cc + 64], rhs=QR[hs, kl, ci, :], start=True, stop=True), reads=GK + QK + SHR, writes=PSK(pBq[hp]))
                        op("pe", lambda E, kl=kl, hp=hp, hs=hs: E.matmul(psb[pC[hp]][rs, kl * 64:(kl + 1) * 64], lhsT=QR[hs, kl, ci, 0:64], rhs=BtT[hs, kl, cc:cc + 64], start=True, stop=True), reads=GK + QK + SHR, writes=PSK(pC[hp]))
                    h4 = lambda t, hp=hp: t[rs, :].rearrange("p (k h v) -> p k h v", k=4, h=2)[:, :, hp, :]
                    stA, stB, stC = Xs[c], Us[c], yh
                    op("act", lambda E, hp=hp: E.copy(out=stA[rs, :], in_=psb[pA[hp]][rs, :]), reads=PSK(pA[hp]) + ZP + SHR, writes=[("Xs", c)])
                    op("act", lambda E, hp=hp: E.copy(out=stB[rs, :], in_=psb[pBq[hp]][rs, :]), reads=PSK(pBq[hp]) + ZP + SHR, writes=[("Us", c)])
                    op("act", lambda E, hp=hp: E.copy(out=stC[rs, 0:256], in_=psb[pC[hp]][rs, 0:256]), reads=PSK(pC[hp]) + SHR, writes=[("yh", c)])
                    pa3 = stA[rs, :].rearrange("p (k q) -> p k q", k=4)
                    pb3 = stB[rs, :].rearrange("p (k q) -> p k q", k=4)
                    pc3 = stC[rs, 0:256].rearrange("p (k q) -> p k q", k=4)
                    op("pool", lambda E, h4=h4, pa3=pa3: E.tensor_tensor(out=h4(LkT[c]), in0=pa3[:, :, 0:64], in1=bc_mid(msu[rs, :], 4), op=ALU.mult), reads=[("Xs", c), ("cstb",)] + ZP + SHR, writes=[("LkT", c)])
                    op("pool", lambda E, h4=h4, pa3=pa3: E.tensor_tensor(out=h4(MkT[c]), in0=pa3[:, :, 64:128], in1=bc_mid(mu_[rs, :], 4), op=ALU.mult), reads=[("Xs", c), ("cstb",)] + ZP + SHR, writes=[("MkT", c)])
                    op("dve", lambda E, h4=h4, pb3=pb3: E.tensor_tensor(out=h4(NN[c][0]), in0=pb3[:, :, 0:64], in1=bc_mid(msu[rs, :], 4), op=ALU.mult), reads=[("Us", c), ("cstb",)] + SHR, writes=[("NN", c, 0)])
                    op("pool", lambda E, h4=h4, pb3=pb3: E.tensor_tensor(out=h4(MbT[c]), in0=pb3[:, :, 64:128], in1=bc_mid(mu_[rs, :], 4), op=ALU.mult), reads=[("Us", c), ("cstb",)] + ZP + SHR, writes=[("MbT", c)])
                    op("dve", lambda E, h4=h4, pc3=pc3: E.tensor_tensor(out=h4(AA[c][0]), in0=pc3, in1=bc_mid(msl[rs, :], 4), op=ALU.mult), reads=[("yh", c), ("cstb",)] + SHR, writes=[("AA", c, 0)])
                op("pool", lambda E: E.tensor_tensor(out=h8(ZZ[c][0][rs, :]), in0=h8(NN[c][0][rs, :]), in1=bc_mid(eqm[rs, :], 8), op=ALU.add), reads=[("NN", c, 0), ("cstb",)] + SHR, writes=[("ZZ", c, 0)])
                nlev = 1 if SAMP else 5
                for st_ in range(nlev + 1):
                    s_, d_ = st_ % 2, (st_ + 1) % 2
                    do_sq = st_ < nlev
                    lastsq = (st_ == nlev - 1)
                    do_prod = st_ >= 1
                    if do_sq:
                        pa_ = ps_take()
                        for hi in range(8):
                            op("pe", lambda E, hi=hi, s_=s_, pa_=pa_: E.matmul(psb[pa_][rs, hi * 64:(hi + 1) * 64], lhsT=h8(NN[c][s_])[rs, hi, :], rhs=h8(AA[c][s_])[rs, hi, :], start=True, stop=True), reads=[("NN", c, s_), ("AA", c, s_)] + SHR, writes=PSK(pa_))
                        if not lastsq:
                            pn_ = ps_take()
                            for hi in range(8):
                                op("pe", lambda E, hi=hi, s_=s_, pn_=pn_: E.matmul(psb[pn_][rs, hi * 64:(hi + 1) * 64], lhsT=h8(AA[c][s_])[rs, hi, :], rhs=h8(NN[c][s_])[rs, hi, :], start=True, stop=True), reads=[("NN", c, s_), ("AA", c, s_)] + SHR, writes=PSK(pn_))
                    if do_prod:
                        zs, zd = (st_ - 1) % 2, st_ % 2
                        pz_ = ps_take()
                        for hi in range(8):
                            op("pe", lambda E, hi=hi, s_=s_, zs=zs, pz_=pz_: E.matmul(psb[pz_][rs, hi * 64:(hi + 1) * 64], lhsT=h8(AA[c][s_])[rs, hi, :], rhs=h8(ZZ[c][zs])[rs, hi, :], start=True, stop=True), reads=[("AA", c, s_), ("ZZ", c, zs)] + SHR, writes=PSK(pz_))
                    if do_sq:
                        op("act", lambda E, d_=d_, pa_=pa_: E.copy(out=AA[c][d_][rs, :], in_=psb[pa_][rs, :]), reads=PSK(pa_) + SHR, writes=[("AA", c, d_)])
                        if not lastsq:
                            op("act", lambda E, d_=d_, pn_=pn_: E.copy(out=NN[c][d_][rs, :], in_=psb[pn_][rs, :]), reads=PSK(pn_) + SHR, writes=[("NN", c, d_)])
                    if do_prod:
                        op("dve", lambda E, zs=zs, zd=zd, pz_=pz_: E.tensor_tensor(out=ZZ[c][zd][rs, :], in0=psb[pz_][rs, :], in1=ZZ[c][zs][rs, :], op=ALU.add), reads=PSK(pz_) + [("ZZ", c, zs)] + ZP + SHR, writes=[("ZZ", c, zd)])

            def chunk(hg, b, c):
                rs = slice(c * 64, c * 64 + 64)
                ci = 2 * b + c
                TT_ = ZZ[c][1]
                px, py, pu, ph = ps_take(), ps_take(), ps_take(), ps_take()
                pair = lambda pb, kl: pairv(psb[pb][rs, :], kl)
                for kl in range(4):
                    op("pe", lambda E, kl=kl: E.matmul(pair(px, kl), lhsT=QR[:, kl, ci, 0:64], rhs=Hbd[hg][c][:, kl, :], start=True, stop=False), reads=QK + [("Hbd", hg, c, 0), ("Hbd", hg, c, 1)] + SHR, writes=PSK(px))
                    for hi in (2 * kl, 2 * kl + 1):
                        op("pe", lambda E, hi=hi: E.matmul(psb[px][rs, hi * 64:(hi + 1) * 64], lhsT=h8(LkT[c])[:, hi, :], rhs=Vc[c][:, hi * 64:(hi + 1) * 64], start=False, stop=(hi % 2 == 1)), reads=[("LkT", c), ("Vc", c)] + ZP + SHR, writes=PSK(px))
                op("act", lambda E: E.copy(out=Xs[c][rs, :], in_=psb[px][rs, :]), reads=PSK(px) + ZP + SHR, writes=[("Xs", c)])
                if cfg.get("chain_stop", 9) <= 1:
                    return
                for hi in range(8):
                    op("pe", lambda E, hi=hi: E.matmul(psb[pu][rs, hi * 64:(hi + 1) * 64], lhsT=h8(TT_)[:, hi, :], rhs=Xs[c][:, hi * 64:(hi + 1) * 64], start=True, stop=True), reads=[("ZZ", c, 1), ("Xs", c)] + ZP + SHR, writes=PSK(pu))
                op("act", lambda E: E.copy(out=Us[c][rs, :], in_=psb[pu][rs, :]), reads=PSK(pu) + ZP + SHR, writes=[("Us", c)])
                if cfg.get("chain_stop", 9) <= 2:
                    return
                for kl in range(4):
                    op("pe", lambda E, kl=kl: E.matmul(psb[ph][:, kl * 128:(kl + 1) * 128], lhsT=Ktok[:, kl * 128:(kl + 1) * 128], rhs=pairv(Vc[c], kl), start=True, stop=False), reads=[("Ktok",), ("Vc", c)] + ZP + SHR, writes=PSK(ph))
                    op("pe", lambda E, kl=kl: E.matmul(psb[ph][:, kl * 128:(kl + 1) * 128], lhsT=Btok[:, kl * 128:(kl + 1) * 128], rhs=pairv(Us[c], kl), start=False, stop=True), reads=[("Btok",), ("Us", c)] + ZP + SHR, writes=PSK(ph))
                for hp in range(2):
                    hs = slice(hp * 64, hp * 64 + 64)
                    phv = psb[ph][hs, :].rearrange("p (k q) -> p k q", k=4)[:, :, hp * 64:(hp + 1) * 64]
                    op("dve", lambda E, hs=hs, phv=phv: E.tensor_tensor(out=Hst[hg][hs, :, :], in0=phv, in1=Hst[hg][hs, :, :], op=ALU.add), reads=PSK(ph) + [("H32", hg, hp)], writes=[("H32", hg, hp)])
                    op("dve", lambda E, hs=hs: E.tensor_tensor(out=Hst[hg][hs, :, :], in0=Hst[hg][hs, :, :], in1=bc_last(WC[hs, :, ci], 64), op=ALU.mult), reads=[("H32", hg, hp)] + [("WC", k) for k in range(4)] + SHR, writes=[("H32", hg, hp)])
                    op("act", lambda E, hs=hs, hp=hp: E.copy(out=Hbd[hg][1 - c][hs, :, hp * 64:(hp + 1) * 64], in_=Hst[hg][hs, :, :]), reads=[("H32", hg, hp)], writes=[("Hbd", hg, 1 - c, hp)])
                for kl in range(4):
                    op("pe", lambda E, kl=kl: E.matmul(pair(py, kl), lhsT=QR[:, kl, ci, 64:128], rhs=Hbd[hg][c][:, kl, :], start=True, stop=False), reads=QK + [("Hbd", hg, c, 0), ("Hbd", hg, c, 1)] + SHR, writes=PSK(py))
                    for hi in (2 * kl, 2 * kl + 1):
                        op("pe", lambda E, hi=hi: E.matmul(psb[py][rs, hi * 64:(hi + 1) * 64], lhsT=h8(MkT[c])[:, hi, :], rhs=Vc[c][:, hi * 64:(hi + 1) * 64], start=False, stop=False), reads=[("MkT", c), ("Vc", c)] + ZP + SHR, writes=PSK(py))
                    for hi in (2 * kl, 2 * kl + 1):
                        op("pe", lambda E, hi=hi: E.matmul(psb[py][rs, hi * 64:(hi + 1) * 64], lhsT=h8(MbT[c])[:, hi, :], rhs=Us[c][:, hi * 64:(hi + 1) * 64], start=False, stop=(hi % 2 == 1)), reads=[("MbT", c), ("Us", c)] + ZP + SHR, writes=PSK(py))
                ypost(py, rs, 0, 8)

            def sample_chain(hg):
                rs = slice(0, 64)
                c = 0
                WK = lambda i: [("w", i), ("wB", i)]
                scores_inverse(hg, 4, 0, True)
                TT_ = ZZ[0][1]
                qmk = cs("qmask").rearrange("p (j t) -> p j t", j=16)
                rowm = cs("rowm")
                w4b, w5b, w6b, w1b = w[4].bitcast(BF16), w[5].bitcast(BF16), w[3].bitcast(BF16), w[1].bitcast(BF16)
                Qm = w4b[:, 0:1024].rearrange("p (j t) -> p j t", j=16)
                Rm = w4b[:, 1024:2048].rearrange("p (j t) -> p j t", j=16)
                Vm = w5b[:, 0:2048].rearrange("p (j c) -> p j c", j=16)
                Um = w6b[:, 0:2048].rearrange("p (j c) -> p j c", j=16)
                Sn32 = w[0][:, 0:1024].rearrange("p (j k) -> p j k", j=16)
                Snb = w1b[:, 0:1024].rearrange("p (j k) -> p j k", j=16)
                So = w[2][:, 0:1024].rearrange("p (j k) -> p j k", j=16)

                def load_s0(kc_):
                    op("sp", lambda E: E.dma_start(out=Sn32, in_=swkv[:, 2 * kc_:2 * kc_ + 2, :, :].rearrange("s h v k -> (h v) s k")), reads=SHR, writes=WK(0), dma=True)

                def pairk(kl):
                    kc = hg * 4 + kl
                    op("dve", lambda E: E.tensor_tensor(out=Qm, in0=bc_mid(QR[:, kl, 8, 0:64], 16), in1=qmk, op=ALU.mult), reads=QK + [("cstb",)] + SHR, writes=WK(4))
                    op("dve", lambda E: E.tensor_tensor(out=Rm, in0=bc_mid(QR[:, kl, 8, 64:128], 16), in1=qmk, op=ALU.mult), reads=QK + [("cstb",)] + SHR, writes=WK(4))
                    v4 = lambda t: pairv(t, kl).unsqueeze(1).to_broadcast([128, 16, 128])
                    rm4 = rowm.unsqueeze(2).to_broadcast([128, 16, 128])
                    op("dve", lambda E: E.tensor_tensor(out=Vm, in0=v4(Vc[0]), in1=rm4, op=ALU.mult), reads=[("Vc", 0), ("cstb",)] + ZP + SHR, writes=WK(5))
                    if kl == 0:
                        load_s0(kc)
                    op("pool", lambda E: E.tensor_copy(out=Snb, in_=Sn32), reads=WK(0) + SHR, writes=WK(1))
                    if kl + 1 < 4:
                        load_s0(kc + 1)
                    op("pool", lambda E: E.memset(H0bd, 0.0), reads=SHR, writes=[("H0bd",)])
                    pT = [ps_take(), ps_take()]
                    for hp in range(2):
                        hs = slice(hp * 64, hp * 64 + 64)
                        vT_ = psb[pT[hp]][:].bitcast(BF16)
                        for j in range(16):
                            op("pe", lambda E, j=j, hs=hs, vT_=vT_: E.transpose(out=vT_[hs, j * 64:(j + 1) * 64], in_=Snb[hs, j, :], identity=identb[hs, hs]),
                               reads=WK(1) + [("cstb",)] + SHR, writes=PSK(pT[hp]))
                        op("act", lambda E, hs=hs, hp=hp, vT_=vT_: E.copy(out=H0bd[hs, :, hp * 64:(hp + 1) * 64], in_=vT_[hs, :].rearrange("p (j c) -> p j c", j=16)),
                           reads=PSK(pT[hp]) + SHR, writes=[("H0bd",)])
                    px, py, pu = ps_take(), ps_take(), ps_take()
                    pair = lambda pb: pairv(psb[pb][rs, :], kl)
                    for j in range(16):
                        op("pe", lambda E, j=j: E.matmul(pair(px), lhsT=Qm[:, j, :], rhs=H0bd[:, j, :], start=(j == 0), stop=False), reads=WK(4) + [("H0bd",)] + SHR, writes=PSK(px))
                        op("pe", lambda E, j=j: E.matmul(pair(py), lhsT=Rm[:, j, :], rhs=H0bd[:, j, :], start=(j == 0), stop=False), reads=WK(4) + [("H0bd",)] + SHR, writes=PSK(py))
                    his = [2 * kl, 2 * kl + 1]
                    for hi in his:
                        op("pe", lambda E, hi=hi: E.matmul(psb[px][rs, hi * 64:(hi + 1) * 64], lhsT=h8(LkT[0])[:, hi, :], rhs=Vc[0][:, hi * 64:(hi + 1) * 64], start=False, stop=(hi % 2 == 1)), reads=[("LkT", 0), ("Vc", 0)] + ZP + SHR, writes=PSK(px))
                        op("pe", lambda E, hi=hi: E.matmul(psb[py][rs, hi * 64:(hi + 1) * 64], lhsT=h8(MkT[0])[:, hi, :], rhs=Vc[0][:, hi * 64:(hi + 1) * 64], start=False, stop=False), reads=[("MkT", 0), ("Vc", 0)] + ZP + SHR, writes=PSK(py))
                    for hi in his:
                        op("act", lambda E, hi=hi: E.copy(out=Xs[0][rs, hi * 64:(hi + 1) * 64], in_=psb[px][rs, hi * 64:(hi + 1) * 64]), reads=PSK(px) + ZP + SHR, writes=[("Xs", 0)])
                    for hi in his:
                        op("pe", lambda E, hi=hi: E.matmul(psb[pu][rs, hi * 64:(hi + 1) * 64], lhsT=h8(TT_)[:, hi, :], rhs=Xs[0][:, hi * 64:(hi + 1) * 64], start=True, stop=True), reads=[("ZZ", 0, 1), ("Xs", 0)] + ZP + SHR, writes=PSK(pu))
                    for hi in his:
                        op("act", lambda E, hi=hi: E.copy(out=Us[0][rs, hi * 64:(hi + 1) * 64], in_=psb[pu][rs, hi * 64:(hi + 1) * 64]), reads=PSK(pu) + ZP + SHR, writes=[("Us", 0)])
                    for hi in his:
                        op("pe", lambda E, hi=hi: E.matmul(psb[py][rs, hi * 64:(hi + 1) * 64], lhsT=h8(MbT[0])[:, hi, :], rhs=Us[0][:, hi * 64:(hi + 1) * 64], start=False, stop=(hi % 2 == 1)), reads=[("MbT", 0), ("Us", 0)] + ZP + SHR, writes=PSK(py))
                    for hi in his:
                        ypost(py, rs, hi, 1)
                    op("pool", lambda E: E.tensor_tensor(out=H0bd, in0=H0bd, in1=bc_last(WCs[:, kl, :], 128), op=ALU.mult), reads=[("H0bd",), ("WCs", kl)] + SHR, writes=[("H0bd",)])
                    op("dve", lambda E: E.tensor_tensor(out=Um, in0=v4(Us[0]), in1=rm4, op=ALU.mult), reads=[("Us", 0), ("cstb",)] + ZP + SHR, writes=WK(3))
                    for jb in range(4):
                        pS = ps_take()
                        for jj in range(4):
                            j = jb * 4 + jj
                            o_ = psb[pS][:, jj * 128:(jj + 1) * 128]
                            op("pe", lambda E, j=j, o_=o_: E.matmul(o_, lhsT=H0bd[:, j, :], rhs=identb, start=True, stop=False), reads=[("H0bd",), ("cstb",)] + SHR, writes=PSK(pS))
                            op("pe", lambda E, j=j, o_=o_: E.matmul(o_, lhsT=Vm[:, j, :], rhs=Khtok[:, kl * 128:(kl + 1) * 128], start=False, stop=False), reads=WK(5) + [("Khtok",)] + ZP + SHR, writes=PSK(pS))
                            op("pe", lambda E, j=j, o_=o_: E.matmul(o_, lhsT=Um[:, j, :], rhs=Bhtok[:, kl * 128:(kl + 1) * 128], start=False, stop=True), reads=WK(3) + [("Bhtok",)] + ZP + SHR, writes=PSK(pS))
                        for hp in range(2):
                            hs = slice(hp * 64, hp * 64 + 64)
                            op("act", lambda E, jb=jb, hs=hs, hp=hp, pS=pS: E.copy(out=So[hs, jb * 4:(jb + 1) * 4, :], in_=psb[pS][hs, :].rearrange("p (j c) -> p j c", j=4)[:, :, hp * 64:(hp + 1) * 64]),
                               reads=PSK(pS) + SHR, writes=WK(2))
                    op("sp", lambda E: E.dma_start(out=wkv_s[:, 2 * kc:2 * kc + 2, :, :].rearrange("s h v k -> (h v) s k"), in_=So), reads=WK(2) + SHR, writes=[("o_wkv_s", kc)], dma=True)
                for kl in range(4):
                    pairk(kl)

            def do_block(hg, b):
                SAMP = (b == 4)
                r = 64 if SAMP else 128
                bc0 = b * 128
                pK, pB, pV = ps_take(), ps_take(), ps_take()
                vK, vB, vV = [psb[x][:].bitcast(BF16) for x in (pK, pB, pV)]
                for kl in range(4):
                    op("pe", lambda E, kl=kl: E.transpose(out=vK[0:r, kl * 128:(kl + 1) * 128], in_=KtT[:, kl, bc0:bc0 + r], identity=identb), reads=[("gbuf", kl), ("cstb",)], writes=PSK(pK))
                    op("pe", lambda E, kl=kl: E.transpose(out=vB[0:r, kl * 128:(kl + 1) * 128], in_=BtT[:, kl, bc0:bc0 + r], identity=identb), reads=[("gbuf", 4 + kl), ("cstb",)], writes=PSK(pB))
                    op("pe", lambda E, kl=kl: E.transpose(out=vV[0:r, kl * 128:(kl + 1) * 128], in_=RKV[2][:, kl, bc0:bc0 + r], identity=identb), reads=[("rkv", 2, kl), ("cstb",)] + SHR, writes=PSK(pV))
                op("act", lambda E: E.copy(out=Ktok[0:r, :], in_=vK[0:r, 0:512]), reads=PSK(pK) + SHR, writes=[("Ktok",)])
                op("act", lambda E: E.copy(out=Btok[0:r, :], in_=vB[0:r, 0:512]), reads=PSK(pB) + SHR, writes=[("Btok",)])
                for c in ([0] if SAMP else [0, 1]):
                    rs = slice(c * 64, c * 64 + 64)
                    op("dve", lambda E, c=c, rs=rs: E.tensor_copy(out=Vc[c][rs, :], in_=vV[rs, 0:512]), reads=PSK(pV) + ZP + SHR, writes=[("Vc", c)])
                if SAMP:
                    pK2, pB2 = ps_take(), ps_take()
                    vK2, vB2 = psb[pK2][:].bitcast(BF16), psb[pB2][:].bitcast(BF16)
                    for kl in range(4):
                        op("pe", lambda E, kl=kl: E.transpose(out=vK2[0:64, kl * 128:(kl + 1) * 128], in_=Khs[:, kl, :], identity=identb), reads=[("Khs", kl), ("cstb",)] + SHR, writes=PSK(pK2))
                        op("pe", lambda E, kl=kl: E.transpose(out=vB2[0:64, kl * 128:(kl + 1) * 128], in_=Bhs[:, kl, :], identity=identb), reads=[("Bhs", kl), ("cstb",)] + SHR, writes=PSK(pB2))
                    op("act", lambda E: E.copy(out=Khtok[0:64, :], in_=vK2[0:64, 0:512]), reads=PSK(pK2) + ZP + SHR, writes=[("Khtok",)])
                    op("act", lambda E: E.copy(out=Bhtok[0:64, :], in_=vB2[0:64, 0:512]), reads=PSK(pB2) + ZP + SHR, writes=[("Bhtok",)])
                    if cfg.get('rw_level', 4) >= 4:
                        sample_chain(hg)
                else:
                    interleave([(lambda: scores_inverse(hg, b, 0, False), [0, 1, 2, 3]), (lambda: scores_inverse(hg, b, 1, False), [4, 5, 6, 7])])
                    for c in (0, 1):
                        chunk(hg, b, c)
                pzt = ps_take()
                vz = psb[pzt][:].bitcast(BF16)
                for kl in range(4):
                    op("pe", lambda E, kl=kl: E.transpose(out=vz[:, kl * 128:kl * 128 + r], in_=pairv(yh[0:r, :], kl), identity=identb[0:r, 0:r]), reads=[("yh",), ("yh", 0), ("yh", 1), ("cstb",)] + SHR, writes=PSK(pzt))
                for kl in range(4):
                    kc = hg * 4 + kl
                    op("dve", lambda E, kl=kl, kc=kc: E.tensor_scalar(out=zt[:, kl, 0:r], in0=vz[:, kl * 128:kl * 128 + r], scalar1=pc(PI["lnx_g"], kc), scalar2=pc(PI["lnx_b"], kc), op0=ALU.mult, op1=ALU.add), reads=PSK(pzt) + [("pcol",)] + SHR, writes=[("zt",)])
                op("dve", lambda E: E.tensor_tensor(out=zt[:, :, 0:r], in0=zt[:, :, 0:r], in1=bon[:, :, bc0:bc0 + r], op=ALU.add), reads=[("zt",)] + [("bon", k) for k in range(4)] + SHR, writes=[("zt",)])
                op("dve", lambda E: E.tensor_tensor(out=zb[:, hg * 4:(hg + 1) * 4, bc0:bc0 + r], in0=zt[:, :, 0:r], in1=gT[:, :, bc0:bc0 + r], op=ALU.mult), reads=[("zt",)] + [("gT", k) for k in range(4)] + SHR, writes=[("zb", hg, b)])

            def do_hg(hg):
                for kind in range(3):
                    sw = wnext()
                    def ev_mix(oc, pvs, kind=kind):
                        mix(kind * 8 + hg * 4 + oc, pvs, RKV[kind][:, oc, :], ("rkv", kind, oc))
                    ws_mm(sw, 4, hT, HTK, segs_r, KC, ev_mix)
                if hg == 0:
                    for g in range(2):
                        sw = wnext()
                        def ev_gb(oc, pvs, g=g):
                            kc = g * 4 + oc
                            for (pv, pb, c0, c1) in pvs:
                                op("act", lambda E, pv=pv, kc=kc, c0=c0, c1=c1: E.activation(out=merged[:, kc, c0:c1], in_=pv, func=AF.Sigmoid), reads=PSK(pb), writes=[("merged", kc)])
                        ws_mm(sw, 4, hT, HTK, segs, KC, ev_gb)
                for kl2 in range(0, 4, 2):
                    interleave([(lambda kl=kl2: elem(hg, kl, 0), [0, 1, 2, 3]), (lambda kl=kl2 + 1: elem(hg, kl, 1), [4, 5, 6, 7])])
                for b in range(nblk):
                    if cfg.get('rw_level', 4) >= 2:
                        do_block(hg, b)
                if LAST:
                    pw = ps_take()
                    Ho = w[3][:, 0:512].rearrange("p (k c) -> p k c", k=4)
                    for kl in range(4):
                        op("pe", lambda E, kl=kl: E.transpose(out=psb[pw][0:64, kl * 128:(kl + 1) * 128], in_=Hst[hg][:, kl, :], identity=identf), reads=[("H32", hg, 0), ("H32", hg, 1), ("cst",)], writes=PSK(pw))
                    op("act", lambda E: E.copy(out=Ho[0:64, :, :], in_=psb[pw][0:64, :].rearrange("p (k c) -> p k c", k=4)), reads=PSK(pw) + SHR, writes=WK(3) + [("wB", 3)])
                    dstv = wkv_p.rearrange("(kc hp) v k -> v kc hp k", hp=2)[:, hg * 4:(hg + 1) * 4, :, :]
                    op("sp", lambda E: E.dma_start(out=dstv, in_=Ho[0:64, :, :].rearrange("p k (hp c) -> p k hp c", hp=2)), reads=WK(3) + [("wB", 3)] + SHR, writes=[("o_wkv_p", hg)], dma=True)
            for hg in range(2):
                do_hg(hg)
            tap("zb", zb[:, :, :], [128, KC, 576], [("zb", hg, b) for hg in range(2) for b in range(nblk)])
            ZBK = [("zb", hg, b) for hg in range(2) for b in range(nblk)]
            for g in range(2):
                sw = wnext()
                def ev_ob(oc, pvs, g=g):
                    kc = g * 4 + oc
                    for (pv, pb, c0, c1) in pvs:
                        op("dve", lambda E, pv=pv, kc=kc, c0=c0, c1=c1: E.tensor_tensor(out=merged[:, kc, c0:c1], in0=pv, in1=merged[:, kc, c0:c1], op=ALU.mult), reads=PSK(pb) + [("merged", kc)], writes=[("merged", kc)])
                ws_mm(sw, 4, zb, ZBK + SHR, segs, KC, ev_ob)

        def phase_n(p):
            P0 = (p == 0)
            LAST = (p == NPASS - 1)
            nblk = 5 if P0 else 4
            brows = lambda b: 64 if b == 4 else 128
            bcol = lambda b: b * 128
            load_gain(0, 0)
            for b in range(nblk):
                r = brows(b)
                xi = b % 2
                src = x_s[0:64, :] if b == 4 else x_p[p * TT + b * 128: p * TT + (b + 1) * 128, :]
                op("sp", lambda E, xi=xi, r=r, src=src: E.dma_start(out=xs[xi][0:r, :], in_=src), writes=[("xs", xi)], dma=True)
                op("act", lambda E, xi=xi, r=r: E.activation(out=junk[0:r, :], in_=xs[xi][0:r, :], func=AF.Square, accum_out=st[0:r, 0:1]),
                   reads=[("xs", xi)], writes=[("junk",), ("st", 0)])
                rstd_from_ssq(st[0:r, 0:1], st[0:r, 1:2], D, 1e-6, [("st", 0)], [("st", 1)])
                need32 = (b == 4) or (LAST and b == 3)
                if need32:
                    op("dve", lambda E, xi=xi, r=r: E.scalar_tensor_tensor(out=junk[0:r, :], in0=xs[xi][0:r, :], scalar=st[0:r, 1:2], in1=gain[0][0:r, :], op0=ALU.mult, op1=ALU.mult),
                       reads=[("xs", xi), ("st", 1), ("gain", 0)], writes=[("junk",)])
                    if b == 4:
                        dst = shift_s.rearrange("(s o) d -> s o d", o=1)
                        op("sp", lambda E, dst=dst: E.dma_start(out=dst[:, 0, :], in_=junk[3:64:4, :]), reads=[("junk",)], writes=[("o_shift_s",)], dma=True)
                    else:
                        op("sp", lambda E: E.dma_start(out=shift_p[0:1, :], in_=junk[127:128, :]), reads=[("junk",)], writes=[("o_shift_p",)], dma=True)
                op("dve", lambda E, xi=xi, r=r: E.scalar_tensor_tensor(out=hb[0:r, :], in0=xs[xi][0:r, :], scalar=st[0:r, 1:2], in1=gain[0][0:r, :], op0=ALU.mult, op1=ALU.mult),
                   reads=[("xs", xi), ("st", 1), ("gain", 0)], writes=[("hb",)])
                pb = ps_take()
                pv = psb[pb][:].bitcast(BF16)
                for kc in range(KC):
                    op("pe", lambda E, kc=kc, r=r, pv=pv: E.transpose(out=pv[:, kc * 128: kc * 128 + r], in_=hb[0:r, kc * 128:(kc + 1) * 128], identity=identb[0:r, 0:r]),
                       reads=[("hb",), ("cstb",)], writes=PSK(pb))
                op("act", lambda E, b=b, r=r, pv=pv: E.copy(out=hT[:, :, bcol(b): bcol(b) + r], in_=pv.rearrange("p (k c) -> p k c", k=KC)[:, :, 0:r]),
                   reads=PSK(pb), writes=[("hT", b)])
            if P0:
                op("sp", lambda E: E.dma_start(out=xs[0][0:16, :], in_=sshift[:, :]), writes=[("xs", 0)], dma=True)
                op("dve", lambda E: E.tensor_copy(out=hb[0:16, :], in_=xs[0][0:16, :]), reads=[("xs", 0)], writes=[("hb",)])
                pb = ps_take()
                pv = psb[pb][:].bitcast(BF16)
                for kc in range(KC):
                    op("pe", lambda E, kc=kc, pv=pv: E.transpose(out=pv[:, kc * 128: kc * 128 + 16], in_=hb[0:16, kc * 128:(kc + 1) * 128], identity=identb[0:16, 0:16]),
                       reads=[("hb",), ("cstb",)], writes=PSK(pb))
                op("act", lambda E, pv=pv: E.copy(out=hT[:, :, 576:592], in_=pv.rearrange("p (k c) -> p k c", k=KC)[:, :, 0:16]),
                   reads=PSK(pb), writes=[("hT", 5)])
            HTK = [("hT", b) for b in range(nblk)] + ([("hT", 5)] if P0 else [])
            tap("hT", hT[:, :, :], [128, KC, NCOL], HTK)


        def do_pass(p):
            P0 = (p == 0)
            LAST = (p == NPASS - 1)
            nblk = 5 if P0 else 4
            segs = [(0, 512)] + ([(512, 576)] if P0 else [])
            segs_r = [(0, 512)] + ([(512, 592)] if P0 else [])
            ncolu = 576 if P0 else 512
            brows = lambda b: 64 if b == 4 else 128
            bcol = lambda b: b * 128

            if p == 0:
                phase_n(0)
            HTK = [("hT", b) for b in range(nblk)] + ([("hT", 5)] if P0 else [])
            def ws_mm(ws, noc, act_t, act_keys, sg, nk, evac, lhs=None):
                for oc in range(noc):
                    pbs = [ps_take() for _ in sg]
                    for si, (c0, c1) in enumerate(sg):
                        for k in range(nk):
                            lh = (lhs(k, oc) if lhs else wb[ws][:, k, oc * 128:(oc + 1) * 128])
                            op("pe", lambda E, lh=lh, k=k, c0=c0, c1=c1, pb=pbs[si]: E.matmul(psb[pb][:, 0:c1 - c0], lhsT=lh, rhs=act_t[:, k, c0:c1], start=(k == 0), stop=(k == nk - 1)),
                               reads=([("wb", ws)] if ws is not None else []) + act_keys, writes=PSK(pbs[si]))
                    evac(oc, [(psb[pb][:, 0:c1 - c0], pb, c0, c1) for pb, (c0, c1) in zip(pbs, sg)])

            if do_rwkv:
                rwkv_phase(p, P0, LAST, nblk, segs, segs_r, HTK, ws_mm)
            else:
                op("pool", lambda E: E.memset(merged[:], 0.0), writes=[("merged", k) for k in range(KC)])

            phase_barrier()
            AR.reset()
            UW = 30 + 576
            uext = AR.take(KC * UW, BF16).rearrange("p (k c) -> p k c", k=KC)
            uexs = AR.take(KC * NSEQ * 34, BF16).rearrange("p (k s c) -> p k s c", k=KC, s=NSEQ)
            cT = AR.take(KC * 576, F32).rearrange("p (k c) -> p k c", k=KC)
            cbf = AR.take(KC * 576, BF16).rearrange("p (k c) -> p k c", k=KC)
            csq = AR.take(KC * 576, BF16).rearrange("p (k c) -> p k c", k=KC)
            csl = AR.take(KC * 576, BF16).rearrange("p (k c) -> p k c", k=KC)
            sgt = AR.take(4 * 576, F32).rearrange("p (k c) -> p k c", k=4)
            mean_sb = AR.take(576, F32)
            rstd_sb = AR.take(576, F32)
            u32 = AR.take(KC * 64, F32).rearrange("p (k c) -> p k c", k=KC)
            tmpb = AR.take(576, BF16)
            dtmp = AR.take(576, F32)

            op("pool", lambda E: E.tensor_copy(out=uext[:, :, 0:30], in_=uhist[:, :, :]), reads=[("uhist",)] + SHR, writes=[("uext", "h")])
            for g in range(2):
                sw = wnext()
                def ev_gate(oc, pvs):
                    for (pv, pb, c0, c1) in pvs:
                        op("act", lambda E, pv=pv, oc=oc, c0=c0, c1=c1: E.activation(out=sgt[:, oc, c0:c1], in_=pv, func=AF.Sigmoid),
                           reads=PSK(pb) + SHR, writes=[("sgt", oc)])
                ws_mm(sw, 4, hT, HTK, segs, KC, ev_gate)
                sw = wnext()
                def ev_val(oc, pvs, g=g):
                    kc = g * 4 + oc
                    for (pv, pb, c0, c1) in pvs:
                        if c0 == 0:
                            op("dve", lambda E, pv=pv, oc=oc, kc=kc: E.tensor_tensor(out=uext[:, kc, 30:542], in0=pv, in1=sgt[:, oc, 0:512], op=ALU.mult),
                               reads=PSK(pb) + [("sgt", oc)] + SHR, writes=[("uext", kc)])
                            if LAST:
                                op("dve", lambda E, pv=pv, oc=oc, kc=kc: E.tensor_tensor(out=u32[:, kc, 0:30], in0=pv[:, 482:512], in1=sgt[:, oc, 482:512], op=ALU.mult),
                                   reads=PSK(pb) + [("sgt", oc)] + SHR, writes=[("u32", kc)])
                        else:
                            op("dve", lambda E, pv=pv, oc=oc, kc=kc: E.tensor_tensor(out=uexs[:, kc, :, 30:34], in0=pv.rearrange("p (s t) -> p s t", t=4), in1=sgt[:, oc, 512:576].rearrange("p (s t) -> p s t", t=4), op=ALU.mult),
                               reads=PSK(pb) + [("sgt", oc)] + SHR, writes=[("uexs", kc)])
                            op("dve", lambda E, pv=pv, oc=oc, kc=kc: E.tensor_tensor(out=u32[:, kc, 0:64], in0=pv, in1=sgt[:, oc, 512:576], op=ALU.mult),
                               reads=PSK(pb) + [("sgt", oc)] + SHR, writes=[("u32", kc)])
                ws_mm(sw, 4, hT, HTK, segs, KC, ev_val)
            UK = [("uext", kc) for kc in range(KC)] + [("uext", "h")]
            op("pool", lambda E: E.tensor_copy(out=uhist[:, :, :], in_=uext[:, :, 512:542]), reads=UK + SHR, writes=[("uhist",)])
            def emit_u32(nrows, dst_fn):
                pbA, pbB = ps_take(), ps_take()
                for kc in range(KC):
                    pb = pbA if kc < 4 else pbB
                    op("pe", lambda E, kc=kc, pb=pb: E.transpose(out=psb[pb][0:nrows, (kc % 4) * 128:(kc % 4 + 1) * 128], in_=u32[:, kc, 0:nrows], identity=identf),
                       reads=[("u32", kc), ("cst",)] + SHR, writes=PSK(pb))
                op("act", lambda E: E.copy(out=junk[0:nrows, 0:512], in_=psb[pbA][0:nrows, :]), reads=PSK(pbA), writes=[("junk",)])
                op("act", lambda E: E.copy(out=junk[0:nrows, 512:1024], in_=psb[pbB][0:nrows, :]), reads=PSK(pbB), writes=[("junk",)])
                dst_fn()
            if LAST:
                emit_u32(30, lambda: op("sp", lambda E: E.dma_start(out=conv_p[:, :], in_=junk[0:30, :]), reads=[("junk",)], writes=[("o_conv_p",)], dma=True))
            if P0:
                def dst():
                    for t in range(4):
                        op("sp", lambda E, t=t: E.dma_start(out=conv_s[:, 26 + t, :], in_=junk[t:64:4, :]), reads=[("junk",)], writes=[("o_conv_s", t)], dma=True)
                    op("sp", lambda E: E.dma_start(out=conv_s[:, 0:26, :], in_=sconv[:, 4:30, :]), writes=[("o_conv_s", 9)], dma=True)
                emit_u32(64, dst)
                for q in range(4):
                    op("pool", lambda E, q=q: E.dma_start(out=hb[0:120, :], in_=sconv[4 * q:4 * q + 4, :, :].rearrange("s r d -> (s r) d")), writes=[("hb",)], dma=True)
                    pb = ps_take()
                    pv = psb[pb][:].bitcast(BF16)
                    for kc in range(KC):
                        op("pe", lambda E, kc=kc, pv=pv: E.transpose(out=pv[:, kc * 120:(kc + 1) * 120], in_=hb[0:120, kc * 128:(kc + 1) * 128], identity=identb[0:120, 0:120]),
                           reads=[("hb",), ("cstb",)], writes=PSK(pb))
                    for kc in range(KC):
                        op("act", lambda E, kc=kc, q=q, pv=pv: E.copy(out=uexs[:, kc, 4 * q:4 * q + 4, 0:30], in_=pv[:, kc * 120:(kc + 1) * 120].rearrange("p (s r) -> p s r", s=4)),
                           reads=PSK(pb) + SHR, writes=[("uexs", kc, q)])
            dcur = 0
            for kc in range(KC):
                pbm = ps_take()
                pbs = ps_take() if P0 else None
                for j in range(31):
                    d = dcur % 8
                    dcur += 1
                    op("dve", lambda E, d=d, kc=kc, j=j: E.tensor_scalar(out=diag[d][:], in0=identb, scalar1=pcol[:, PC_CW + kc * 31 + j: PC_CW + kc * 31 + j + 1], scalar2=None, op0=ALU.mult),
                       reads=[("cstb",), ("pcol",)], writes=[("diag", d)])
                    op("pe", lambda E, d=d, kc=kc, j=j, pbm=pbm: E.matmul(psb[pbm][:, 0:512], lhsT=diag[d][:], rhs=uext[:, kc, j:j + 512], start=(j == 0), stop=(j == 30)),
                       reads=[("diag", d), ("uext", kc), ("uext", "h")] + SHR, writes=PSK(pbm))
                    if P0:
                        op("pe", lambda E, d=d, kc=kc, j=j, pbs=pbs: E.matmul(psb[pbs][:, 0:64].rearrange("p (s t) -> p s t", t=4), lhsT=diag[d][:], rhs=uexs[:, kc, :, j:j + 4], start=(j == 0), stop=(j == 30)),
                           reads=[("diag", d), ("uexs", kc)] + [("uexs", kc, q) for q in range(4)] + SHR, writes=PSK(pbs))
                for (pb, c0, c1) in [(pbm, 0, 512)] + ([(pbs, 512, 576)] if P0 else []):
                    op("act", lambda E, pb=pb, kc=kc, c0=c0, c1=c1: E.activation(out=cT[:, kc, c0:c1], in_=psb[pb][:, 0:c1 - c0], func=AF.Identity, bias=pc(PI["conv_b"], kc), scale=1.0),
                       reads=PSK(pb) + [("pcol",)] + SHR, writes=[("cT", kc)])
                op("pool", lambda E, kc=kc: E.tensor_copy(out=cbf[:, kc, 0:ncolu], in_=cT[:, kc, 0:ncolu]), reads=[("cT", kc)] + SHR, writes=[("cbf", kc)])
                op("act", lambda E, kc=kc: E.activation(out=csq[:, kc, 0:ncolu], in_=cT[:, kc, 0:ncolu], func=AF.Square), reads=[("cT", kc)] + SHR, writes=[("csq", kc)])
            odv = cs("odiv", BF16)
            for (c0, c1) in segs:
                pm_, pq_ = ps_take(), ps_take()
                for kc in range(KC):
                    op("pe", lambda E, kc=kc, c0=c0, c1=c1, pm_=pm_: E.matmul(psb[pm_][:, 0:c1 - c0], lhsT=odv, rhs=cbf[:, kc, c0:c1], start=(kc == 0), stop=(kc == KC - 1)),
                       reads=[("cbf", kc), ("cstb",)] + SHR, writes=PSK(pm_))
                for kc in range(KC):
                    op("pe", lambda E, kc=kc, c0=c0, c1=c1, pq_=pq_: E.matmul(psb[pq_][:, 0:c1 - c0], lhsT=odv, rhs=csq[:, kc, c0:c1], start=(kc == 0), stop=(kc == KC - 1)),
                       reads=[("csq", kc), ("cstb",)] + SHR, writes=PSK(pq_))
                op("act", lambda E, c0=c0, c1=c1, pm_=pm_: E.copy(out=mean_sb[:, c0:c1], in_=psb[pm_][:, 0:c1 - c0]), reads=PSK(pm_) + SHR, writes=[("mean_sb", c0)])
                op("dve", lambda E, c0=c0, c1=c1: E.tensor_tensor(out=dtmp[:, c0:c1], in0=mean_sb[:, c0:c1], in1=mean_sb[:, c0:c1], op=ALU.mult), reads=[("mean_sb", c0)] + SHR, writes=[("dtmp",)])
                op("dve", lambda E, c0=c0, c1=c1, pq_=pq_: E.tensor_tensor(out=rstd_sb[:, c0:c1], in0=psb[pq_][:, 0:c1 - c0], in1=dtmp[:, c0:c1], op=ALU.subtract), reads=PSK(pq_) + [("dtmp",)] + SHR, writes=[("rstd_sb", c0)])
                op("act", lambda E, c0=c0, c1=c1: E.activation(out=rstd_sb[:, c0:c1], in_=rstd_sb[:, c0:c1], func=AF.Sqrt, bias=1e-5, scale=1.0), reads=[("rstd_sb", c0)] + SHR, writes=[("rstd_sb", c0)])
                op("dve", lambda E, c0=c0, c1=c1: E.reciprocal(out=rstd_sb[:, c0:c1], in_=rstd_sb[:, c0:c1]), reads=[("rstd_sb", c0)] + SHR, writes=[("rstd_sb", c0)])
            for g in range(2):
                sw = wnext()
                def ev_ga(oc, pvs, g=g):
                    kc = g * 4 + oc
                    for (pv, pb, c0, c1) in pvs:
                        op("act", lambda E, pv=pv, kc=kc, c0=c0, c1=c1: E.activation(out=gbuf[:, kc, c0:c1], in_=pv, func=AF.Sigmoid), reads=PSK(pb), writes=[("gbuf", kc)])
                ws_mm(sw, 4, hT, HTK, segs, KC, ev_ga)
            LNK = [("mean_sb", c0) for c0, _ in segs] + [("rstd_sb", c0) for c0, _ in segs]
            for kc in range(KC):
                op("dve", lambda E, kc=kc: E.tensor_tensor(out=dtmp[:, 0:ncolu], in0=cT[:, kc, 0:ncolu], in1=mean_sb[:, 0:ncolu], op=ALU.subtract), reads=[("cT", kc)] + LNK + SHR, writes=[("dtmp",)])
                op("dve", lambda E, kc=kc: E.tensor_tensor(out=dtmp[:, 0:ncolu], in0=dtmp[:, 0:ncolu], in1=rstd_sb[:, 0:ncolu], op=ALU.mult), reads=[("dtmp",)] + LNK + SHR, writes=[("dtmp",)])
                op("act", lambda E, kc=kc: E.activation(out=csl[:, kc, 0:ncolu], in_=dtmp[:, 0:ncolu], func=AF.Silu, bias=pc(PI["conv_ln_b"], kc), scale=pc(PI["conv_ln_g"], kc)),
                   reads=[("dtmp",), ("pcol",)] + SHR, writes=[("csl", kc)])
            tap("csl", csl[:, :, :], [128, KC, 576], [("csl", kc) for kc in range(KC)])
            for g in range(2):
                sw = wnext()
                def ev_oa(oc, pvs, g=g):
                    kc = g * 4 + oc
                    for (pv, pb, c0, c1) in pvs:
                        op("dve", lambda E, pv=pv, kc=kc, c0=c0, c1=c1: E.tensor_tensor(out=tmpb[:, c0:c1], in0=pv, in1=gbuf[:, kc, c0:c1], op=ALU.mult), reads=PSK(pb) + [("gbuf", kc)] + SHR, writes=[("tmpb",)])
                        op("pool", lambda E, kc=kc, c0=c0, c1=c1: E.tensor_tensor(out=merged[:, kc, c0:c1], in0=merged[:, kc, c0:c1], in1=tmpb[:, c0:c1], op=ALU.add), reads=[("tmpb",), ("merged", kc)] + SHR, writes=[("merged", kc)])
                ws_mm(sw, 4, csl, [("csl", kc) for kc in range(KC)] + SHR, segs, KC, ev_oa)
            MK = [("merged", kc) for kc in range(KC)]
            tap("merged", merged[:, :, :], [128, KC, NCOL], MK)

            phase_barrier()
            AR.reset()
            X1 = AR.take(5 * D, F32).rearrange("p (b d) -> p b d", b=5)
            load_gain(1, 1)
            load_gain(0, 2)
            swo = [wnext(ahead=1), wnext(ahead=1)]
            junk2 = AR.take(D, F32); hb2 = AR.take(D, BF16)
            def wout_block(b, junk, hb, so, JK, HK):
                r = brows(b)
                xi = b % 2
                pbh = [ps_take(), ps_take()]
                for hf in range(2):
                    for kc in range(KC):
                        op("pe", lambda E, hf=hf, kc=kc, b=b, r=r, pb=pbh[hf]: E.matmul(psb[pb][0:r, :], lhsT=merged[:, kc, bcol(b):bcol(b) + r], rhs=wb[swo[hf]][:, kc, :], start=(kc == 0), stop=(kc == KC - 1)),
                           reads=[("merged", kc), ("wb", swo[hf])], writes=PSK(pbh[hf]))
                    op("act", lambda E, hf=hf, r=r, pb=pbh[hf]: E.activation(out=junk[0:r, hf * 512:(hf + 1) * 512], in_=psb[pb][0:r, :], func=AF.Square, accum_out=st[0:r, so + 2 + hf:so + 3 + hf]),
                       reads=PSK(pbh[hf]) + SHR, writes=[JK, ("st", so + 2 + hf)])
                op("dve", lambda E, r=r: E.tensor_tensor(out=st[0:r, so + 4:so + 5], in0=st[0:r, so + 2:so + 3], in1=st[0:r, so + 3:so + 4], op=ALU.add), reads=[("st", so + 2), ("st", so + 3)], writes=[("st", so + 4)])
                rstd_from_ssq(st[0:r, so + 4:so + 5], st[0:r, so + 5:so + 6], D, 1e-6, [("st", so + 4)], [("st", so + 5)])
                src = x_s[0:64, :] if b == 4 else x_p[p * TT + b * 128: p * TT + (b + 1) * 128, :]
                op("sp", lambda E, xi=xi, r=r, src=src: E.dma_start(out=xs[xi][0:r, :], in_=src), writes=[("xs", xi)], dma=True)
                for hf in range(2):
                    op("dve", lambda E, hf=hf, r=r, pb=pbh[hf]: E.scalar_tensor_tensor(out=junk[0:r, hf * 512:(hf + 1) * 512], in0=psb[pb][0:r, :], scalar=st[0:r, so + 5:so + 6], in1=gain[1][0:r, hf * 512:(hf + 1) * 512], op0=ALU.mult, op1=ALU.mult),
                       reads=PSK(pbh[hf]) + [("st", so + 5), ("gain", 1)] + SHR, writes=[JK])
                op("pool", lambda E, b=b, r=r, xi=xi: E.tensor_tensor(out=X1[0:r, b, :], in0=xs[xi][0:r, :], in1=junk[0:r, :], op=ALU.add), reads=[("xs", xi), JK] + SHR, writes=[("X1", b)])
                op("act", lambda E, b=b, r=r: E.activation(out=junk[0:r, :], in_=X1[0:r, b, :], func=AF.Square, accum_out=st[0:r, so + 6:so + 7]), reads=[("X1", b)] + SHR, writes=[JK, ("st", so + 6)])
                rstd_from_ssq(st[0:r, so + 6:so + 7], st[0:r, so + 7:so + 8], D, 1e-6, [("st", so + 6)], [("st", so + 7)])
                op("dve", lambda E, b=b, r=r: E.scalar_tensor_tensor(out=hb[0:r, :], in0=X1[0:r, b, :], scalar=st[0:r, so + 7:so + 8], in1=gain[0][0:r, :], op0=ALU.mult, op1=ALU.mult),
                   reads=[("X1", b), ("st", so + 7), ("gain", 0)] + SHR, writes=[HK])
                pb = ps_take()
                pv = psb[pb][:].bitcast(BF16)
                for kc in range(KC):
                    op("pe", lambda E, kc=kc, r=r, pv=pv: E.transpose(out=pv[:, kc * 128: kc * 128 + r], in_=hb[0:r, kc * 128:(kc + 1) * 128], identity=identb[0:r, 0:r]),
                       reads=[HK, ("cstb",)] + SHR, writes=PSK(pb))
                op("act", lambda E, b=b, r=r, pv=pv: E.copy(out=hT[:, :, bcol(b): bcol(b) + r], in_=pv.rearrange("p (k c) -> p k c", k=KC)[:, :, 0:r]),
                   reads=PSK(pb), writes=[("hT", b)])
            for b0 in range(0, nblk, 2):
                gens = [(lambda b=b0: wout_block(b, junk, hb, 0, ("junk",), ("hb",)), [0, 1, 2, 3])]
                if b0 + 1 < nblk:
                    gens.append((lambda b=b0 + 1: wout_block(b, junk2, hb2, 40, ("junk2",), ("hb2",)), [4, 5, 6, 7]))
                interleave(gens)
            H2K = [("hT", b) for b in range(nblk)]

            aT = AR.take(32 * 576, BF16).rearrange("p (f c) -> p f c", f=32)
            fT = AR.take(KC * 576, F32).rearrange("p (k c) -> p k c", k=KC)
            rtmp = AR.take(576, F32)
            load_gain(1, 3)
            for g in range(8):
                sw = wnext()
                def ev_up(oc, pvs, g=g):
                    fc = g * 4 + oc
                    for (pv, pb, c0, c1) in pvs:
                        op("act", lambda E, pv=pv, c0=c0, c1=c1: E.activation(out=rtmp[:, c0:c1], in_=pv, func=AF.Relu), reads=PSK(pb) + SHR, writes=[("rtmp",)])
                        op("dve", lambda E, pv=pv, fc=fc, c0=c0, c1=c1: E.tensor_tensor(out=aT[:, fc, c0:c1], in0=pv, in1=rtmp[:, c0:c1], op=ALU.mult), reads=PSK(pb) + [("rtmp",)] + SHR, writes=[("aT", fc)])
                ws_mm(sw, 4, hT, H2K, segs, KC, ev_up)
            ATK = [("aT", fc) for fc in range(32)]
            if p + 1 < npass:
                phase_n(p + 1)
            for oc in range(KC):
                s_ = wnext()
                wv = wb[s_][:].rearrange("p k c -> p (k c)").rearrange("p (f c) -> p f c", f=32)
                def ev_dn(oc_, pvs, oc=oc):
                    for (pv, pb, c0, c1) in pvs:
                        op("act", lambda E, pv=pv, c0=c0, c1=c1: E.copy(out=fT[:, oc, c0:c1], in_=pv), reads=PSK(pb) + SHR, writes=[("fT", oc)])
                ws_mm(s_, 1, aT, ATK + SHR, segs, 32, ev_dn, lhs=lambda k, oc_, wv=wv: wv[:, k, :])
            FTK = [("fT", kc) for kc in range(KC)]
            def fout_block(b, junk, so, JK):
                r = brows(b)
                xi = b % 2
                pbh = [ps_take(), ps_take()]
                for kc in range(KC):
                    pb = pbh[kc // 4]
                    op("pe", lambda E, kc=kc, b=b, r=r, pb=pb: E.transpose(out=psb[pb][0:r, (kc % 4) * 128:(kc % 4 + 1) * 128], in_=fT[:, kc, bcol(b):bcol(b) + r], identity=identf),
                       reads=[("fT", kc), ("cst",)] + SHR, writes=PSK(pb))
                for hf in range(2):
                    op("act", lambda E, hf=hf, r=r, pb=pbh[hf]: E.activation(out=junk[0:r, hf * 512:(hf + 1) * 512], in_=psb[pb][0:r, :], func=AF.Square, accum_out=st[0:r, so + 8 + hf:so + 9 + hf]),
                       reads=PSK(pbh[hf]) + SHR, writes=[JK, ("st", so + 8 + hf)])
                op("dve", lambda E, r=r: E.tensor_tensor(out=st[0:r, so + 10:so + 11], in0=st[0:r, so + 8:so + 9], in1=st[0:r, so + 9:so + 10], op=ALU.add), reads=[("st", so + 8), ("st", so + 9)], writes=[("st", so + 10)])
                rstd_from_ssq(st[0:r, so + 10:so + 11], st[0:r, so + 11:so + 12], D, 1e-6, [("st", so + 10)], [("st", so + 11)])
                for hf in range(2):
                    op("dve", lambda E, hf=hf, r=r, pb=pbh[hf]: E.scalar_tensor_tensor(out=junk[0:r, hf * 512:(hf + 1) * 512], in0=psb[pb][0:r, :], scalar=st[0:r, so + 11:so + 12], in1=gain[1][0:r, hf * 512:(hf + 1) * 512], op0=ALU.mult, op1=ALU.mult),
                       reads=PSK(pbh[hf]) + [("st", so + 11), ("gain", 1)] + SHR, writes=[JK])
                op("pool", lambda E, b=b, r=r, xi=xi: E.tensor_tensor(out=xs[xi][0:r, :], in0=X1[0:r, b, :], in1=junk[0:r, :], op=ALU.add), reads=[("X1", b), JK] + SHR, writes=[("xs", xi)])
                dst = y_s[0:64, :] if b == 4 else y_p[p * TT + b * 128: p * TT + (b + 1) * 128, :]
                op("sp", lambda E, xi=xi, r=r, dst=dst: E.dma_start(out=dst, in_=xs[xi][0:r, :]), reads=[("xs", xi)], writes=[("o_y", p, b)], dma=True)
            for b0 in range(0, nblk, 2):
                gens = [(lambda b=b0: fout_block(b, junk, 0, ("junk",)), [0, 1, 2, 3])]
                if b0 + 1 < nblk:
                    gens.append((lambda b=b0 + 1: fout_block(b, junk2, 40, ("junk2",)), [4, 5, 6, 7]))
                interleave(gens)

        for p_ in range(npass):
            do_pass(p_)

        outk = [k for k in S.last_w if isinstance(k, tuple) and (str(k[0]).startswith("o_") or k[0] == "tap")]
        op("sp", None, reads=outk)
        S.emit(lambda name: es.enter_context(nc.semaphore(name)))
    nc_sched[0] = S
    return nc, tapd


def _in_maps(inp, cores):
    f = lambda a: np.ascontiguousarray(np.asarray(a, np.float32))
    pcolv = _pack_pcol(inp)
    gains = f(np.stack([inp["pre_mix_g"][0], inp["post_mix_g"][0], inp["pre_ffn_g"][0], inp["post_ffn_g"][0]]))
    shared = dict(gains=gains, pcol=pcolv, cst=_CST, w_in=f(inp["w_in"][0]), w_co=f(inp["w_conv_out"][0]),
                  w_ro=f(inp["w_rwkv_out"][0]), w_out=f(inp["w_out"][0]), w_up=f(inp["w_ff_up"][0]), w_dn=f(inp["w_ff_down"][0]),
                  w_dec=f(inp["w_decay_up"][0]), w_icl=f(inp["w_iclr_up"][0]), w_g=f(inp["w_gate_up"][0]))
    maps = []
    for c in cores:
        sl = slice(16 * c, 16 * c + 16)
        m = dict(shared)
        m.update(x_p=f(inp["x_prompt"][c]), x_s=f(inp["x_sample"][sl]).reshape(NS, D), sconv=f(inp["state_conv"][0][sl]),
                 sshift=f(inp["state_shift"][0][sl]), swkv=f(inp["state_wkv"][0][sl]))
        maps.append(m)
    return maps


_NC_CACHE = {}
nc_sched = [None]


def kernel(**inp):
    if "nc" not in _NC_CACHE:
        _NC_CACHE["nc"] = build_nc()[0]
    nc = _NC_CACHE["nc"]
    res = run_bass_kernel_spmd(nc, _in_maps(inp, range(NCORES)), core_ids=list(range(NCORES))).results
    g = lambda n: [np.asarray(r[n], np.float32) for r in res]
    y_p = np.stack(g("y_p"))
    y_s = np.concatenate(g("y_s")).reshape(128, 4, D)
    conv_p = np.stack(g("conv_p"))[None]
    shift_p = np.concatenate(g("shift_p"))[None]
    wkv_p = np.stack(g("wkv_p"))[None]
    conv_s = np.concatenate(g("conv_s"))[None]
    shift_s = np.concatenate(g("shift_s"))[None]
    wkv_s = np.concatenate(g("wkv_s"))[None]
    return (y_p, y_s, conv_p, shift_p, wkv_p, conv_s, shift_s, wkv_s)
```

```python
import numpy as np
from contextlib import ExitStack
import concourse.bass as bass
import concourse.mybir as mybir
from concourse.bass_utils import run_bass_kernel_spmd
from concourse.alu_op_type import AluOpType as ALU

F32 = mybir.dt.float32
BF16 = mybir.dt.bfloat16
AF = mybir.ActivationFunctionType
AX = mybir.AxisListType

NCORES = 8
D = 1024
SEQ = 2048
TT = 512
NPASS = 4
NS = 64
NSEQ = 16
NCOL = 592
KC = 8
DFF = 4096
NIN = 7424
C0 = float(np.exp(-0.5))


class Sched:
    COMPUTE = ("pe", "act", "dve", "pool")

    def __init__(self, nc, n_dma_sems=10, epoch_cap=4000):
        self.nc = nc
        self.ops = []
        self.last_w = {}
        self.readers = {}
        self.n_dma_sems = n_dma_sems
        self.epoch_cap = epoch_cap
        self.eng = {"pe": nc.tensor, "act": nc.scalar, "dve": nc.vector,
                    "pool": nc.gpsimd, "sp": nc.sync}

    def add(self, eng, fn, reads=(), writes=(), dma=False):
        i = len(self.ops)
        deps = set()
        for k in reads:
            w = self.last_w.get(k)
            if w is not None:
                deps.add(w)
        for k in writes:
            w = self.last_w.get(k)
            if w is not None:
                deps.add(w)
            for r in self.readers.get(k, ()):
                deps.add(r)
        for k in reads:
            self.readers.setdefault(k, []).append(i)
        for k in writes:
            self.last_w[k] = i
            self.readers[k] = []
        deps.discard(i)
        self.ops.append(dict(eng=eng, fn=fn, deps=deps, dma=dma, sig=False))
        return i

    def emit(self, sem_ctx):
        ops = self.ops

        def skip(p, o):
            return (not p["dma"]) and (not o["dma"]) and p["eng"] == "pe" and o["eng"] == "pe"

        for o in ops:
            for d in o["deps"]:
                p = ops[d]
                if p["dma"] or skip(p, o):
                    continue
                p["sig"] = True
        cnt = {e: 0 for e in self.COMPUTE}
        epoch = {e: 0 for e in self.COMPUTE}
        sems = {}

        def get_sem(name):
            if name not in sems:
                sems[name] = sem_ctx(name)
            return sems[name]

        for o in ops:
            if o["dma"] or o["eng"] not in self.COMPUTE:
                continue
            e = o["eng"]
            if o["sig"]:
                if cnt[e] >= self.epoch_cap:
                    epoch[e] += 1
                    cnt[e] = 0
                cnt[e] += 1
                o["semname"] = f"s_{e}_{epoch[e]}"
                o["semval"] = cnt[e]
        dcount, dlast, dlast_idx = {}, {}, {}
        for _i, _o in enumerate(ops):
            _o["_i"] = _i
        for o in ops:
            if not o["dma"]:
                continue
            q = o["eng"]
            n = dcount.get(q, 0)
            dcount[q] = n + 1
            name = f"d_{q}_{n % self.n_dma_sems}"
            prev = dlast.get(name, 0)
            o["semname"], o["semval"], o["prev_val"] = name, prev + 16, prev
            o["prev_idx"] = dlast_idx.get(name, -1)
            dlast[name] = prev + 16
            dlast_idx[name] = ops.index(o) if False else o["_i"]
        eclock = {e: {} for e in self.eng}
        oclock = {}
        for idx, o in enumerate(ops):
            e = o["eng"]
            E = self.eng[e]
            ck = eclock[e]
            need = {}
            for d in o["deps"]:
                p = ops[d]
                if skip(p, o):
                    continue
                cur = need.get(p["semname"])
                if cur is None or p["semval"] > cur[0]:
                    need[p["semname"]] = (p["semval"], d)
            if o["dma"] and o["prev_val"] > 0:
                cur = need.get(o["semname"])
                if cur is None or o["prev_val"] > cur[0]:
                    need[o["semname"]] = (o["prev_val"], o.get("prev_idx", -1))
            for sn, (v, pidx) in sorted(need.items(), key=lambda kv: -kv[1][1]):
                if ck.get(sn, 0) >= v:
                    continue
                E.wait_ge(get_sem(sn), v)
                pc_ = oclock.get(pidx)
                if pc_:
                    for k2, v2 in pc_.items():
                        if ck.get(k2, 0) < v2:
                            ck[k2] = v2
                if ck.get(sn, 0) < v:
                    ck[sn] = v
            if o["fn"] is None:
                continue
            ins = o["fn"](E)
            if o["dma"]:
                ins.then_inc(get_sem(o["semname"]), 16)
                c2 = dict(ck)
                c2[o["semname"]] = o["semval"]
                oclock[idx] = c2
            elif o["sig"]:
                ins.then_inc(get_sem(o["semname"]), 1)
                c2 = dict(ck)
                c2[o["semname"]] = o["semval"]
                oclock[idx] = c2


def _consts():
    c = {}
    p = np.arange(128)
    col = np.arange(64)
    s = (p % 64)[:, None]
    t = col[None, :]
    c["ident"] = np.eye(128, dtype=np.float32)
    c["m_su"] = (t > s).astype(np.float32)
    c["m_u"] = (t >= s).astype(np.float32)
    c["m_sl"] = (t < s).astype(np.float32)
    same = ((s // 4) == (t // 4)).astype(np.float32)
    c["s_su"] = c["m_su"] * same
    c["s_u"] = c["m_u"] * same
    c["s_sl"] = c["m_sl"] * same
    c["eq"] = (t == s).astype(np.float32)
    c["bones"] = ((p[:, None] // 64) == (np.arange(128)[None, :] // 64)).astype(np.float32)
    c["odiv"] = np.full((128, 128), 1.0 / 1024.0, np.float32)
    sm = np.ones((128, 576), np.float32)
    sm[:, 0:512:64] = 0.0
    sm[:, 512:576:4] = 0.0
    c["scanm"] = sm
    qm = np.zeros((128, 16, 64), np.float32)
    for j in range(16):
        qm[:, j, 4 * j:4 * j + 4] = 1.0
    c["qmask"] = qm.reshape(128, 1024)
    rm = np.zeros((128, 16), np.float32)
    for r in range(64):
        rm[r, r // 4] = 1.0
    c["rowm"] = rm
    names = list(c.keys())
    offs, o = {}, 0
    for n in names:
        offs[n] = (o, c[n].shape[1])
        o += c[n].shape[1]
    return np.ascontiguousarray(np.concatenate([c[n] for n in names], axis=1)), offs


_CST, _COFF = _consts()
_PCN = ["conv_b", "conv_ln_g", "conv_ln_b", "decay_base", "iclr_base", "k_k", "k_a", "r_k", "lnx_g", "lnx_b"]
PC_MU = 80
PC_CW = 106
PC_N = 106 + 248


def _fm(v):
    return np.ascontiguousarray(np.asarray(v, np.float32).reshape(8, 128).T)


def _pack_pcol(inp):
    cols = [_fm(inp[n][0].reshape(-1)) for n in _PCN]
    mu = np.asarray(inp["shift_mu"][0], np.float32).reshape(26, 128).T
    cw = np.asarray(inp["conv_w"][0], np.float32)
    cwf = cw.reshape(31, 8, 128).transpose(2, 1, 0).reshape(128, 248)
    return np.ascontiguousarray(np.concatenate(cols + [mu, cwf], axis=1))


def build_nc(cfg=None):
    cfg = cfg or {}
    npass = cfg.get("npass", NPASS)
    do_rwkv = cfg.get("rwkv", True)
    taps = cfg.get("taps", ())
    nc = bass.Bass("TRN2", target_bir_lowering=False)
    dram = lambda n, s, kind="ExternalInput": nc.dram_tensor(n, list(s), F32, kind=kind).ap()
    x_p = dram("x_p", [SEQ, D]); x_s = dram("x_s", [NS, D])
    sconv = dram("sconv", [NSEQ, 30, D]); sshift = dram("sshift", [NSEQ, D]); swkv = dram("swkv", [NSEQ, 16, 64, 64])
    gains = dram("gains", [4, D]); pcol_d = dram("pcol", [128, PC_N]); cst_d = dram("cst", [128, _CST.shape[1]])
    w_in = dram("w_in", [D, NIN]); w_co = dram("w_co", [D, D]); w_ro = dram("w_ro", [D, D]); w_out = dram("w_out", [D, D])
    w_up = dram("w_up", [D, DFF]); w_dn = dram("w_dn", [DFF, D])
    w_dec = dram("w_dec", [64, D]); w_icl = dram("w_icl", [64, D]); w_g = dram("w_g", [128, D])
    OUT = "ExternalOutput"
    y_p = dram("y_p", [SEQ, D], OUT); y_s = dram("y_s", [NS, D], OUT)
    conv_p = dram("conv_p", [30, D], OUT); shift_p = dram("shift_p", [1, D], OUT); wkv_p = dram("wkv_p", [16, 64, 64], OUT)
    conv_s = dram("conv_s", [NSEQ, 30, D], OUT); shift_s = dram("shift_s", [NSEQ, D], OUT); wkv_s = dram("wkv_s", [NSEQ, 16, 64, 64], OUT)
    tapd = {}

    es = ExitStack()
    with es:
        S = Sched(nc, epoch_cap=cfg.get("epoch_cap", 4000))
        sbt = lambda n, s, d: es.enter_context(nc.sbuf_tensor(n, list(s), d))
        cstb = sbt("cstb_sb", [128, _CST.shape[1]], BF16)
        identf_t = sbt("identf_sb", [128, 128], F32)
        pcol = sbt("pcolt", [128, PC_N], F32)
        pneg = sbt("pneg", [128, 34], F32)
        gain = [sbt(f"gain{i}", [128, D], F32) for i in range(2)]
        wsm = sbt("wsm", [128, 3, D], BF16)
        wb = [sbt(f"wb{i}", [128, KC, 512], BF16) for i in range(3)]
        xs = [sbt(f"xs{i}", [128, D], F32) for i in range(2)]
        hb = sbt("hb", [128, D], BF16)
        junk = sbt("junk", [128, D], F32)
        st = sbt("st", [128, 64], F32)
        hT = sbt("hT", [128, KC, NCOL], BF16)
        merged = sbt("merged", [128, KC, NCOL], BF16)
        gbuf = sbt("gbuf", [128, KC, NCOL], BF16)
        uhist = sbt("uhist", [128, KC, 30], BF16)
        plast = sbt("plast", [128, 26], F32)
        diag = [sbt(f"diag{i}", [128, 128], BF16) for i in range(8)]
        Hst = [sbt(f"H32_{g}", [128, 4, 64], F32) for g in range(2)]
        Hbd = [[sbt(f"Hbd_{g}_{q_}", [128, 4, 128], BF16) for q_ in range(2)] for g in range(2)]
        SHN = cfg.get("shn", 56) * 1024
        SH = sbt("SH", [128, SHN], BF16)
        psb = [es.enter_context(nc.psum_tensor(f"ps{i}", [128, 512], F32)) for i in range(8)]

        def cs(name, dt=BF16):
            o, n = _COFF[name]
            return cstb[:, o:o + n]

        identf = identf_t[:]; identb = cs("ident", BF16)
        cst = SH[:, 0:2 * _CST.shape[1]].bitcast(F32)

        class Arena:
            def __init__(self):
                self.off = 0
            def reset(self):
                self.off = 0
            def take(self, nelem, dt):
                n16 = nelem * (2 if dt == F32 else 1)
                o = self.off
                self.off += (n16 + 1) // 2 * 2
                assert self.off <= SHN, ("arena overflow", self.off)
                v = SH[:, o:o + n16]
                return v.bitcast(F32) if dt == F32 else v
        AR = Arena()

        cur_stream = [None]

        def op(eng, fn, reads=(), writes=(), dma=False):
            r = [k for k in reads if k[0] != "ps"]
            w = list(writes) + [k for k in reads if k[0] == "ps"]
            if cur_stream[0] is not None:
                cur_stream[0].append((eng, fn, r, w, dma))
                return None
            return S.add(eng, fn, r, w, dma)

        def interleave(gens):
            streams = []
            for f, banks in gens:
                cur_stream[0] = []
                saved = (pspool[0], pscur[0])
                pspool[0], pscur[0] = banks, 0
                f()
                streams.append(cur_stream[0])
                pspool[0], pscur[0] = saved
                cur_stream[0] = None
            idx = [0] * len(streams)
            while any(idx[i] < len(streams[i]) for i in range(len(streams))):
                for i in range(len(streams)):
                    if idx[i] < len(streams[i]):
                        S.add(*streams[i][idx[i]])
                        idx[i] += 1

        def tap(name, view, shape, reads):
            if name not in taps:
                return
            t = dram("tap_" + name, shape, OUT)
            tapd[name] = t
            op("pool", lambda E: E.dma_start(out=t, in_=view), reads=list(reads) + [("SHE",)], writes=[("tap", name)], dma=True)

        pscur = [0]
        pspool = [list(range(8))]
        def ps_take():
            b = pspool[0][pscur[0] % len(pspool[0])]
            pscur[0] += 1
            return b
        PSK = lambda b: [("ps", b)]

        op("sp", lambda E: E.dma_start(out=cst, in_=cst_d[:, :]), writes=[("cst0",), ("SHE",)], dma=True)
        op("sp", lambda E: E.dma_start(out=pcol[:], in_=pcol_d[:, :]), writes=[("pcol",)], dma=True)
        op("dve", lambda E: E.tensor_copy(out=cstb[:], in_=cst), reads=[("cst0",)], writes=[("cstb",)])
        op("dve", lambda E: E.tensor_copy(out=identf_t[:], in_=cst[:, 0:128]), reads=[("cst0",)], writes=[("cst",)])
        op("dve", lambda E: E.tensor_scalar(out=pneg[:, 0:8], in0=pcol[:, 48:56], scalar1=-1.0, scalar2=1.0, op0=ALU.mult, op1=ALU.add),
           reads=[("pcol",)], writes=[("pneg",)])
        op("dve", lambda E: E.tensor_scalar(out=pneg[:, 8:34], in0=pcol[:, PC_MU:PC_MU + 26], scalar1=-1.0, scalar2=1.0, op0=ALU.mult, op1=ALU.add),
           reads=[("pcol",)], writes=[("pneg",)])
        op("pool", lambda E: E.memset(wsm[:], 0.0), writes=[("wsm",)])
        op("pool", lambda E: E.dma_start(out=wsm[0:64, 0, :], in_=w_dec[:, :]), reads=[("wsm",)], writes=[("wsm", 0)], dma=True)
        op("pool", lambda E: E.dma_start(out=wsm[64:128, 1, :], in_=w_icl[:, :]), reads=[("wsm",)], writes=[("wsm", 1)], dma=True)
        op("pool", lambda E: E.dma_start(out=wsm[:, 2, :], in_=w_g[:, :]), reads=[("wsm",)], writes=[("wsm", 2)], dma=True)
        op("pool", lambda E: E.memset(uhist[:], 0.0), writes=[("uhist",)])
        op("pool", lambda E: E.memset(plast[:], 0.0), writes=[("plast",)])
        for g in range(2):
            op("pool", lambda E, g=g: E.memset(Hst[g][:], 0.0), writes=[("H32", g, 0), ("H32", g, 1)])
            for q_ in range(2):
                op("pool", lambda E, g=g, q_=q_: E.memset(Hbd[g][q_][:], 0.0), writes=[("Hbd", g, q_, 0), ("Hbd", g, q_, 1)])
        NZ = 32
        for zi in range(NZ):
            z0, z1 = zi * SHN // NZ, (zi + 1) * SHN // NZ
            op("pool", lambda E, z0=z0, z1=z1: E.memset(SH[:, z0:z1], 0.0), reads=[("SHE",)] if zi == 0 else [], writes=[("SHE",), ("cst0",)])

        pc = lambda pi, kc: pcol[:, pi * 8 + kc: pi * 8 + kc + 1]
        PI = {n: i for i, n in enumerate(_PCN)}

        def load_gain(slot, gi):
            op("sp", lambda E: E.dma_start(out=gain[slot][:], in_=gains[gi:gi + 1, :].partition_broadcast(128)),
               writes=[("gain", slot)], dma=True)

        wcur = [0]
        def wload(dview, ncols=512):
            s = wcur[0] % 3
            wcur[0] += 1
            op("pool", lambda E: E.dma_start(out=wb[s][:, :, 0:ncols], in_=dview), writes=[("wb", s)], dma=True)
            return s

        def wload_v(dview):
            s_ = wcur[0] % 3
            wcur[0] += 1
            if dview.shape[1] == 32:
                outv = wb[s_][:].rearrange("p k c -> p (k c)").rearrange("p (f c) -> p f c", f=32)
            else:
                outv = wb[s_][:, :, 0:dview.shape[2]]
            op("pool", lambda E: E.dma_start(out=outv, in_=dview), writes=[("wb", s_)], dma=True)
            return s_

        def all_weight_views():
            v = []
            for p_ in range(npass):
                if do_rwkv:
                    v.append(wview(w_in, 2048 + 3072, 256))
                    for hg in range(2):
                        for kind in range(3):
                            v.append(wview(w_in, 2048 + kind * 1024 + hg * 512, 512))
                    v += [wview(w_in, 6400, 512), wview(w_in, 6400 + 512, 512), wview(w_ro, 0, 512), wview(w_ro, 512, 512)]
                v += [wview(w_in, 1024, 512), wview(w_in, 0, 512), wview(w_in, 1024 + 512, 512), wview(w_in, 512, 512)]
                v += [wview(w_in, 5376, 512), wview(w_in, 5376 + 512, 512), wview(w_co, 0, 512), wview(w_co, 512, 512)]
                v += [wview(w_out, 0, 512), wview(w_out, 512, 512)]
                v += [wview(w_up, g * 512, 512) for g in range(8)]
                v += [w_dn.rearrange("(f p) n -> p f n", p=128)[:, :, oc * 128:(oc + 1) * 128] for oc in range(KC)]
            return v

        wq = dict(views=None, issued=0, taken=0, slots={})
        def wnext(ahead=2):
            if wq["views"] is None:
                wq["views"] = all_weight_views()
            vs = wq["views"]
            while wq["issued"] < min(len(vs), wq["taken"] + 1 + ahead):
                i = wq["issued"]
                wq["slots"][i] = wload_v(vs[i])
                wq["issued"] += 1
            sl = wq["slots"].pop(wq["taken"])
            wq["taken"] += 1
            return sl

        def wstream(views, ahead=2):
            n = len(views)
            slots = {}
            for i in range(min(ahead, n)):
                slots[i] = wload(views[i])
            for i in range(n):
                if i + ahead < n:
                    slots[i + ahead] = wload(views[i + ahead])
                yield i, slots[i]

        def wview(W, c0, n):
            return W.rearrange("(kc p) n -> p kc n", p=128)[:, :, c0:c0 + n]

        def phase_barrier():
            op("pool", lambda E: E.memset(st[:, 60:64], 0.0), reads=[], writes=[("SHE",)])
        SHR = [("SHE",)]

        def rstd_from_ssq(ssq_ap, out_ap, n, eps, rk, wk, rows=128):
            op("act", lambda E: E.activation(out=out_ap, in_=ssq_ap, func=AF.Sqrt, bias=eps, scale=1.0 / n), reads=rk, writes=wk)
            op("dve", lambda E: E.reciprocal(out=out_ap, in_=out_ap), reads=wk, writes=wk)


        GN_EPS = 64e-5

        def rwkv_phase(p, P0, LAST, nblk, segs, segs_r, HTK, ws_mm):
            phase_barrier()
            AR.reset()
            n = 576 if P0 else 512
            nch = 9 if P0 else 8
            t3 = lambda k, c, dt: AR.take(k * c, dt).rearrange("p (k c) -> p k c", k=k)
            loraA = AR.take(592, BF16); sgl = AR.take(592, BF16)
            zb = t3(KC, 576, BF16)
            QR = AR.take(4 * 9 * 128, BF16).rearrange("p (k c q) -> p k c q", k=4, c=9)
            RKV = [t3(4, 592, BF16) for _ in range(3)]
            bon = t3(4, 576, BF16); gT = t3(4, 576, BF16)
            WC = t3(4, 9, F32); WCs = t3(4, 16, F32)
            w_full = [AR.take(1184, F32) for _ in range(6)]
            w = w_full
            sqb2 = [AR.take(576, BF16) for _ in range(2)]; rkb2 = [AR.take(576, BF16) for _ in range(2)]
            Ktok = AR.take(512, BF16); Btok = AR.take(512, BF16)
            Vc = [AR.take(512, BF16) for _ in range(2)]
            h8 = lambda t: t.rearrange("p (h v) -> p h v", h=8)
            LkT = [AR.take(512, BF16) for _ in range(2)]; MkT = [AR.take(512, BF16) for _ in range(2)]; MbT = [AR.take(512, BF16) for _ in range(2)]
            _nn = [AR.take(512, BF16) for _ in range(2)]; NN = [_nn, _nn]
            _aa = [AR.take(512, BF16) for _ in range(2)]; AA = [_aa, _aa]
            _z0 = AR.take(512, BF16); ZZ = [[_z0, AR.take(512, BF16)], [_z0, AR.take(512, BF16)]]
            Xs = [AR.take(512, BF16) for _ in range(2)]; Us = [AR.take(512, BF16) for _ in range(2)]
            ysq = AR.take(512, F32); yc = ysq; yh = AR.take(512, BF16)
            zt = AR.take(512, F32).rearrange("p (k c) -> p k c", k=4)
            Khs = t3(4, 64, BF16); Bhs = t3(4, 64, BF16)
            Khtok = AR.take(512, BF16); Bhtok = AR.take(512, BF16)
            H0bd = AR.take(2048, BF16).rearrange("p (j c) -> p j c", j=16)
            KtT = gbuf[:, 0:4, :]; BtT = gbuf[:, 4:8, :]
            a1, mixo = w[4], w[5]
            mu = lambda rc: pcol[:, PC_MU + rc: PC_MU + rc + 1]
            omu = lambda rc: pneg[:, 8 + rc: 9 + rc]
            WK = lambda i: [("w", i)]
            bc_mid = lambda ap, nmid: ap.unsqueeze(1).to_broadcast([ap.shape[0], nmid, ap.shape[1]])
            bc_last = lambda ap, nl: ap.unsqueeze(2).to_broadcast([ap.shape[0], ap.shape[1], nl])
            c64 = lambda ap: ap.rearrange("p (c q) -> p c q", q=64)
            s4 = lambda ap: ap.rearrange("p (s t) -> p s t", t=4)
            pairv = lambda t, kl: t[:, kl * 128:(kl + 1) * 128]
            bones = cs("bones"); eqm = cs("eq")
            GK = [("gbuf", k) for k in range(8)]
            QK = [("QR", k) for k in range(4)]
            for i, tz in enumerate(LkT + MkT + MbT + [ZZ[0][1], ZZ[1][1]] + Xs + Us + Vc + [Khtok, Bhtok]):
                op("pool", lambda E, tz=tz: E.memset(tz, 0.0), reads=SHR, writes=[("zp", i)])
            if P0:
                op("pool", lambda E: E.memset(H0bd, 0.0), reads=SHR, writes=[("H0bd",)])
            ZP = [("zp", i) for i in range(18)]

            def mix(rc, pvs, dst, dkey):
                for (pv, pb, c0, c1) in pvs:
                    if c0 == 0:
                        op("act", lambda E, pv=pv: E.activation(out=a1[:, 0:512], in_=pv, func=AF.Identity, scale=omu(rc)), reads=PSK(pb) + [("pneg",)] + SHR, writes=WK(4))
                        op("dve", lambda E, pv=pv: E.scalar_tensor_tensor(out=dst[:, 1:512], in0=pv[:, 0:511], scalar=mu(rc), in1=a1[:, 1:512], op0=ALU.mult, op1=ALU.add),
                           reads=PSK(pb) + WK(4) + [("pcol",)] + SHR, writes=[dkey])
                        op("dve", lambda E: E.scalar_tensor_tensor(out=dst[:, 0:1], in0=plast[:, rc:rc + 1], scalar=mu(rc), in1=a1[:, 0:1], op0=ALU.mult, op1=ALU.add),
                           reads=WK(4) + [("plast", rc), ("plast",), ("pcol",)] + SHR, writes=[dkey])
                        op("act", lambda E, pv=pv: E.copy(out=plast[:, rc:rc + 1], in_=pv[:, 511:512]), reads=PSK(pb) + [("plast",)], writes=[("plast", rc)])
                    else:
                        op("act", lambda E, pv=pv: E.activation(out=a1[:, 512:576], in_=pv[:, 0:64], func=AF.Identity, scale=omu(rc)), reads=PSK(pb) + [("pneg",)] + SHR, writes=WK(4))
                        d3, a3, p3 = s4(dst[:, 512:576]), s4(a1[:, 512:576]), s4(pv[:, 0:64])
                        op("dve", lambda E, d3=d3, a3=a3, p3=p3: E.scalar_tensor_tensor(out=d3[:, :, 1:4], in0=p3[:, :, 0:3], scalar=mu(rc), in1=a3[:, :, 1:4], op0=ALU.mult, op1=ALU.add),
                           reads=PSK(pb) + WK(4) + [("pcol",)] + SHR, writes=[dkey])
                        op("dve", lambda E, d3=d3, a3=a3, pv=pv: E.scalar_tensor_tensor(out=d3[:, :, 0], in0=pv[:, 64:80], scalar=mu(rc), in1=a3[:, :, 0], op0=ALU.mult, op1=ALU.add),
                           reads=PSK(pb) + WK(4) + [("pcol",)] + SHR, writes=[dkey])

            sw = wnext()
            def ev_lora(oc, pvs):
                mix(24 + oc, pvs, mixo, ("w", 5))
                if oc == 0:
                    op("act", lambda E: E.activation(out=loraA[0:64, 0:n], in_=mixo[0:64, 0:n], func=AF.Tanh), reads=WK(5) + SHR, writes=[("loraA", 0)])
                    op("act", lambda E: E.copy(out=loraA[64:128, 0:n], in_=mixo[64:128, 0:n]), reads=WK(5) + SHR, writes=[("loraA", 1)])
                else:
                    op("act", lambda E: E.activation(out=sgl[:, 0:n], in_=mixo[:, 0:n], func=AF.Sigmoid), reads=WK(5) + SHR, writes=[("sgl",)])
            ws_mm(sw, 2, hT, HTK, segs_r, KC, ev_lora)

            def elem(hg, kl, sid=0):
                kc = hg * 4 + kl
                w = [t[:, sid * 592:(sid + 1) * 592] for t in w_full]
                WK = lambda i: [("w", i) if sid == 0 else ("wB", i)]
                sqb, rkb = sqb2[sid], rkb2[sid]
                SQK, RKK = [("sqb", sid)], [("rkb", sid)]
                Rr, Kk, Vv = RKV[0][:, kl, :], RKV[1][:, kl, :], RKV[2][:, kl, :]
                RK = lambda i: [("rkv", i, kl)]
                pz = [None, None, None]
                for qi, (lh, rh, rkey) in enumerate([(wsm[:, 0, kc * 128:(kc + 1) * 128], loraA, [("loraA", 0), ("loraA", 1), ("wsm", 0), ("wsm",)]),
                                                     (wsm[:, 1, kc * 128:(kc + 1) * 128], loraA, [("loraA", 0), ("loraA", 1), ("wsm", 1), ("wsm",)]),
                                                     (wsm[:, 2, kc * 128:(kc + 1) * 128], sgl, [("sgl",), ("wsm", 2), ("wsm",)])]):
                    pz[qi] = [ps_take() for _ in segs]
                    for si, (c0, c1) in enumerate(segs):
                        op("pe", lambda E, lh=lh, rh=rh, c0=c0, c1=c1, pb=pz[qi][si]: E.matmul(psb[pb][:, 0:c1 - c0], lhsT=lh, rhs=rh[:, c0:c1], start=True, stop=True),
                           reads=rkey + SHR, writes=PSK(pz[qi][si]))
                    for si, (c0, c1) in enumerate(segs):
                        if qi == 0:
                            op("act", lambda E, c0=c0, c1=c1, pb=pz[0][si]: E.activation(out=w[0][:, c0:c1], in_=psb[pb][:, 0:c1 - c0], func=AF.Sigmoid, bias=pc(PI["decay_base"], kc), scale=1.0),
                               reads=PSK(pz[0][si]) + [("pcol",)] + SHR, writes=WK(0))
                        elif qi == 1:
                            op("act", lambda E, c0=c0, c1=c1, pb=pz[1][si]: E.activation(out=w[1][:, c0:c1], in_=psb[pb][:, 0:c1 - c0], func=AF.Sigmoid, bias=pc(PI["iclr_base"], kc), scale=1.0),
                               reads=PSK(pz[1][si]) + [("pcol",)] + SHR, writes=WK(1))
                        else:
                            op("act", lambda E, c0=c0, c1=c1, pb=pz[2][si]: E.copy(out=gT[:, kl, c0:c1], in_=psb[pb][:, 0:c1 - c0]), reads=PSK(pz[2][si]) + SHR, writes=[("gT", kl)])
                op("dve", lambda E: E.tensor_tensor_scan(out=w[2][:, 0:n], data0=cs("scanm")[:, 0:n], data1=w[0][:, 0:n], initial=0.0, op0=ALU.mult, op1=ALU.add),
                   reads=WK(0) + [("cstb",)] + SHR, writes=WK(2))
                op("pool", lambda E: E.tensor_tensor(out=w[3][:, 0:n], in0=w[2][:, 0:n], in1=w[0][:, 0:n], op=ALU.subtract), reads=WK(2) + WK(0) + SHR, writes=WK(3))
                op("act", lambda E: E.activation(out=w[3][:, 0:n], in_=w[3][:, 0:n], func=AF.Exp, scale=-C0), reads=WK(3) + SHR, writes=WK(3))
                op("act", lambda E: E.activation(out=w[4][:, 0:n], in_=w[2][:, 0:n], func=AF.Exp, scale=C0), reads=WK(2) + SHR, writes=WK(4))
                op("act", lambda E: E.activation(out=w[2][:, 0:n], in_=w[2][:, 0:n], func=AF.Exp, scale=-C0), reads=WK(2) + WK(4) + SHR, writes=WK(2))
                op("pool", lambda E: E.tensor_copy(out=WC[:, kl, 0:8], in_=w[2][:, 63:512:64]), reads=WK(2) + SHR, writes=[("WC", kl)])
                if P0:
                    op("pool", lambda E: E.tensor_copy(out=WCs[:, kl, :], in_=w[2][:, 515:576:4]), reads=WK(2) + SHR, writes=[("WCs", kl)])
                op("dve", lambda E: E.tensor_scalar(out=w[5][:, 0:n], in0=Kk[:, 0:n], scalar1=pc(PI["k_k"], kc), scalar2=None, op0=ALU.mult), reads=RK(1) + [("pcol",)] + SHR, writes=WK(5))
                op("act", lambda E: E.activation(out=sqb[:, 0:n], in_=w[5][:, 0:n], func=AF.Square), reads=WK(5) + SHR, writes=SQK)
                pss = [ps_take() for _ in segs]
                for si, (c0, c1) in enumerate(segs):
                    op("pe", lambda E, c0=c0, c1=c1, pb=pss[si]: E.matmul(psb[pb][:, 0:c1 - c0], lhsT=bones, rhs=sqb[:, c0:c1], start=True, stop=True), reads=SQK + [("cstb",)] + SHR, writes=PSK(pss[si]))
                    op("act", lambda E, c0=c0, c1=c1, pb=pss[si]: E.activation(out=w[0][:, c0:c1], in_=psb[pb][:, 0:c1 - c0], func=AF.Sqrt, bias=1e-24, scale=1.0), reads=PSK(pss[si]) + WK(3) + SHR, writes=WK(0))
                op("dve", lambda E: E.reciprocal(out=w[0][:, 0:n], in_=w[0][:, 0:n]), reads=WK(0) + SHR, writes=WK(0))
                op("dve", lambda E: E.tensor_tensor(out=w[5][:, 0:n], in0=w[5][:, 0:n], in1=w[0][:, 0:n], op=ALU.mult), reads=WK(5) + WK(0) + SHR, writes=WK(5))
                op("dve", lambda E: E.tensor_scalar(out=w[0][:, 0:n], in0=w[1][:, 0:n], scalar1=pc(PI["k_a"], kc), scalar2=pneg[:, kc:kc + 1], op0=ALU.mult, op1=ALU.add),
                   reads=WK(1) + WK(0) + [("pcol",), ("pneg",)] + SHR, writes=WK(0))
                op("dve", lambda E: E.tensor_tensor(out=w[0][:, 0:n], in0=Kk[:, 0:n], in1=w[0][:, 0:n], op=ALU.mult), reads=RK(1) + WK(0) + SHR, writes=WK(0))
                op("dve", lambda E: E.tensor_tensor(out=QR[:, kl, 0:nch, 0:64], in0=c64(w[5][:, 0:n]), in1=c64(w[3][:, 0:n]), op=ALU.mult), reads=WK(5) + WK(3) + SHR, writes=[("QR", kl)])
                op("dve", lambda E: E.tensor_tensor(out=w[5][:, 0:n], in0=w[5][:, 0:n], in1=w[1][:, 0:n], op=ALU.mult), reads=WK(5) + WK(1) + [("QR", kl)] + SHR, writes=WK(5))
                op("dve", lambda E: E.scalar_tensor_tensor(out=BtT[:, kl, 0:n], in0=w[5][:, 0:n], scalar=-1.0, in1=w[4][:, 0:n], op0=ALU.mult, op1=ALU.mult), reads=WK(5) + WK(4) + SHR, writes=[("gbuf", 4 + kl)])
                op("pool", lambda E: E.tensor_tensor(out=KtT[:, kl, 0:n], in0=w[0][:, 0:n], in1=w[4][:, 0:n], op=ALU.mult), reads=WK(0) + WK(4) + SHR, writes=[("gbuf", kl)])
                op("pool", lambda E: E.tensor_tensor(out=QR[:, kl, 0:nch, 64:128], in0=c64(Rr[:, 0:n]), in1=c64(w[2][:, 0:n]), op=ALU.mult), reads=RK(0) + WK(2) + SHR, writes=[("QR", kl)])
                op("dve", lambda E: E.scalar_tensor_tensor(out=rkb[:, 0:n], in0=Rr[:, 0:n], scalar=pc(PI["r_k"], kc), in1=w[0][:, 0:n], op0=ALU.mult, op1=ALU.mult), reads=RK(0) + WK(0) + [("pcol",)] + SHR, writes=RKK)
                psq = [ps_take() for _ in segs]
                for si, (c0, c1) in enumerate(segs):
                    op("pe", lambda E, c0=c0, c1=c1, pb=psq[si]: E.matmul(psb[pb][:, 0:c1 - c0], lhsT=bones, rhs=rkb[:, c0:c1], start=True, stop=True), reads=RKK + [("cstb",)] + SHR, writes=PSK(psq[si]))
                    op("dve", lambda E, c0=c0, c1=c1, pb=psq[si]: E.tensor_tensor(out=bon[:, kl, c0:c1], in0=psb[pb][:, 0:c1 - c0], in1=Vv[:, c0:c1], op=ALU.mult), reads=PSK(psq[si]) + RK(2) + SHR, writes=[("bon", kl)])
                if P0:
                    op("dve", lambda E: E.tensor_tensor(out=s4(Khs[:, kl, :]), in0=s4(KtT[:, kl, 512:576]), in1=bc_last(WCs[:, kl, :], 4), op=ALU.mult), reads=[("gbuf", kl), ("WCs", kl)] + SHR, writes=[("Khs", kl)])
                    op("dve", lambda E: E.tensor_tensor(out=s4(Bhs[:, kl, :]), in0=s4(BtT[:, kl, 512:576]), in1=bc_last(WCs[:, kl, :], 4), op=ALU.mult), reads=[("gbuf", 4 + kl), ("WCs", kl)] + SHR, writes=[("Bhs", kl)])

            def ypost(pyc, rs, h0, nh):
                hsl = slice(h0 * 64, (h0 + nh) * 64)
                y3 = psb[pyc][rs, hsl].rearrange("p (h v) -> p h v", h=nh)
                v3 = lambda t: t[rs, hsl].rearrange("p (h v) -> p h v", h=nh)
                sa, sb_, sc_ = st[rs, 16 + h0:16 + h0 + nh], st[rs, 24 + h0:24 + h0 + nh], st[rs, 32 + h0:32 + h0 + nh]
                if cfg.get('yp_stop', 99) < 1:
                    return
                op("act", lambda E: E.activation(out=ysq[rs, hsl], in_=psb[pyc][rs, hsl], func=AF.Square), reads=PSK(pyc) + SHR, writes=[("ysq",)])
                if cfg.get('yp_stop', 99) < 2:
                    return
                op("dve", lambda E: E.tensor_reduce(out=sa, in_=y3, axis=AX.X, op=ALU.add), reads=PSK(pyc), writes=[("st", 16)])
                if cfg.get('yp_stop', 99) < 3:
                    return
                op("dve", lambda E: E.tensor_reduce(out=sb_, in_=v3(ysq), axis=AX.X, op=ALU.add), reads=[("ysq",)] + SHR, writes=[("st", 24)])
                if cfg.get('yp_stop', 99) < 4:
                    return
                op("dve", lambda E: E.tensor_scalar(out=sa, in0=sa, scalar1=1.0 / 64, scalar2=None, op0=ALU.mult), reads=[("st", 16)], writes=[("st", 16)])
                if cfg.get('yp_stop', 99) < 5:
                    return
                op("dve", lambda E: E.tensor_tensor(out=sc_, in0=sa, in1=sa, op=ALU.mult), reads=[("st", 16)], writes=[("st", 32)])
                if cfg.get('yp_stop', 99) < 6:
                    return
                op("dve", lambda E: E.scalar_tensor_tensor(out=sb_, in0=sb_, scalar=1.0 / 64, in1=sc_, op0=ALU.mult, op1=ALU.subtract), reads=[("st", 24), ("st", 32)], writes=[("st", 24)])
                if cfg.get('yp_stop', 99) < 7:
                    return
                op("act", lambda E: E.activation(out=sb_, in_=sb_, func=AF.Sqrt, bias=GN_EPS, scale=1.0), reads=[("st", 24)], writes=[("st", 24)])
                if cfg.get('yp_stop', 99) < 8:
                    return
                op("dve", lambda E: E.reciprocal(out=sb_, in_=sb_), reads=[("st", 24)], writes=[("st", 24)])
                if cfg.get('yp_stop', 99) < 9:
                    return
                op("dve", lambda E: E.tensor_tensor(out=v3(yc), in0=y3, in1=bc_last(sa, 64), op=ALU.subtract), reads=PSK(pyc) + [("st", 16)] + SHR, writes=[("ysq",)])
                if cfg.get('yp_stop', 99) < 10:
                    return
                op("dve", lambda E: E.tensor_tensor(out=v3(yh), in0=v3(yc), in1=bc_last(sb_, 64), op=ALU.mult), reads=[("ysq",), ("st", 24)] + SHR, writes=[("yh",), ("yh", 0), ("yh", 1)])

            def scores_inverse(hg, b, c, SAMP):
                rs = slice(c * 64, c * 64 + 64)
                ci = 8 if SAMP else 2 * b + c
                cc = b * 128 + c * 64
                msu, mu_, msl = (cs("s_su"), cs("s_u"), cs("s_sl")) if SAMP else (cs("m_su"), cs("m_u"), cs("m_sl"))
                pA = [None, None]; pBq = [None, None]; pC = [None, None]
                for hp in range(2):
                    hs = slice(hp * 64, hp * 64 + 64)
                    pA[hp], pBq[hp], pC[hp] = ps_take(), ps_take(), ps_take()
                    for kl in range(4):
                        op("pe", lambda E, kl=kl, hp=hp, hs=hs: E.matmul(psb[pA[hp]][rs, kl * 128:(kl + 1) * 128], lhsT=KtT[hs, kl, cc:cc + 64], rhs=QR[hs, kl, ci, :], start=True, stop=True), reads=GK + QK + SHR, writes=PSK(pA[hp]))
                        op("pe", lambda E, kl=kl, hp=hp, hs=hs: E.matmul(psb[pBq[hp]][rs, kl * 128:(kl + 1) * 128], lhsT=BtT[hs, kl, cc:cc + 64], rhs=QR[hs, kl, ci, :], start=True, stop=True), reads=GK + QK + SHR, writes=PSK(pBq[hp]))
                        op("pe", lambda E, kl=kl, hp=hp, hs=hs: E.matmul(psb[pC[hp]][rs, kl * 64:(kl + 1) * 64], lhsT=QR[hs, kl, ci, 0:64], rhs=BtT[hs, kl, cc:cc + 64], start=True, stop=True), reads=GK + QK + SHR, writes=PSK(pC[hp]))
                    h4 = lambda t, hp=hp: t[rs, :].rearrange("p (k h v) -> p k h v", k=4, h=2)[:, :, hp, :]
                    stA, stB, stC = Xs[c], Us[c], yh
                    op("act", lambda E, hp=hp: E.copy(out=stA[rs, :], in_=psb[pA[hp]][rs, :]), reads=PSK(pA[hp]) + ZP + SHR, writes=[("Xs", c)])
                    op("act", lambda E, hp=hp: E.copy(out=stB[rs, :], in_=psb[pBq[hp]][rs, :]), reads=PSK(pBq[hp]) + ZP + SHR, writes=[("Us", c)])
                    op("act", lambda E, hp=hp: E.copy(out=stC[rs, 0:256], in_=psb[pC[hp]][rs, 0:256]), reads=PSK(pC[hp]) + SHR, writes=[("yh", c)])
                    pa3 = stA[rs, :].rearrange("p (k q) -> p k q", k=4)
                    pb3 = stB[rs, :].rearrange("p (k q) -> p k q", k=4)
                    pc3 = stC[rs, 0:256].rearrange("p (k q) -> p k q", k=4)
                    op("pool", lambda E, h4=h4, pa3=pa3: E.tensor_tensor(out=h4(LkT[c]), in0=pa3[:, :, 0:64], in1=bc_mid(msu[rs, :], 4), op=ALU.mult), reads=[("Xs", c), ("cstb",)] + ZP + SHR, writes=[("LkT", c)])
                    op("pool", lambda E, h4=h4, pa3=pa3: E.tensor_tensor(out=h4(MkT[c]), in0=pa3[:, :, 64:128], in1=bc_mid(mu_[rs, :], 4), op=ALU.mult), reads=[("Xs", c), ("cstb",)] + ZP + SHR, writes=[("MkT", c)])
                    op("dve", lambda E, h4=h4, pb3=pb3: E.tensor_tensor(out=h4(NN[c][0]), in0=pb3[:, :, 0:64], in1=bc_mid(msu[rs, :], 4), op=ALU.mult), reads=[("Us", c), ("cstb",)] + SHR, writes=[("NN", c, 0)])
                    op("pool", lambda E, h4=h4, pb3=pb3: E.tensor_tensor(out=h4(MbT[c]), in0=pb3[:, :, 64:128], in1=bc_mid(mu_[rs, :], 4), op=ALU.mult), reads=[("Us", c), ("cstb",)] + ZP + SHR, writes=[("MbT", c)])
                    op("dve", lambda E, h4=h4, pc3=pc3: E.tensor_tensor(out=h4(AA[c][0]), in0=pc3, in1=bc_mid(msl[rs, :], 4), op=ALU.mult), reads=[("yh", c), ("cstb",)] + SHR, writes=[("AA", c, 0)])
                op("pool", lambda E: E.tensor_tensor(out=h8(ZZ[c][0][rs, :]), in0=h8(NN[c][0][rs, :]), in1=bc_mid(eqm[rs, :], 8), op=ALU.add), reads=[("NN", c, 0), ("cstb",)] + SHR, writes=[("ZZ", c, 0)])
                nlev = 1 if SAMP else 5
                for st_ in range(nlev + 1):
                    s_, d_ = st_ % 2, (st_ + 1) % 2
                    do_sq = st_ < nlev
                    lastsq = (st_ == nlev - 1)
                    do_prod = st_ >= 1
                    if do_sq:
                        pa_ = ps_take()
                        for hi in range(8):
                            op("pe", lambda E, hi=hi, s_=s_, pa_=pa_: E.matmul(psb[pa_][rs, hi * 64:(hi + 1) * 64], lhsT=h8(NN[c][s_])[rs, hi, :], rhs=h8(AA[c][s_])[rs, hi, :], start=True, stop=True), reads=[("NN", c, s_), ("AA", c, s_)] + SHR, writes=PSK(pa_))
                        if not lastsq:
                            pn_ = ps_take()
                            for hi in range(8):
                                op("pe", lambda E, hi=hi, s_=s_, pn_=pn_: E.matmul(psb[pn_][rs, hi * 64:(hi + 1) * 64], lhsT=h8(AA[c][s_])[rs, hi, :], rhs=h8(NN[c][s_])[rs, hi, :], start=True, stop=True), reads=[("NN", c, s_), ("AA", c, s_)] + SHR, writes=PSK(pn_))
                    if do_prod:
                        zs, zd = (st_ - 1) % 2, st_ % 2
                        pz_ = ps_take()
                        for hi in range(8):
                            op("pe", lambda E, hi=hi, s_=s_, zs=zs, pz_=pz_: E.matmul(psb[pz_][rs, hi * 64:(hi + 1) * 64], lhsT=h8(AA[c][s_])[rs, hi, :], rhs=h8(ZZ[c][zs])[rs, hi, :], start=True, stop=True), reads=[("AA", c, s_), ("ZZ", c, zs)] + SHR, writes=PSK(pz_))
                    if do_sq:
                        op("act", lambda E, d_=d_, pa_=pa_: E.copy(out=AA[c][d_][rs, :], in_=psb[pa_][rs, :]), reads=PSK(pa_) + SHR, writes=[("AA", c, d_)])
                        if not lastsq:
                            op("act", lambda E, d_=d_, pn_=pn_: E.copy(out=NN[c][d_][rs, :], in_=psb[pn_][rs, :]), reads=PSK(pn_) + SHR, writes=[("NN", c, d_)])
                    if do_prod:
                        op("dve", lambda E, zs=zs, zd=zd, pz_=pz_: E.tensor_tensor(out=ZZ[c][zd][rs, :], in0=psb[pz_][rs, :], in1=ZZ[c][zs][rs, :], op=ALU.add), reads=PSK(pz_) + [("ZZ", c, zs)] + ZP + SHR, writes=[("ZZ", c, zd)])

            def chunk(hg, b, c):
                rs = slice(c * 64, c * 64 + 64)
                ci = 2 * b + c
                TT_ = ZZ[c][1]
                px, py, pu, ph = ps_take(), ps_take(), ps_take(), ps_take()
                pair = lambda pb, kl: pairv(psb[pb][rs, :], kl)
                for kl in range(4):
                    op("pe", lambda E, kl=kl: E.matmul(pair(px, kl), lhsT=QR[:, kl, ci, 0:64], rhs=Hbd[hg][c][:, kl, :], start=True, stop=False), reads=QK + [("Hbd", hg, c, 0), ("Hbd", hg, c, 1)] + SHR, writes=PSK(px))
                    for hi in (2 * kl, 2 * kl + 1):
                        op("pe", lambda E, hi=hi: E.matmul(psb[px][rs, hi * 64:(hi + 1) * 64], lhsT=h8(LkT[c])[:, hi, :], rhs=Vc[c][:, hi * 64:(hi + 1) * 64], start=False, stop=(hi % 2 == 1)), reads=[("LkT", c), ("Vc", c)] + ZP + SHR, writes=PSK(px))
                op("act", lambda E: E.copy(out=Xs[c][rs, :], in_=psb[px][rs, :]), reads=PSK(px) + ZP + SHR, writes=[("Xs", c)])
                if cfg.get("chain_stop", 9) <= 1:
                    return
                for hi in range(8):
                    op("pe", lambda E, hi=hi: E.matmul(psb[pu][rs, hi * 64:(hi + 1) * 64], lhsT=h8(TT_)[:, hi, :], rhs=Xs[c][:, hi * 64:(hi + 1) * 64], start=True, stop=True), reads=[("ZZ", c, 1), ("Xs", c)] + ZP + SHR, writes=PSK(pu))
                op("act", lambda E: E.copy(out=Us[c][rs, :], in_=psb[pu][rs, :]), reads=PSK(pu) + ZP + SHR, writes=[("Us", c)])
                if cfg.get("chain_stop", 9) <= 2:
                    return
                for kl in range(4):
                    op("pe", lambda E, kl=kl: E.matmul(psb[ph][:, kl * 128:(kl + 1) * 128], lhsT=Ktok[:, kl * 128:(kl + 1) * 128], rhs=pairv(Vc[c], kl), start=True, stop=False), reads=[("Ktok",), ("Vc", c)] + ZP + SHR, writes=PSK(ph))
                    op("pe", lambda E, kl=kl: E.matmul(psb[ph][:, kl * 128:(kl + 1) * 128], lhsT=Btok[:, kl * 128:(kl + 1) * 128], rhs=pairv(Us[c], kl), start=False, stop=True), reads=[("Btok",), ("Us", c)] + ZP + SHR, writes=PSK(ph))
                for hp in range(2):
                    hs = slice(hp * 64, hp * 64 + 64)
                    phv = psb[ph][hs, :].rearrange("p (k q) -> p k q", k=4)[:, :, hp * 64:(hp + 1) * 64]
                    op("dve", lambda E, hs=hs, phv=phv: E.tensor_tensor(out=Hst[hg][hs, :, :], in0=phv, in1=Hst[hg][hs, :, :], op=ALU.add), reads=PSK(ph) + [("H32", hg, hp)], writes=[("H32", hg, hp)])
                    op("dve", lambda E, hs=hs: E.tensor_tensor(out=Hst[hg][hs, :, :], in0=Hst[hg][hs, :, :], in1=bc_last(WC[hs, :, ci], 64), op=ALU.mult), reads=[("H32", hg, hp)] + [("WC", k) for k in range(4)] + SHR, writes=[("H32", hg, hp)])
                    op("act", lambda E, hs=hs, hp=hp: E.copy(out=Hbd[hg][1 - c][hs, :, hp * 64:(hp + 1) * 64], in_=Hst[hg][hs, :, :]), reads=[("H32", hg, hp)], writes=[("Hbd", hg, 1 - c, hp)])
                for kl in range(4):
                    op("pe", lambda E, kl=kl: E.matmul(pair(py, kl), lhsT=QR[:, kl, ci, 64:128], rhs=Hbd[hg][c][:, kl, :], start=True, stop=False), reads=QK + [("Hbd", hg, c, 0), ("Hbd", hg, c, 1)] + SHR, writes=PSK(py))
                    for hi in (2 * kl, 2 * kl + 1):
                        op("pe", lambda E, hi=hi: E.matmul(psb[py][rs, hi * 64:(hi + 1) * 64], lhsT=h8(MkT[c])[:, hi, :], rhs=Vc[c][:, hi * 64:(hi + 1) * 64], start=False, stop=False), reads=[("MkT", c), ("Vc", c)] + ZP + SHR, writes=PSK(py))
                    for hi in (2 * kl, 2 * kl + 1):
                        op("pe", lambda E, hi=hi: E.matmul(psb[py][rs, hi * 64:(hi + 1) * 64], lhsT=h8(MbT[c])[:, hi, :], rhs=Us[c][:, hi * 64:(hi + 1) * 64], start=False, stop=(hi % 2 == 1)), reads=[("MbT", c), ("Us", c)] + ZP + SHR, writes=PSK(py))
                ypost(py, rs, 0, 8)

            def sample_chain(hg):
                rs = slice(0, 64)
                c = 0
                WK = lambda i: [("w", i), ("wB", i)]
                scores_inverse(hg, 4, 0, True)
                TT_ = ZZ[0][1]
                qmk = cs("qmask").rearrange("p (j t) -> p j t", j=16)
                rowm = cs("rowm")
                w4b, w5b, w6b, w1b = w[4].bitcast(BF16), w[5].bitcast(BF16), w[3].bitcast(BF16), w[1].bitcast(BF16)
                Qm = w4b[:, 0:1024].rearrange("p (j t) -> p j t", j=16)
                Rm = w4b[:, 1024:2048].rearrange("p (j t) -> p j t", j=16)
                Vm = w5b[:, 0:2048].rearrange("p (j c) -> p j c", j=16)
                Um = w6b[:, 0:2048].rearrange("p (j c) -> p j c", j=16)
                Sn32 = w[0][:, 0:1024].rearrange("p (j k) -> p j k", j=16)
                Snb = w1b[:, 0:1024].rearrange("p (j k) -> p j k", j=16)
                So = w[2][:, 0:1024].rearrange("p (j k) -> p j k", j=16)

                def load_s0(kc_):
                    op("sp", lambda E: E.dma_start(out=Sn32, in_=swkv[:, 2 * kc_:2 * kc_ + 2, :, :].rearrange("s h v k -> (h v) s k")), reads=SHR, writes=WK(0), dma=True)

                def pairk(kl):
                    kc = hg * 4 + kl
                    op("dve", lambda E: E.tensor_tensor(out=Qm, in0=bc_mid(QR[:, kl, 8, 0:64], 16), in1=qmk, op=ALU.mult), reads=QK + [("cstb",)] + SHR, writes=WK(4))
                    op("dve", lambda E: E.tensor_tensor(out=Rm, in0=bc_mid(QR[:, kl, 8, 64:128], 16), in1=qmk, op=ALU.mult), reads=QK + [("cstb",)] + SHR, writes=WK(4))
                    v4 = lambda t: pairv(t, kl).unsqueeze(1).to_broadcast([128, 16, 128])
                    rm4 = rowm.unsqueeze(2).to_broadcast([128, 16, 128])
                    op("dve", lambda E: E.tensor_tensor(out=Vm, in0=v4(Vc[0]), in1=rm4, op=ALU.mult), reads=[("Vc", 0), ("cstb",)] + ZP + SHR, writes=WK(5))
                    if kl == 0:
                        load_s0(kc)
                    op("pool", lambda E: E.tensor_copy(out=Snb, in_=Sn32), reads=WK(0) + SHR, writes=WK(1))
                    if kl + 1 < 4:
                        load_s0(kc + 1)
                    op("pool", lambda E: E.memset(H0bd, 0.0), reads=SHR, writes=[("H0bd",)])
                    pT = [ps_take(), ps_take()]
                    for hp in range(2):
                        hs = slice(hp * 64, hp * 64 + 64)
                        vT_ = psb[pT[hp]][:].bitcast(BF16)
                        for j in range(16):
                            op("pe", lambda E, j=j, hs=hs, vT_=vT_: E.transpose(out=vT_[hs, j * 64:(j + 1) * 64], in_=Snb[hs, j, :], identity=identb[hs, hs]),
                               reads=WK(1) + [("cstb",)] + SHR, writes=PSK(pT[hp]))
                        op("act", lambda E, hs=hs, hp=hp, vT_=vT_: E.copy(out=H0bd[hs, :, hp * 64:(hp + 1) * 64], in_=vT_[hs, :].rearrange("p (j c) -> p j c", j=16)),
                           reads=PSK(pT[hp]) + SHR, writes=[("H0bd",)])
                    px, py, pu = ps_take(), ps_take(), ps_take()
                    pair = lambda pb: pairv(psb[pb][rs, :], kl)
                    for j in range(16):
                        op("pe", lambda E, j=j: E.matmul(pair(px), lhsT=Qm[:, j, :], rhs=H0bd[:, j, :], start=(j == 0), stop=False), reads=WK(4) + [("H0bd",)] + SHR, writes=PSK(px))
                        op("pe", lambda E, j=j: E.matmul(pair(py), lhsT=Rm[:, j, :], rhs=H0bd[:, j, :], start=(j == 0), stop=False), reads=WK(4) + [("H0bd",)] + SHR, writes=PSK(py))
                    his = [2 * kl, 2 * kl + 1]
                    for hi in his:
                        op("pe", lambda E, hi=hi: E.matmul(psb[px][rs, hi * 64:(hi + 1) * 64], lhsT=h8(LkT[0])[:, hi, :], rhs=Vc[0][:, hi * 64:(hi + 1) * 64], start=False, stop=(hi % 2 == 1)), reads=[("LkT", 0), ("Vc", 0)] + ZP + SHR, writes=PSK(px))
                        op("pe", lambda E, hi=hi: E.matmul(psb[py][rs, hi * 64:(hi + 1) * 64], lhsT=h8(MkT[0])[:, hi, :], rhs=Vc[0][:, hi * 64:(hi + 1) * 64], start=False, stop=False), reads=[("MkT", 0), ("Vc", 0)] + ZP + SHR, writes=PSK(py))
                    for hi in his:
                        op("act", lambda E, hi=hi: E.copy(out=Xs[0][rs, hi * 64:(hi + 1) * 64], in_=psb[px][rs, hi * 64:(hi + 1) * 64]), reads=PSK(px) + ZP + SHR, writes=[("Xs", 0)])
                    for hi in his:
                        op("pe", lambda E, hi=hi: E.matmul(psb[pu][rs, hi * 64:(hi + 1) * 64], lhsT=h8(TT_)[:, hi, :], rhs=Xs[0][:, hi * 64:(hi + 1) * 64], start=True, stop=True), reads=[("ZZ", 0, 1), ("Xs", 0)] + ZP + SHR, writes=PSK(pu))
                    for hi in his:
                        op("act", lambda E, hi=hi: E.copy(out=Us[0][rs, hi * 64:(hi + 1) * 64], in_=psb[pu][rs, hi * 64:(hi + 1) * 64]), reads=PSK(pu) + ZP + SHR, writes=[("Us", 0)])
                    for hi in his:
                        op("pe", lambda E, hi=hi: E.matmul(psb[py][rs, hi * 64:(hi + 1) * 64], lhsT=h8(MbT[0])[:, hi, :], rhs=Us[0][:, hi * 64:(hi + 1) * 64], start=False, stop=(hi % 2 == 1)), reads=[("MbT", 0), ("Us", 0)] + ZP + SHR, writes=PSK(py))
                    for hi in his:
                        ypost(py, rs, hi, 1)
                    op("pool", lambda E: E.tensor_tensor(out=H0bd, in0=H0bd, in1=bc_last(WCs[:, kl, :], 128), op=ALU.mult), reads=[("H0bd",), ("WCs", kl)] + SHR, writes=[("H0bd",)])
                    op("dve", lambda E: E.tensor_tensor(out=Um, in0=v4(Us[0]), in1=rm4, op=ALU.mult), reads=[("Us", 0), ("cstb",)] + ZP + SHR, writes=WK(3))
                    for jb in range(4):
                        pS = ps_take()
                        for jj in range(4):
                            j = jb * 4 + jj
                            o_ = psb[pS][:, jj * 128:(jj + 1) * 128]
                            op("pe", lambda E, j=j, o_=o_: E.matmul(o_, lhsT=H0bd[:, j, :], rhs=identb, start=True, stop=False), reads=[("H0bd",), ("cstb",)] + SHR, writes=PSK(pS))
                            op("pe", lambda E, j=j, o_=o_: E.matmul(o_, lhsT=Vm[:, j, :], rhs=Khtok[:, kl * 128:(kl + 1) * 128], start=False, stop=False), reads=WK(5) + [("Khtok",)] + ZP + SHR, writes=PSK(pS))
                            op("pe", lambda E, j=j, o_=o_: E.matmul(o_, lhsT=Um[:, j, :], rhs=Bhtok[:, kl * 128:(kl + 1) * 128], start=False, stop=True), reads=WK(3) + [("Bhtok",)] + ZP + SHR, writes=PSK(pS))
                        for hp in range(2):
                            hs = slice(hp * 64, hp * 64 + 64)
                            op("act", lambda E, jb=jb, hs=hs, hp=hp, pS=pS: E.copy(out=So[hs, jb * 4:(jb + 1) * 4, :], in_=psb[pS][hs, :].rearrange("p (j c) -> p j c", j=4)[:, :, hp * 64:(hp + 1) * 64]),
                               reads=PSK(pS) + SHR, writes=WK(2))
                    op("sp", lambda E: E.dma_start(out=wkv_s[:, 2 * kc:2 * kc + 2, :, :].rearrange("s h v k -> (h v) s k"), in_=So), reads=WK(2) + SHR, writes=[("o_wkv_s", kc)], dma=True)
                for kl in range(4):
                    pairk(kl)

            def do_block(hg, b):
                SAMP = (b == 4)
                r = 64 if SAMP else 128
                bc0 = b * 128
                pK, pB, pV = ps_take(), ps_take(), ps_take()
                vK, vB, vV = [psb[x][:].bitcast(BF16) for x in (pK, pB, pV)]
                for kl in range(4):
                    op("pe", lambda E, kl=kl: E.transpose(out=vK[0:r, kl * 128:(kl + 1) * 128], in_=KtT[:, kl, bc0:bc0 + r], identity=identb), reads=[("gbuf", kl), ("cstb",)], writes=PSK(pK))
                    op("pe", lambda E, kl=kl: E.transpose(out=vB[0:r, kl * 128:(kl + 1) * 128], in_=BtT[:, kl, bc0:bc0 + r], identity=identb), reads=[("gbuf", 4 + kl), ("cstb",)], writes=PSK(pB))
                    op("pe", lambda E, kl=kl: E.transpose(out=vV[0:r, kl * 128:(kl + 1) * 128], in_=RKV[2][:, kl, bc0:bc0 + r], identity=identb), reads=[("rkv", 2, kl), ("cstb",)] + SHR, writes=PSK(pV))
                op("act", lambda E: E.copy(out=Ktok[0:r, :], in_=vK[0:r, 0:512]), reads=PSK(pK) + SHR, writes=[("Ktok",)])
                op("act", lambda E: E.copy(out=Btok[0:r, :], in_=vB[0:r, 0:512]), reads=PSK(pB) + SHR, writes=[("Btok",)])
                for c in ([0] if SAMP else [0, 1]):
                    rs = slice(c * 64, c * 64 + 64)
                    op("dve", lambda E, c=c, rs=rs: E.tensor_copy(out=Vc[c][rs, :], in_=vV[rs, 0:512]), reads=PSK(pV) + ZP + SHR, writes=[("Vc", c)])
                if SAMP:
                    pK2, pB2 = ps_take(), ps_take()
                    vK2, vB2 = psb[pK2][:].bitcast(BF16), psb[pB2][:].bitcast(BF16)
                    for kl in range(4):
                        op("pe", lambda E, kl=kl: E.transpose(out=vK2[0:64, kl * 128:(kl + 1) * 128], in_=Khs[:, kl, :], identity=identb), reads=[("Khs", kl), ("cstb",)] + SHR, writes=PSK(pK2))
                        op("pe", lambda E, kl=kl: E.transpose(out=vB2[0:64, kl * 128:(kl + 1) * 128], in_=Bhs[:, kl, :], identity=identb), reads=[("Bhs", kl), ("cstb",)] + SHR, writes=PSK(pB2))
                    op("act", lambda E: E.copy(out=Khtok[0:64, :], in_=vK2[0:64, 0:512]), reads=PSK(pK2) + ZP + SHR, writes=[("Khtok",)])
                    op("act", lambda E: E.copy(out=Bhtok[0:64, :], in_=vB2[0:64, 0:512]), reads=PSK(pB2) + ZP + SHR, writes=[("Bhtok",)])
                    if cfg.get('rw_level', 4) >= 4:
                        sample_chain(hg)
                else:
                    interleave([(lambda: scores_inverse(hg, b, 0, False), [0, 1, 2, 3]), (lambda: scores_inverse(hg, b, 1, False), [4, 5, 6, 7])])
                    for c in (0, 1):
                        chunk(hg, b, c)
                pzt = ps_take()
                vz = psb[pzt][:].bitcast(BF16)
                for kl in range(4):
                    op("pe", lambda E, kl=kl: E.transpose(out=vz[:, kl * 128:kl * 128 + r], in_=pairv(yh[0:r, :], kl), identity=identb[0:r, 0:r]), reads=[("yh",), ("yh", 0), ("yh", 1), ("cstb",)] + SHR, writes=PSK(pzt))
                for kl in range(4):
                    kc = hg * 4 + kl
                    op("dve", lambda E, kl=kl, kc=kc: E.tensor_scalar(out=zt[:, kl, 0:r], in0=vz[:, kl * 128:kl * 128 + r], scalar1=pc(PI["lnx_g"], kc), scalar2=pc(PI["lnx_b"], kc), op0=ALU.mult, op1=ALU.add), reads=PSK(pzt) + [("pcol",)] + SHR, writes=[("zt",)])
                op("dve", lambda E: E.tensor_tensor(out=zt[:, :, 0:r], in0=zt[:, :, 0:r], in1=bon[:, :, bc0:bc0 + r], op=ALU.add), reads=[("zt",)] + [("bon", k) for k in range(4)] + SHR, writes=[("zt",)])
                op("dve", lambda E: E.tensor_tensor(out=zb[:, hg * 4:(hg + 1) * 4, bc0:bc0 + r], in0=zt[:, :, 0:r], in1=gT[:, :, bc0:bc0 + r], op=ALU.mult), reads=[("zt",)] + [("gT", k) for k in range(4)] + SHR, writes=[("zb", hg, b)])

            def do_hg(hg):
                for kind in range(3):
                    sw = wnext()
                    def ev_mix(oc, pvs, kind=kind):
                        mix(kind * 8 + hg * 4 + oc, pvs, RKV[kind][:, oc, :], ("rkv", kind, oc))
                    ws_mm(sw, 4, hT, HTK, segs_r, KC, ev_mix)
                for kl2 in range(0, 4, 2):
                    interleave([(lambda kl=kl2: elem(hg, kl, 0), [0, 1, 2, 3]), (lambda kl=kl2 + 1: elem(hg, kl, 1), [4, 5, 6, 7])])
                for b in range(nblk):
                    if cfg.get('rw_level', 4) >= 2:
                        do_block(hg, b)
                if LAST:
                    pw = ps_take()
                    Ho = w[3][:, 0:512].rearrange("p (k c) -> p k c", k=4)
                    for kl in range(4):
                        op("pe", lambda E, kl=kl: E.transpose(out=psb[pw][0:64, kl * 128:(kl + 1) * 128], in_=Hst[hg][:, kl, :], identity=identf), reads=[("H32", hg, 0), ("H32", hg, 1), ("cst",)], writes=PSK(pw))
                    op("act", lambda E: E.copy(out=Ho[0:64, :, :], in_=psb[pw][0:64, :].rearrange("p (k c) -> p k c", k=4)), reads=PSK(pw) + SHR, writes=WK(3) + [("wB", 3)])
                    dstv = wkv_p.rearrange("(kc hp) v k -> v kc hp k", hp=2)[:, hg * 4:(hg + 1) * 4, :, :]
                    op("sp", lambda E: E.dma_start(out=dstv, in_=Ho[0:64, :, :].rearrange("p k (hp c) -> p k hp c", hp=2)), reads=WK(3) + [("wB", 3)] + SHR, writes=[("o_wkv_p", hg)], dma=True)
            for hg in range(2):
                do_hg(hg)
            tap("zb", zb[:, :, :], [128, KC, 576], [("zb", hg, b) for hg in range(2) for b in range(nblk)])
            for g in range(2):
                sw = wnext()
                def ev_gb(oc, pvs, g=g):
                    kc = g * 4 + oc
                    for (pv, pb, c0, c1) in pvs:
                        op("act", lambda E, pv=pv, kc=kc, c0=c0, c1=c1: E.activation(out=gbuf[:, kc, c0:c1], in_=pv, func=AF.Sigmoid), reads=PSK(pb), writes=[("gbuf", kc)])
                ws_mm(sw, 4, hT, HTK, segs, KC, ev_gb)
            ZBK = [("zb", hg, b) for hg in range(2) for b in range(nblk)]
            for g in range(2):
                sw = wnext()
                def ev_ob(oc, pvs, g=g):
                    kc = g * 4 + oc
                    for (pv, pb, c0, c1) in pvs:
                        op("dve", lambda E, pv=pv, kc=kc, c0=c0, c1=c1: E.tensor_tensor(out=merged[:, kc, c0:c1], in0=pv, in1=gbuf[:, kc, c0:c1], op=ALU.mult), reads=PSK(pb) + [("gbuf", kc)], writes=[("merged", kc)])
                ws_mm(sw, 4, zb, ZBK + SHR, segs, KC, ev_ob)

        def phase_n(p):
            P0 = (p == 0)
            LAST = (p == NPASS - 1)
            nblk = 5 if P0 else 4
            brows = lambda b: 64 if b == 4 else 128
            bcol = lambda b: b * 128
            load_gain(0, 0)
            for b in range(nblk):
                r = brows(b)
                xi = b % 2
                src = x_s[0:64, :] if b == 4 else x_p[p * TT + b * 128: p * TT + (b + 1) * 128, :]
                op("sp", lambda E, xi=xi, r=r, src=src: E.dma_start(out=xs[xi][0:r, :], in_=src), writes=[("xs", xi)], dma=True)
                op("act", lambda E, xi=xi, r=r: E.activation(out=junk[0:r, :], in_=xs[xi][0:r, :], func=AF.Square, accum_out=st[0:r, 0:1]),
                   reads=[("xs", xi)], writes=[("junk",), ("st", 0)])
                rstd_from_ssq(st[0:r, 0:1], st[0:r, 1:2], D, 1e-6, [("st", 0)], [("st", 1)])
                need32 = (b == 4) or (LAST and b == 3)
                if need32:
                    op("dve", lambda E, xi=xi, r=r: E.scalar_tensor_tensor(out=junk[0:r, :], in0=xs[xi][0:r, :], scalar=st[0:r, 1:2], in1=gain[0][0:r, :], op0=ALU.mult, op1=ALU.mult),
                       reads=[("xs", xi), ("st", 1), ("gain", 0)], writes=[("junk",)])
                    if b == 4:
                        dst = shift_s.rearrange("(s o) d -> s o d", o=1)
                        op("sp", lambda E, dst=dst: E.dma_start(out=dst[:, 0, :], in_=junk[3:64:4, :]), reads=[("junk",)], writes=[("o_shift_s",)], dma=True)
                    else:
                        op("sp", lambda E: E.dma_start(out=shift_p[0:1, :], in_=junk[127:128, :]), reads=[("junk",)], writes=[("o_shift_p",)], dma=True)
                op("dve", lambda E, xi=xi, r=r: E.scalar_tensor_tensor(out=hb[0:r, :], in0=xs[xi][0:r, :], scalar=st[0:r, 1:2], in1=gain[0][0:r, :], op0=ALU.mult, op1=ALU.mult),
                   reads=[("xs", xi), ("st", 1), ("gain", 0)], writes=[("hb",)])
                pb = ps_take()
                pv = psb[pb][:].bitcast(BF16)
                for kc in range(KC):
                    op("pe", lambda E, kc=kc, r=r, pv=pv: E.transpose(out=pv[:, kc * 128: kc * 128 + r], in_=hb[0:r, kc * 128:(kc + 1) * 128], identity=identb[0:r, 0:r]),
                       reads=[("hb",), ("cstb",)], writes=PSK(pb))
                op("act", lambda E, b=b, r=r, pv=pv: E.copy(out=hT[:, :, bcol(b): bcol(b) + r], in_=pv.rearrange("p (k c) -> p k c", k=KC)[:, :, 0:r]),
                   reads=PSK(pb), writes=[("hT", b)])
            if P0:
                op("sp", lambda E: E.dma_start(out=xs[0][0:16, :], in_=sshift[:, :]), writes=[("xs", 0)], dma=True)
                op("dve", lambda E: E.tensor_copy(out=hb[0:16, :], in_=xs[0][0:16, :]), reads=[("xs", 0)], writes=[("hb",)])
                pb = ps_take()
                pv = psb[pb][:].bitcast(BF16)
                for kc in range(KC):
                    op("pe", lambda E, kc=kc, pv=pv: E.transpose(out=pv[:, kc * 128: kc * 128 + 16], in_=hb[0:16, kc * 128:(kc + 1) * 128], identity=identb[0:16, 0:16]),
                       reads=[("hb",), ("cstb",)], writes=PSK(pb))
                op("act", lambda E, pv=pv: E.copy(out=hT[:, :, 576:592], in_=pv.rearrange("p (k c) -> p k c", k=KC)[:, :, 0:16]),
                   reads=PSK(pb), writes=[("hT", 5)])
            HTK = [("hT", b) for b in range(nblk)] + ([("hT", 5)] if P0 else [])
            tap("hT", hT[:, :, :], [128, KC, NCOL], HTK)


        def do_pass(p):
            P0 = (p == 0)
            LAST = (p == NPASS - 1)
            nblk = 5 if P0 else 4
            segs = [(0, 512)] + ([(512, 576)] if P0 else [])
            segs_r = [(0, 512)] + ([(512, 592)] if P0 else [])
            ncolu = 576 if P0 else 512
            brows = lambda b: 64 if b == 4 else 128
            bcol = lambda b: b * 128

            if p == 0:
                phase_n(0)
            HTK = [("hT", b) for b in range(nblk)] + ([("hT", 5)] if P0 else [])
            def ws_mm(ws, noc, act_t, act_keys, sg, nk, evac, lhs=None):
                for oc in range(noc):
                    pbs = [ps_take() for _ in sg]
                    for si, (c0, c1) in enumerate(sg):
                        for k in range(nk):
                            lh = (lhs(k, oc) if lhs else wb[ws][:, k, oc * 128:(oc + 1) * 128])
                            op("pe", lambda E, lh=lh, k=k, c0=c0, c1=c1, pb=pbs[si]: E.matmul(psb[pb][:, 0:c1 - c0], lhsT=lh, rhs=act_t[:, k, c0:c1], start=(k == 0), stop=(k == nk - 1)),
                               reads=([("wb", ws)] if ws is not None else []) + act_keys, writes=PSK(pbs[si]))
                    evac(oc, [(psb[pb][:, 0:c1 - c0], pb, c0, c1) for pb, (c0, c1) in zip(pbs, sg)])

            if do_rwkv:
                rwkv_phase(p, P0, LAST, nblk, segs, segs_r, HTK, ws_mm)
            else:
                op("pool", lambda E: E.memset(merged[:], 0.0), writes=[("merged", k) for k in range(KC)])

            phase_barrier()
            AR.reset()
            UW = 30 + 576
            uext = AR.take(KC * UW, BF16).rearrange("p (k c) -> p k c", k=KC)
            uexs = AR.take(KC * NSEQ * 34, BF16).rearrange("p (k s c) -> p k s c", k=KC, s=NSEQ)
            cT = AR.take(KC * 576, F32).rearrange("p (k c) -> p k c", k=KC)
            cbf = AR.take(KC * 576, BF16).rearrange("p (k c) -> p k c", k=KC)
            csq = AR.take(KC * 576, BF16).rearrange("p (k c) -> p k c", k=KC)
            csl = AR.take(KC * 576, BF16).rearrange("p (k c) -> p k c", k=KC)
            sgt = AR.take(4 * 576, F32).rearrange("p (k c) -> p k c", k=4)
            mean_sb = AR.take(576, F32)
            rstd_sb = AR.take(576, F32)
            u32 = AR.take(KC * 64, F32).rearrange("p (k c) -> p k c", k=KC)
            tmpb = AR.take(576, BF16)
            dtmp = AR.take(576, F32)

            op("pool", lambda E: E.tensor_copy(out=uext[:, :, 0:30], in_=uhist[:, :, :]), reads=[("uhist",)] + SHR, writes=[("uext", "h")])
            for g in range(2):
                sw = wnext()
                def ev_gate(oc, pvs):
                    for (pv, pb, c0, c1) in pvs:
                        op("act", lambda E, pv=pv, oc=oc, c0=c0, c1=c1: E.activation(out=sgt[:, oc, c0:c1], in_=pv, func=AF.Sigmoid),
                           reads=PSK(pb) + SHR, writes=[("sgt", oc)])
                ws_mm(sw, 4, hT, HTK, segs, KC, ev_gate)
                sw = wnext()
                def ev_val(oc, pvs, g=g):
                    kc = g * 4 + oc
                    for (pv, pb, c0, c1) in pvs:
                        if c0 == 0:
                            op("dve", lambda E, pv=pv, oc=oc, kc=kc: E.tensor_tensor(out=uext[:, kc, 30:542], in0=pv, in1=sgt[:, oc, 0:512], op=ALU.mult),
                               reads=PSK(pb) + [("sgt", oc)] + SHR, writes=[("uext", kc)])
                            if LAST:
                                op("dve", lambda E, pv=pv, oc=oc, kc=kc: E.tensor_tensor(out=u32[:, kc, 0:30], in0=pv[:, 482:512], in1=sgt[:, oc, 482:512], op=ALU.mult),
                                   reads=PSK(pb) + [("sgt", oc)] + SHR, writes=[("u32", kc)])
                        else:
                            op("dve", lambda E, pv=pv, oc=oc, kc=kc: E.tensor_tensor(out=uexs[:, kc, :, 30:34], in0=pv.rearrange("p (s t) -> p s t", t=4), in1=sgt[:, oc, 512:576].rearrange("p (s t) -> p s t", t=4), op=ALU.mult),
                               reads=PSK(pb) + [("sgt", oc)] + SHR, writes=[("uexs", kc)])
                            op("dve", lambda E, pv=pv, oc=oc, kc=kc: E.tensor_tensor(out=u32[:, kc, 0:64], in0=pv, in1=sgt[:, oc, 512:576], op=ALU.mult),
                               reads=PSK(pb) + [("sgt", oc)] + SHR, writes=[("u32", kc)])
                ws_mm(sw, 4, hT, HTK, segs, KC, ev_val)
            UK = [("uext", kc) for kc in range(KC)] + [("uext", "h")]
            op("pool", lambda E: E.tensor_copy(out=uhist[:, :, :], in_=uext[:, :, 512:542]), reads=UK + SHR, writes=[("uhist",)])
            def emit_u32(nrows, dst_fn):
                pbA, pbB = ps_take(), ps_take()
                for kc in range(KC):
                    pb = pbA if kc < 4 else pbB
                    op("pe", lambda E, kc=kc, pb=pb: E.transpose(out=psb[pb][0:nrows, (kc % 4) * 128:(kc % 4 + 1) * 128], in_=u32[:, kc, 0:nrows], identity=identf),
                       reads=[("u32", kc), ("cst",)] + SHR, writes=PSK(pb))
                op("act", lambda E: E.copy(out=junk[0:nrows, 0:512], in_=psb[pbA][0:nrows, :]), reads=PSK(pbA), writes=[("junk",)])
                op("act", lambda E: E.copy(out=junk[0:nrows, 512:1024], in_=psb[pbB][0:nrows, :]), reads=PSK(pbB), writes=[("junk",)])
                dst_fn()
            if LAST:
                emit_u32(30, lambda: op("sp", lambda E: E.dma_start(out=conv_p[:, :], in_=junk[0:30, :]), reads=[("junk",)], writes=[("o_conv_p",)], dma=True))
            if P0:
                def dst():
                    for t in range(4):
                        op("sp", lambda E, t=t: E.dma_start(out=conv_s[:, 26 + t, :], in_=junk[t:64:4, :]), reads=[("junk",)], writes=[("o_conv_s", t)], dma=True)
                    op("sp", lambda E: E.dma_start(out=conv_s[:, 0:26, :], in_=sconv[:, 4:30, :]), writes=[("o_conv_s", 9)], dma=True)
                emit_u32(64, dst)
                for q in range(4):
                    op("pool", lambda E, q=q: E.dma_start(out=hb[0:120, :], in_=sconv[4 * q:4 * q + 4, :, :].rearrange("s r d -> (s r) d")), writes=[("hb",)], dma=True)
                    pb = ps_take()
                    pv = psb[pb][:].bitcast(BF16)
                    for kc in range(KC):
                        op("pe", lambda E, kc=kc, pv=pv: E.transpose(out=pv[:, kc * 120:(kc + 1) * 120], in_=hb[0:120, kc * 128:(kc + 1) * 128], identity=identb[0:120, 0:120]),
                           reads=[("hb",), ("cstb",)], writes=PSK(pb))
                    for kc in range(KC):
                        op("act", lambda E, kc=kc, q=q, pv=pv: E.copy(out=uexs[:, kc, 4 * q:4 * q + 4, 0:30], in_=pv[:, kc * 120:(kc + 1) * 120].rearrange("p (s r) -> p s r", s=4)),
                           reads=PSK(pb) + SHR, writes=[("uexs", kc, q)])
            dcur = 0
            for kc in range(KC):
                pbm = ps_take()
                pbs = ps_take() if P0 else None
                for j in range(31):
                    d = dcur % 8
                    dcur += 1
                    op("dve", lambda E, d=d, kc=kc, j=j: E.tensor_scalar(out=diag[d][:], in0=identb, scalar1=pcol[:, PC_CW + kc * 31 + j: PC_CW + kc * 31 + j + 1], scalar2=None, op0=ALU.mult),
                       reads=[("cstb",), ("pcol",)], writes=[("diag", d)])
                    op("pe", lambda E, d=d, kc=kc, j=j, pbm=pbm: E.matmul(psb[pbm][:, 0:512], lhsT=diag[d][:], rhs=uext[:, kc, j:j + 512], start=(j == 0), stop=(j == 30)),
                       reads=[("diag", d), ("uext", kc), ("uext", "h")] + SHR, writes=PSK(pbm))
                    if P0:
                        op("pe", lambda E, d=d, kc=kc, j=j, pbs=pbs: E.matmul(psb[pbs][:, 0:64].rearrange("p (s t) -> p s t", t=4), lhsT=diag[d][:], rhs=uexs[:, kc, :, j:j + 4], start=(j == 0), stop=(j == 30)),
                           reads=[("diag", d), ("uexs", kc)] + [("uexs", kc, q) for q in range(4)] + SHR, writes=PSK(pbs))
                for (pb, c0, c1) in [(pbm, 0, 512)] + ([(pbs, 512, 576)] if P0 else []):
                    op("act", lambda E, pb=pb, kc=kc, c0=c0, c1=c1: E.activation(out=cT[:, kc, c0:c1], in_=psb[pb][:, 0:c1 - c0], func=AF.Identity, bias=pc(PI["conv_b"], kc), scale=1.0),
                       reads=PSK(pb) + [("pcol",)] + SHR, writes=[("cT", kc)])
                op("pool", lambda E, kc=kc: E.tensor_copy(out=cbf[:, kc, 0:ncolu], in_=cT[:, kc, 0:ncolu]), reads=[("cT", kc)] + SHR, writes=[("cbf", kc)])
                op("act", lambda E, kc=kc: E.activation(out=csq[:, kc, 0:ncolu], in_=cT[:, kc, 0:ncolu], func=AF.Square), reads=[("cT", kc)] + SHR, writes=[("csq", kc)])
            odv = cs("odiv", BF16)
            for (c0, c1) in segs:
                pm_, pq_ = ps_take(), ps_take()
                for kc in range(KC):
                    op("pe", lambda E, kc=kc, c0=c0, c1=c1, pm_=pm_: E.matmul(psb[pm_][:, 0:c1 - c0], lhsT=odv, rhs=cbf[:, kc, c0:c1], start=(kc == 0), stop=(kc == KC - 1)),
                       reads=[("cbf", kc), ("cstb",)] + SHR, writes=PSK(pm_))
                for kc in range(KC):
                    op("pe", lambda E, kc=kc, c0=c0, c1=c1, pq_=pq_: E.matmul(psb[pq_][:, 0:c1 - c0], lhsT=odv, rhs=csq[:, kc, c0:c1], start=(kc == 0), stop=(kc == KC - 1)),
                       reads=[("csq", kc), ("cstb",)] + SHR, writes=PSK(pq_))
                op("act", lambda E, c0=c0, c1=c1, pm_=pm_: E.copy(out=mean_sb[:, c0:c1], in_=psb[pm_][:, 0:c1 - c0]), reads=PSK(pm_) + SHR, writes=[("mean_sb", c0)])
                op("dve", lambda E, c0=c0, c1=c1: E.tensor_tensor(out=dtmp[:, c0:c1], in0=mean_sb[:, c0:c1], in1=mean_sb[:, c0:c1], op=ALU.mult), reads=[("mean_sb", c0)] + SHR, writes=[("dtmp",)])
                op("dve", lambda E, c0=c0, c1=c1, pq_=pq_: E.tensor_tensor(out=rstd_sb[:, c0:c1], in0=psb[pq_][:, 0:c1 - c0], in1=dtmp[:, c0:c1], op=ALU.subtract), reads=PSK(pq_) + [("dtmp",)] + SHR, writes=[("rstd_sb", c0)])
                op("act", lambda E, c0=c0, c1=c1: E.activation(out=rstd_sb[:, c0:c1], in_=rstd_sb[:, c0:c1], func=AF.Sqrt, bias=1e-5, scale=1.0), reads=[("rstd_sb", c0)] + SHR, writes=[("rstd_sb", c0)])
                op("dve", lambda E, c0=c0, c1=c1: E.reciprocal(out=rstd_sb[:, c0:c1], in_=rstd_sb[:, c0:c1]), reads=[("rstd_sb", c0)] + SHR, writes=[("rstd_sb", c0)])
            for g in range(2):
                sw = wnext()
                def ev_ga(oc, pvs, g=g):
                    kc = g * 4 + oc
                    for (pv, pb, c0, c1) in pvs:
                        op("act", lambda E, pv=pv, kc=kc, c0=c0, c1=c1: E.activation(out=gbuf[:, kc, c0:c1], in_=pv, func=AF.Sigmoid), reads=PSK(pb), writes=[("gbuf", kc)])
                ws_mm(sw, 4, hT, HTK, segs, KC, ev_ga)
            LNK = [("mean_sb", c0) for c0, _ in segs] + [("rstd_sb", c0) for c0, _ in segs]
            dtmps = [dtmp, AR.take(576, F32)]
            for kc in range(KC):
                dt_ = dtmps[kc % 2]
                DK = [("dtmp",)] if kc % 2 == 0 else [("dtmp", 1)]
                op("dve", lambda E, kc=kc, dt_=dt_: E.tensor_tensor(out=dt_[:, 0:ncolu], in0=cT[:, kc, 0:ncolu], in1=mean_sb[:, 0:ncolu], op=ALU.subtract), reads=[("cT", kc)] + LNK + SHR, writes=DK)
                op("dve", lambda E, kc=kc, dt_=dt_: E.tensor_tensor(out=dt_[:, 0:ncolu], in0=dt_[:, 0:ncolu], in1=rstd_sb[:, 0:ncolu], op=ALU.mult), reads=DK + LNK + SHR, writes=DK)
                op("act", lambda E, kc=kc, dt_=dt_: E.activation(out=csl[:, kc, 0:ncolu], in_=dt_[:, 0:ncolu], func=AF.Silu, bias=pc(PI["conv_ln_b"], kc), scale=pc(PI["conv_ln_g"], kc)),
                   reads=DK + [("pcol",)] + SHR, writes=[("csl", kc)])
            tap("csl", csl[:, :, :], [128, KC, 576], [("csl", kc) for kc in range(KC)])
            for g in range(2):
                sw = wnext()
                def ev_oa(oc, pvs, g=g):
                    kc = g * 4 + oc
                    for (pv, pb, c0, c1) in pvs:
                        op("dve", lambda E, pv=pv, kc=kc, c0=c0, c1=c1: E.tensor_tensor(out=tmpb[:, c0:c1], in0=pv, in1=gbuf[:, kc, c0:c1], op=ALU.mult), reads=PSK(pb) + [("gbuf", kc)] + SHR, writes=[("tmpb",)])
                        op("pool", lambda E, kc=kc, c0=c0, c1=c1: E.tensor_tensor(out=merged[:, kc, c0:c1], in0=merged[:, kc, c0:c1], in1=tmpb[:, c0:c1], op=ALU.add), reads=[("tmpb",), ("merged", kc)] + SHR, writes=[("merged", kc)])
                ws_mm(sw, 4, csl, [("csl", kc) for kc in range(KC)] + SHR, segs, KC, ev_oa)
            MK = [("merged", kc) for kc in range(KC)]
            tap("merged", merged[:, :, :], [128, KC, NCOL], MK)

            phase_barrier()
            AR.reset()
            X1 = AR.take(5 * D, F32).rearrange("p (b d) -> p b d", b=5)
            load_gain(1, 1)
            load_gain(0, 2)
            swo = [wnext(ahead=1), wnext(ahead=1)]
            junk2 = AR.take(D, F32); hb2 = AR.take(D, BF16)
            def wout_block(b, junk, hb, so, JK, HK):
                r = brows(b)
                xi = b % 2
                pbh = [ps_take(), ps_take()]
                for hf in range(2):
                    for kc in range(KC):
                        op("pe", lambda E, hf=hf, kc=kc, b=b, r=r, pb=pbh[hf]: E.matmul(psb[pb][0:r, :], lhsT=merged[:, kc, bcol(b):bcol(b) + r], rhs=wb[swo[hf]][:, kc, :], start=(kc == 0), stop=(kc == KC - 1)),
                           reads=[("merged", kc), ("wb", swo[hf])], writes=PSK(pbh[hf]))
                    op("act", lambda E, hf=hf, r=r, pb=pbh[hf]: E.activation(out=junk[0:r, hf * 512:(hf + 1) * 512], in_=psb[pb][0:r, :], func=AF.Square, accum_out=st[0:r, so + 2 + hf:so + 3 + hf]),
                       reads=PSK(pbh[hf]) + SHR, writes=[JK, ("st", so + 2 + hf)])
                op("dve", lambda E, r=r: E.tensor_tensor(out=st[0:r, so + 4:so + 5], in0=st[0:r, so + 2:so + 3], in1=st[0:r, so + 3:so + 4], op=ALU.add), reads=[("st", so + 2), ("st", so + 3)], writes=[("st", so + 4)])
                rstd_from_ssq(st[0:r, so + 4:so + 5], st[0:r, so + 5:so + 6], D, 1e-6, [("st", so + 4)], [("st", so + 5)])
                src = x_s[0:64, :] if b == 4 else x_p[p * TT + b * 128: p * TT + (b + 1) * 128, :]
                op("sp", lambda E, xi=xi, r=r, src=src: E.dma_start(out=xs[xi][0:r, :], in_=src), writes=[("xs", xi)], dma=True)
                for hf in range(2):
                    op("dve", lambda E, hf=hf, r=r, pb=pbh[hf]: E.scalar_tensor_tensor(out=junk[0:r, hf * 512:(hf + 1) * 512], in0=psb[pb][0:r, :], scalar=st[0:r, so + 5:so + 6], in1=gain[1][0:r, hf * 512:(hf + 1) * 512], op0=ALU.mult, op1=ALU.mult),
                       reads=PSK(pbh[hf]) + [("st", so + 5), ("gain", 1)] + SHR, writes=[JK])
                op("pool", lambda E, b=b, r=r, xi=xi: E.tensor_tensor(out=X1[0:r, b, :], in0=xs[xi][0:r, :], in1=junk[0:r, :], op=ALU.add), reads=[("xs", xi), JK] + SHR, writes=[("X1", b)])
                op("act", lambda E, b=b, r=r: E.activation(out=junk[0:r, :], in_=X1[0:r, b, :], func=AF.Square, accum_out=st[0:r, so + 6:so + 7]), reads=[("X1", b)] + SHR, writes=[JK, ("st", so + 6)])
                rstd_from_ssq(st[0:r, so + 6:so + 7], st[0:r, so + 7:so + 8], D, 1e-6, [("st", so + 6)], [("st", so + 7)])
                op("dve", lambda E, b=b, r=r: E.scalar_tensor_tensor(out=hb[0:r, :], in0=X1[0:r, b, :], scalar=st[0:r, so + 7:so + 8], in1=gain[0][0:r, :], op0=ALU.mult, op1=ALU.mult),
                   reads=[("X1", b), ("st", so + 7), ("gain", 0)] + SHR, writes=[HK])
                pb = ps_take()
                pv = psb[pb][:].bitcast(BF16)
                for kc in range(KC):
                    op("pe", lambda E, kc=kc, r=r, pv=pv: E.transpose(out=pv[:, kc * 128: kc * 128 + r], in_=hb[0:r, kc * 128:(kc + 1) * 128], identity=identb[0:r, 0:r]),
                       reads=[HK, ("cstb",)] + SHR, writes=PSK(pb))
                op("act", lambda E, b=b, r=r, pv=pv: E.copy(out=hT[:, :, bcol(b): bcol(b) + r], in_=pv.rearrange("p (k c) -> p k c", k=KC)[:, :, 0:r]),
                   reads=PSK(pb), writes=[("hT", b)])
            for b0 in range(0, nblk, 2):
                gens = [(lambda b=b0: wout_block(b, junk, hb, 0, ("junk",), ("hb",)), [0, 1, 2, 3])]
                if b0 + 1 < nblk:
                    gens.append((lambda b=b0 + 1: wout_block(b, junk2, hb2, 40, ("junk2",), ("hb2",)), [4, 5, 6, 7]))
                interleave(gens)
            H2K = [("hT", b) for b in range(nblk)]

            aT = AR.take(32 * 576, BF16).rearrange("p (f c) -> p f c", f=32)
            fT = AR.take(KC * 576, F32).rearrange("p (k c) -> p k c", k=KC)
            rtmp = AR.take(576, F32)
            load_gain(1, 3)
            for g in range(8):
                sw = wnext()
                def ev_up(oc, pvs, g=g):
                    fc = g * 4 + oc
                    for (pv, pb, c0, c1) in pvs:
                        op("act", lambda E, pv=pv, c0=c0, c1=c1: E.activation(out=rtmp[:, c0:c1], in_=pv, func=AF.Relu), reads=PSK(pb) + SHR, writes=[("rtmp",)])
                        op("dve", lambda E, pv=pv, fc=fc, c0=c0, c1=c1: E.tensor_tensor(out=aT[:, fc, c0:c1], in0=pv, in1=rtmp[:, c0:c1], op=ALU.mult), reads=PSK(pb) + [("rtmp",)] + SHR, writes=[("aT", fc)])
                ws_mm(sw, 4, hT, H2K, segs, KC, ev_up)
            ATK = [("aT", fc) for fc in range(32)]
            if p + 1 < npass:
                phase_n(p + 1)
            for oc in range(KC):
                s_ = wnext()
                wv = wb[s_][:].rearrange("p k c -> p (k c)").rearrange("p (f c) -> p f c", f=32)
                def ev_dn(oc_, pvs, oc=oc):
                    for (pv, pb, c0, c1) in pvs:
                        op("act", lambda E, pv=pv, c0=c0, c1=c1: E.copy(out=fT[:, oc, c0:c1], in_=pv), reads=PSK(pb) + SHR, writes=[("fT", oc)])
                ws_mm(s_, 1, aT, ATK + SHR, segs, 32, ev_dn, lhs=lambda k, oc_, wv=wv: wv[:, k, :])
            FTK = [("fT", kc) for kc in range(KC)]
            def fout_block(b, junk, so, JK):
                r = brows(b)
                xi = b % 2
                pbh = [ps_take(), ps_take()]
                for kc in range(KC):
                    pb = pbh[kc // 4]
                    op("pe", lambda E, kc=kc, b=b, r=r, pb=pb: E.transpose(out=psb[pb][0:r, (kc % 4) * 128:(kc % 4 + 1) * 128], in_=fT[:, kc, bcol(b):bcol(b) + r], identity=identf),
                       reads=[("fT", kc), ("cst",)] + SHR, writes=PSK(pb))
                for hf in range(2):
                    op("act", lambda E, hf=hf, r=r, pb=pbh[hf]: E.activation(out=junk[0:r, hf * 512:(hf + 1) * 512], in_=psb[pb][0:r, :], func=AF.Square, accum_out=st[0:r, so + 8 + hf:so + 9 + hf]),
                       reads=PSK(pbh[hf]) + SHR, writes=[JK, ("st", so + 8 + hf)])
                op("dve", lambda E, r=r: E.tensor_tensor(out=st[0:r, so + 10:so + 11], in0=st[0:r, so + 8:so + 9], in1=st[0:r, so + 9:so + 10], op=ALU.add), reads=[("st", so + 8), ("st", so + 9)], writes=[("st", so + 10)])
                rstd_from_ssq(st[0:r, so + 10:so + 11], st[0:r, so + 11:so + 12], D, 1e-6, [("st", so + 10)], [("st", so + 11)])
                for hf in range(2):
                    op("dve", lambda E, hf=hf, r=r, pb=pbh[hf]: E.scalar_tensor_tensor(out=junk[0:r, hf * 512:(hf + 1) * 512], in0=psb[pb][0:r, :], scalar=st[0:r, so + 11:so + 12], in1=gain[1][0:r, hf * 512:(hf + 1) * 512], op0=ALU.mult, op1=ALU.mult),
                       reads=PSK(pbh[hf]) + [("st", so + 11), ("gain", 1)] + SHR, writes=[JK])
                op("pool", lambda E, b=b, r=r, xi=xi: E.tensor_tensor(out=xs[xi][0:r, :], in0=X1[0:r, b, :], in1=junk[0:r, :], op=ALU.add), reads=[("X1", b), JK] + SHR, writes=[("xs", xi)])
                dst = y_s[0:64, :] if b == 4 else y_p[p * TT + b * 128: p * TT + (b + 1) * 128, :]
                op("sp", lambda E, xi=xi, r=r, dst=dst: E.dma_start(out=dst, in_=xs[xi][0:r, :]), reads=[("xs", xi)], writes=[("o_y", p, b)], dma=True)
            for b0 in range(0, nblk, 2):
                gens = [(lambda b=b0: fout_block(b, junk, 0, ("junk",)), [0, 1, 2, 3])]
                if b0 + 1 < nblk:
                    gens.append((lambda b=b0 + 1: fout_block(b, junk2, 40, ("junk2",)), [4, 5, 6, 7]))
                interleave(gens)

        for p_ in range(npass):
            do_pass(p_)

        outk = [k for k in S.last_w if isinstance(k, tuple) and (str(k[0]).startswith("o_") or k[0] == "tap")]
        op("sp", None, reads=outk)
        S.emit(lambda name: es.enter_context(nc.semaphore(name)))
    nc_sched[0] = S
    return nc, tapd


def _in_maps(inp, cores):
    f = lambda a: np.ascontiguousarray(np.asarray(a, np.float32))
    pcolv = _pack_pcol(inp)
    gains = f(np.stack([inp["pre_mix_g"][0], inp["post_mix_g"][0], inp["pre_ffn_g"][0], inp["post_ffn_g"][0]]))
    shared = dict(gains=gains, pcol=pcolv, cst=_CST, w_in=f(inp["w_in"][0]), w_co=f(inp["w_conv_out"][0]),
                  w_ro=f(inp["w_rwkv_out"][0]), w_out=f(inp["w_out"][0]), w_up=f(inp["w_ff_up"][0]), w_dn=f(inp["w_ff_down"][0]),
                  w_dec=f(inp["w_decay_up"][0]), w_icl=f(inp["w_iclr_up"][0]), w_g=f(inp["w_gate_up"][0]))
    maps = []
    for c in cores:
        sl = slice(16 * c, 16 * c + 16)
        m = dict(shared)
        m.update(x_p=f(inp["x_prompt"][c]), x_s=f(inp["x_sample"][sl]).reshape(NS, D), sconv=f(inp["state_conv"][0][sl]),
                 sshift=f(inp["state_shift"][0][sl]), swkv=f(inp["state_wkv"][0][sl]))
        maps.append(m)
    return maps


_NC_CACHE = {}
nc_sched = [None]


def kernel(**inp):
    if "nc" not in _NC_CACHE:
        _NC_CACHE["nc"] = build_nc()[0]
    nc = _NC_CACHE["nc"]
    res = run_bass_kernel_spmd(nc, _in_maps(inp, range(NCORES)), core_ids=list(range(NCORES))).results
    g = lambda n: [np.asarray(r[n], np.float32) for r in res]
    y_p = np.stack(g("y_p"))
    y_s = np.concatenate(g("y_s")).reshape(128, 4, D)
    conv_p = np.stack(g("conv_p"))[None]
    shift_p = np.concatenate(g("shift_p"))[None]
    wkv_p = np.stack(g("wkv_p"))[None]
    conv_s = np.concatenate(g("conv_s"))[None]
    shift_s = np.concatenate(g("shift_s"))[None]
    wkv_s = np.concatenate(g("wkv_s"))[None]
    return (y_p, y_s, conv_p, shift_p, wkv_p, conv_s, shift_s, wkv_s)
```
